# Optimizing a Trainium2 kernel written in Bass

```python
import math
import jax, jax.numpy as jnp
from jax import lax
import numpy as np

D_MODEL = 1024
BATCH = 32
SEQ = 2048
DEPTH = 1

CHUNK = 64
Q_BLOCK = 128
N_HEADS_A = 8
NOPE_DIM = 64
ROPE_DIM = 32
V_DIM_A = 64
Q_LORA = 384
KV_LORA = 256
N_HEADS_B = 8
HEAD_DIM_B = 64
WIDTH_A = N_HEADS_A * V_DIM_A
WIDTH_B = N_HEADS_B * HEAD_DIM_B
ROPE_THETA = 10000.0
EPS = 1e-6
SPLITS = (Q_LORA, KV_LORA, ROPE_DIM, WIDTH_A, WIDTH_B, WIDTH_B, WIDTH_B,
          N_HEADS_B, WIDTH_B, D_MODEL, D_MODEL)
D_IN = sum(SPLITS)

kernel_name = "hybrid_mla_fox_gated_block"


def rms_norm(x, w):
    xf = x.astype(jnp.float32)
    y = xf * lax.rsqrt(jnp.mean(xf * xf, axis=-1, keepdims=True) + EPS)
    return (y * w.astype(jnp.float32)).astype(x.dtype)


def rope_angles(positions):
    inv_freq = ROPE_THETA ** (-jnp.arange(0, ROPE_DIM, 2, dtype=jnp.float32) / ROPE_DIM)
    ang = positions.astype(jnp.float32)[..., None] * inv_freq
    return jnp.cos(ang), jnp.sin(ang)


def apply_rope(x, cos, sin):
    x1, x2 = jnp.split(x.astype(jnp.float32), 2, axis=-1)
    y = jnp.concatenate([x1 * cos - x2 * sin, x2 * cos + x1 * sin], axis=-1)
    return y.astype(x.dtype)


def chunk_causal_mask(lo, hi):
    qp = lo + jnp.arange(Q_BLOCK)
    kp = jnp.arange(hi)
    return (kp[None, :] // CHUNK) <= (qp[:, None] // CHUNK)


def mla_attention(q_nope, q_rope, k_nope, k_rope, v):
    seq = q_nope.shape[1]
    scale = 1.0 / math.sqrt(NOPE_DIM + ROPE_DIM)
    outs = []
    for i in range(seq // Q_BLOCK):
        lo, hi = i * Q_BLOCK, (i + 1) * Q_BLOCK
        s = (jnp.einsum('bqhd,bkhd->bhqk', q_nope[:, lo:hi], k_nope[:, :hi],
                        preferred_element_type=jnp.float32)
             + jnp.einsum('bqhr,bkr->bhqk', q_rope[:, lo:hi], k_rope[:, :hi],
                          preferred_element_type=jnp.float32)) * scale
        s = jnp.where(chunk_causal_mask(lo, hi), s, -jnp.inf)
        p = jax.nn.softmax(s, axis=-1).astype(v.dtype)
        outs.append(jnp.einsum('bhqk,bkhd->bqhd', p, v[:, :hi]))
    return jnp.concatenate(outs, axis=1)


def fox_attention(q, k, v, log_f):
    seq = q.shape[1]
    scale = 1.0 / math.sqrt(HEAD_DIM_B)
    cum = jnp.swapaxes(jnp.cumsum(log_f, axis=1), 1, 2)
    outs = []
    for i in range(seq // Q_BLOCK):
        lo, hi = i * Q_BLOCK, (i + 1) * Q_BLOCK
        s = jnp.einsum('bqhd,bkhd->bhqk', q[:, lo:hi], k[:, :hi],
                       preferred_element_type=jnp.float32) * scale
        s = s + (cum[:, :, lo:hi, None] - cum[:, :, None, :hi])
        qp = lo + jnp.arange(Q_BLOCK)
        kp = jnp.arange(hi)
        s = jnp.where(kp[None, :] <= qp[:, None], s, -jnp.inf)
        p = jax.nn.softmax(s, axis=-1).astype(v.dtype)
        outs.append(jnp.einsum('bhqk,bkhd->bqhd', p, v[:, :hi]))
    return jnp.concatenate(outs, axis=1)


def setup_inputs(seed: int = 0) -> dict:
    key = jax.random.key(seed)
    ks = jax.random.split(key, 24)
    f32 = jnp.float32
    nrm = lambda k, shape, s: jax.random.normal(k, shape, f32) * s
    gain = lambda k, n: 1.0 + 0.05 * jax.random.normal(k, (DEPTH, n), f32)
    x = jax.random.normal(ks[0], (BATCH, SEQ, D_MODEL), f32)
    c = jax.random.normal(ks[1], (BATCH, D_MODEL), f32)
    offset = jax.random.randint(ks[2], (BATCH, 1), 0, 4096, dtype=jnp.int32)
    positions = offset + jnp.arange(SEQ, dtype=jnp.int32)[None, :]
    return {
        "x": x,
        "c": c,
        "positions": positions,
        "w_ada": nrm(ks[3], (DEPTH, D_MODEL, 3 * D_MODEL), 0.5 * D_MODEL ** -0.5),
        "b_ada": nrm(ks[4], (DEPTH, 3 * D_MODEL), 0.02),
        "norm_w": gain(ks[5], D_MODEL),
        "w_in": nrm(ks[6], (DEPTH, D_MODEL, D_IN), D_MODEL ** -0.5),
        "b_f": jax.random.uniform(ks[7], (DEPTH, N_HEADS_B), f32, 1.0, 5.0),
        "q_lora_norm_w": gain(ks[8], Q_LORA),
        "kv_lora_norm_w": gain(ks[9], KV_LORA),
        "w_uq": nrm(ks[10], (DEPTH, Q_LORA, N_HEADS_A * (NOPE_DIM + ROPE_DIM)), Q_LORA ** -0.5),
        "w_ukv": nrm(ks[11], (DEPTH, KV_LORA, N_HEADS_A * (NOPE_DIM + V_DIM_A)), KV_LORA ** -0.5),
        "qn_nope_a": gain(ks[12], NOPE_DIM),
        "qn_rope_a": gain(ks[13], ROPE_DIM),
        "kn_nope_a": gain(ks[14], NOPE_DIM),
        "kn_rope_a": gain(ks[15], ROPE_DIM),
        "qn_b": gain(ks[16], HEAD_DIM_B),
        "kn_b": gain(ks[17], HEAD_DIM_B),
        "w_branch_a": nrm(ks[18], (DEPTH, WIDTH_A, D_MODEL), WIDTH_A ** -0.5),
        "w_branch_b": nrm(ks[19], (DEPTH, WIDTH_B, D_MODEL), WIDTH_B ** -0.5),
        "w_out": nrm(ks[20], (DEPTH, D_MODEL, D_MODEL), D_MODEL ** -0.5),
    }


def reference(x, c, positions, w_ada, b_ada, norm_w, w_in, b_f, q_lora_norm_w,
              kv_lora_norm_w, w_uq, w_ukv, qn_nope_a, qn_rope_a, kn_nope_a,
              kn_rope_a, qn_b, kn_b, w_branch_a, w_branch_b, w_out):
    B, S, _ = x.shape
    cos, sin = rope_angles(positions)
    split_points = np.cumsum(SPLITS)[:-1].tolist()
    for l in range(DEPTH):
        ada = c @ w_ada[l] + b_ada[l]
        shift, scale, gate = jnp.split(ada, 3, axis=-1)
        h = rms_norm(x, norm_w[l]) * (1.0 + scale[:, None, :]) + shift[:, None, :]

        proj = h @ w_in[l]
        (cq, ckv, k_rope, gate_a, q_b, k_b, v_b, f_b, gate_b,
         mg_a, mg_b) = jnp.split(proj, split_points, axis=-1)

        q = (rms_norm(cq, q_lora_norm_w[l]) @ w_uq[l]).reshape(B, S, N_HEADS_A, NOPE_DIM + ROPE_DIM)
        q_nope, q_rope = q[..., :NOPE_DIM], q[..., NOPE_DIM:]
        kv = (rms_norm(ckv, kv_lora_norm_w[l]) @ w_ukv[l]).reshape(B, S, N_HEADS_A, NOPE_DIM + V_DIM_A)
        k_nope, v_a = kv[..., :NOPE_DIM], kv[..., NOPE_DIM:]
        q_nope = rms_norm(q_nope, qn_nope_a[l])
        q_rope = apply_rope(rms_norm(q_rope, qn_rope_a[l]), cos[:, :, None, :], sin[:, :, None, :])
        k_nope = rms_norm(k_nope, kn_nope_a[l])
        k_rope = apply_rope(rms_norm(k_rope, kn_rope_a[l]), cos, sin)
        o_a = mla_attention(q_nope, q_rope, k_nope, k_rope, v_a).reshape(B, S, WIDTH_A)
        y_a = o_a * jax.nn.silu(gate_a)

        qb = rms_norm(q_b.reshape(B, S, N_HEADS_B, HEAD_DIM_B), qn_b[l])
        kb = rms_norm(k_b.reshape(B, S, N_HEADS_B, HEAD_DIM_B), kn_b[l])
        vb = v_b.reshape(B, S, N_HEADS_B, HEAD_DIM_B)
        log_f = jax.nn.log_sigmoid(f_b.astype(jnp.float32) + b_f[l].astype(jnp.float32))
        o_b = fox_attention(qb, kb, vb, log_f).reshape(B, S, WIDTH_B)
        y_b = o_b * jax.nn.silu(gate_b)

        merged = (jax.nn.sigmoid(mg_a) * (y_a @ w_branch_a[l])
                  + jax.nn.sigmoid(mg_b) * (y_b @ w_branch_b[l]))
        x = x + gate[:, None, :] * (merged @ w_out[l])
    return x
```

```python
import math
from contextlib import ExitStack

import numpy as np
import concourse.bass as bass
import concourse.mybir as mybir
from concourse.bass_utils import run_bass_kernel_spmd

F32 = mybir.dt.float32
BF16 = mybir.dt.bfloat16
I32 = mybir.dt.int32
AF = mybir.ActivationFunctionType
ALU = mybir.AluOpType
AX = mybir.AxisListType

NCORES = 8
BPC = 4
S = 2048
D = 1024
NT = 16
D_IN = 5288
EPS = 1e-6
C_CQ, C_CKV, C_KR, C_GA, C_QB, C_KB, C_VB, C_FB, C_GB, C_MA, C_MB = (
    0, 384, 640, 672, 1184, 1696, 2208, 2720, 2728, 3240, 4264)
TWO_PI_HI = 6.28125
TWO_PI_LO = 2.0 * math.pi - 6.28125
NEG = -30000.0


class Buf:
    __slots__ = ("w", "r", "excl")

    def __init__(self, excl=False):
        self.w = None
        self.r = {}
        self.excl = excl


class _TagList(list):
    tag = [""]

    def append(self, x):
        list.append(self, x)
        self.tags.append(_TagList.tag[0])


class _Eng:
    def __init__(self, name, semkey):
        self.name = name
        self.semkey = semkey
        self.count = 0
        self.waited = {}
        self.ops = _TagList()
        self.ops.tags = []
        self.dma_n = 0
        self.dma_vals = {}


class _Rec:
    def __init__(self):
        self.calls = []

    def __getattr__(self, name):
        def f(*a, **k):
            self.calls.append((name, a, k))
        return f


class Prog:
    ENGS = ("pe", "act", "dve", "pool", "sp")
    NDMA = {"sp": 8, "act": 2, "pool": 2}

    def __init__(self, nc):
        self.nc = nc
        self.E = {n: _Eng(n, ("eng", n)) for n in self.ENGS}
        self.semkeys = [e.semkey for e in self.E.values()]
        for q, k in self.NDMA.items():
            for i in range(k):
                self.semkeys.append(("dma", q, i))
                self.E[q].dma_vals[i] = 0

    def _wait(self, E, k, v):
        if E.waited.get(k, 0) < v:
            E.ops.append(("wait", k, v))
            E.waited[k] = v

    def _need(self, E, reads, writes):
        need = {}
        for b in reads:
            if b.w is not None and need.get(b.w[0], 0) < b.w[1]:
                need[b.w[0]] = b.w[1]
        for b in writes:
            if b.w is not None and need.get(b.w[0], 0) < b.w[1]:
                need[b.w[0]] = b.w[1]
            for k, v in b.r.items():
                if need.get(k, 0) < v:
                    need[k] = v
        for k, v in need.items():
            if k == E.semkey:
                if E.name == "pe" or v > E.count:
                    continue
            self._wait(E, k, v)

    def _mark(self, tok, reads, writes):
        for b in writes:
            b.w = tok
            b.r = {}
        for b in reads:
            if b.r.get(tok[0], 0) < tok[1]:
                b.r[tok[0]] = tok[1]

    def op(self, eng, fn, reads=(), writes=(), inc=True):
        E = self.E[eng]
        if any(b.excl for b in reads):
            writes = list(writes) + [b for b in reads if b.excl]
            reads = [b for b in reads if not b.excl]
        self._need(E, reads, writes)
        tok = (E.semkey, E.count + 1)
        rec = _Rec()
        fn(rec)
        E.ops.append(("inst", rec.calls[0], inc))
        if inc:
            E.count += 1
        self._mark(tok, reads, writes)

    def dma(self, q, out, in_, reads=(), writes=(), **kw):
        E = self.E[q]
        slot = E.dma_n % self.NDMA[q]
        E.dma_n += 1
        k = ("dma", q, slot)
        prev = E.dma_vals[slot]
        if prev > 0:
            self._wait(E, k, prev)
        self._need(E, reads, writes)
        E.dma_vals[slot] = prev + 16
        E.ops.append(("dma", out, in_, k, kw))
        self._mark((k, prev + 16), reads, writes)

    def barrier(self):
        toks = []
        for n in ("pe", "act", "dve", "pool"):
            e = self.E[n]
            if e.count > 0:
                toks.append((e.semkey, e.count))
        for q in self.NDMA:
            for slot, v in self.E[q].dma_vals.items():
                if v > 0:
                    toks.append((("dma", q, slot), v))
        for n in self.ENGS:
            E = self.E[n]
            for k, v in toks:
                if k != E.semkey:
                    self._wait(E, k, v)

    def finish(self):
        self.barrier()

    def emit(self, stack):
        nc = self.nc
        sems = {}
        for k in self.semkeys:
            sems[k] = stack.enter_context(nc.semaphore("s_" + "_".join(str(x) for x in k)))
        block = stack.enter_context(nc.Block())

        def run(E):
            def body(e):
                own = sems[E.semkey]
                for o in E.ops:
                    if o[0] == "wait":
                        e.wait_ge(sems[o[1]], o[2])
                    elif o[0] == "inst":
                        name, a, k = o[1]
                        ins = getattr(e, name)(*a, **k)
                        if o[2]:
                            ins.then_inc(own, 1)
                    else:
                        e.dma_start(out=o[1], in_=o[2], **o[4]).then_inc(sems[o[3]], 16)
            return body

        block.tensor(run(self.E["pe"]))
        block.scalar(run(self.E["act"]))
        block.vector(run(self.E["dve"]))
        block.gpsimd(run(self.E["pool"]))
        block.sync(run(self.E["sp"]))


_DBG = {}


class _Stop(Exception):
    pass


def build_nc(nseq=BPC, stage=99):
    def stage_check(n):
        if stage == n:
            raise _Stop()

    nc = bass.Bass("TRN2", target_bir_lowering=False)
    din = lambda n, s, dt=F32: nc.dram_tensor(n, list(s), dt, kind="ExternalInput").ap()
    x_d = din("x", [BPC, S, D])
    cT_d = din("cT", [128, 8, BPC])
    pos_d = din("pos", [128, BPC, NT], I32)
    wada_d = din("w_ada", [D, 3 * D])
    bada_d = din("b_ada", [1, 3 * D])
    normw_d = din("normw", [128, 8])
    win_d = din("w_in", [D, D_IN])
    bf_d = din("b_f", [1, 8])
    qlw_d = din("qlw", [128, 3])
    kvlw_d = din("kvlw", [128, 2])
    wuq_d = din("w_uq", [384, 768])
    wukv_d = din("w_ukv", [256, 1024])
    gqna_d = din("g_qna", [1, 64])
    gqra_d = din("g_qra", [1, 32])
    gkna_d = din("g_kna", [1, 64])
    gkra_d = din("g_kra", [1, 32])
    gqb_d = din("g_qb", [1, 64])
    gkb_d = din("g_kb", [1, 64])
    wba_d = din("w_ba", [512, D])
    wbb_d = din("w_bb", [512, D])
    wout_d = din("w_out", [D, D])
    ident_d = din("ident", [128, 128])
    tri_d = din("tri", [128, 128])
    fmask_d = din("fmask", [128, 128])
    mmask_d = din("mmask", [128, 128])
    invf_d = din("invf", [1, 16])
    gcols_d = din("gcols", [128, 4])
    out_d = nc.dram_tensor("out", [BPC, S, D], F32, kind="ExternalOutput").ap()
    wbf_d = nc.dram_tensor("wbf_scr", [128, 8 * D_IN], BF16).ap()
    wbf3 = wbf_d.rearrange("p (c n) -> p c n", c=8)
    ada_d = nc.dram_tensor("ada_scr", [BPC, 3 * D], F32).ap()

    P = Prog(nc)
    op, dma = P.op, P.dma
    with ExitStack() as st:
        try:
            _n = [0]

            def sb(shape, dt):
                _n[0] += 1
                return st.enter_context(nc.sbuf_tensor("sb%d" % _n[0], list(shape), dt))

            ident = sb([128, 128], BF16)
            identf = sb([128, 128], F32)
            tri = sb([128, 128], F32)
            ones = sb([128, 128], F32)
            negones = sb([128, 128], F32)
            twos = sb([128, 128], BF16)
            fmask = sb([128, 128], F32)
            mmask = sb([128, 128], F32)
            fmaskb = sb([128, 128], BF16)
            mmaskb = sb([128, 128], BF16)
            g_qna = sb([128, 64], F32)
            g_qra = sb([128, 32], F32)
            g_kna = sb([128, 64], F32)
            g_kra = sb([128, 32], F32)
            g_qb = sb([128, 64], F32)
            g_kb = sb([128, 64], F32)
            bf_bc = sb([128, 8], F32)
            invf = sb([128, 16], F32)
            cT = sb([128, 8, BPC], F32)
            normw = sb([128, 8], F32)
            qlw = sb([128, 3], F32)
            kvlw = sb([128, 2], F32)
            pos_sb = sb([128, BPC, NT], I32)
            wuq = sb([128, 3, 1024], BF16)
            wukv = sb([128, 2, 1024], BF16)
            wba = sb([128, 4, D], BF16)
            wbb = sb([128, 4, D], BF16)
            wout = sb([128, 8, D], BF16)
            hT = sb([128, 8, S], BF16)
            yT = sb([128, 8, S], BF16)
            ovl1 = sb([128, D], F32)
            gate_bc = ovl1
            sc_col = sb([128, 8], F32)
            sh_col = sb([128, 8], F32)
            A_col = sb([128, 8], F32)
            posf = sb([128, NT], F32)
            GCS = sb([128, NT, 64], F32)
            gcols = sb([128, 4], F32)
            g_qra_s = sb([128, 64], F32)
            sint = sb([128, NT, 16], F32)
            cost = sb([128, NT, 16], F32)
            ssq = sb([128, NT], F32)
            sskv = sb([128, NT], F32)
            epsq = sb([128, NT], F32)
            epskv = sb([128, NT], F32)
            rstdkv = sb([128, NT], F32)
            kfraw = sb([128, NT, 40], F32)
            krss = sb([128, NT], F32)
            krr = sb([128, NT, 32], BF16)
            spt = sb([128, NT, 8], F32)
            Wf = sb([128, NT * 8], F32)
            Wr = sb([128, NT * 8], F32)
            Ws = sb([128, NT, 8, 3], BF16)
            nWs = sb([128, NT, 8, 3], BF16)
            Ctab = sb([128, 48], F32)
            ssx = [sb([128, 1], F32) for _ in range(2)]
            rsx = [sb([128, 1], F32) for _ in range(2)]
            pt = [sb([128, 2, 512], BF16) for _ in range(2)]
            X = sb([128, 32768], BF16)

            def xv(off, n, dt):
                if dt == BF16:
                    return X[:, off // 2: off // 2 + n]
                return X[:, off // 2: off // 2 + 2 * n].bitcast(dt)

            psall = st.enter_context(nc.psum_tensor("psall", [128, 4096], F32))
            banks = [psall[:, i * 512:(i + 1) * 512] for i in range(8)]
            bB = [Buf(excl=True) for _ in range(8)]
            bankbf = [b.bitcast(BF16) for b in banks]

            B = {k: Buf() for k in (
                "ident", "identf", "tri", "ones", "negones", "twos", "fmask", "mmask", "maskb", "vtwos", "qkz", "gains", "bf", "invf", "cT",
                "normw", "qlw", "kvlw", "pos", "bada4", "ada4", "ada_d", "wuq", "wukv", "wba", "wbb", "wout", "wbf_d",
                "gate_bc", "cols", "A_col", "posf", "ang", "angk", "angi", "sint", "cost", "ssq", "sskv", "epsq", "epskv",
                "rstdkv", "GCS", "gcols", "kfraw", "krt", "krs", "krss", "krm", "krr", "spt", "Wf", "Wr", "Ws", "nWs", "Ctab")}
            b_hT = [Buf() for _ in range(NT)]
            b_yT = [[Buf() for _ in range(4)] for _ in range(8)]
            b_pt = [Buf() for _ in range(2)]
            b_ssx = [Buf(), Buf()]
            b_rsx = [Buf(), Buf()]

            _TagList.tag[0] = "setup"
            dma("sp", identf[:], ident_d, writes=[B["identf"]])
            op("dve", lambda e: e.tensor_copy(ident[:], identf[:]), reads=[B["identf"]], writes=[B["ident"]])
            dma("sp", tri[:], tri_d, writes=[B["tri"]])
            dma("sp", fmask[:], fmask_d, writes=[B["fmask"]])
            dma("sp", mmask[:], mmask_d, writes=[B["mmask"]])
            op("dve", lambda e: e.tensor_copy(fmaskb[:], fmask[:]), reads=[B["fmask"]], writes=[B["maskb"]])
            op("dve", lambda e: e.tensor_copy(mmaskb[:], mmask[:]), reads=[B["mmask"]], writes=[B["maskb"]])
            op("pool", lambda e: e.memset(ones[:], 1.0), writes=[B["ones"]])
            op("pool", lambda e: e.memset(negones[:], -1.0), writes=[B["negones"]])
            op("pool", lambda e: e.memset(twos[:], 2.0), writes=[B["twos"]])
            for t_, d_ in ((g_qna, gqna_d), (g_qra, gqra_d), (g_kna, gkna_d), (g_kra, gkra_d), (g_qb, gqb_d),
                           (g_kb, gkb_d), (bf_bc, bf_d), (invf, invf_d)):
                dma("sp", t_[:], d_[0].partition_broadcast(128), writes=[B["gains"]])
            op("dve", lambda e: e.tensor_scalar(g_qna[:], g_qna[:], 1.0 / math.sqrt(96.0), None, ALU.mult),
               reads=[B["gains"]], writes=[B["gains"]])
            op("dve", lambda e: e.tensor_scalar(g_qra[:], g_qra[:], 1.0 / math.sqrt(96.0), None, ALU.mult),
               reads=[B["gains"]], writes=[B["gains"]])
            op("dve", lambda e: e.tensor_scalar(g_qb[:], g_qb[:], 0.125, None, ALU.mult),
               reads=[B["gains"]], writes=[B["gains"]])
            dma("sp", gcols[:], gcols_d, writes=[B["gcols"]])
            op("dve", lambda e: e.tensor_scalar(gcols[0:64, 0:1], gcols[0:64, 0:1], 1.0 / math.sqrt(96.0), None, ALU.mult),
               reads=[B["gcols"]], writes=[B["gcols"]])
            op("dve", lambda e: e.tensor_scalar(gcols[0:64, 2:3], gcols[0:64, 2:3], 0.125, None, ALU.mult),
               reads=[B["gcols"]], writes=[B["gcols"]])
            op("dve", lambda e: e.tensor_copy(g_qra_s[:, 0:32], g_qra[:]), reads=[B["gains"]], writes=[B["gains"]])
            op("dve", lambda e: e.tensor_scalar(g_qra_s[:, 32:48], g_qra[:, 16:32], -1.0, None, ALU.mult),
               reads=[B["gains"]], writes=[B["gains"]])
            op("dve", lambda e: e.tensor_copy(g_qra_s[:, 48:64], g_qra[:, 0:16]), reads=[B["gains"]], writes=[B["gains"]])
            dma("sp", cT[:], cT_d, writes=[B["cT"]])
            dma("sp", normw[:], normw_d, writes=[B["normw"]])
            dma("sp", qlw[:], qlw_d, writes=[B["qlw"]])
            dma("sp", kvlw[:], kvlw_d, writes=[B["kvlw"]])
            dma("sp", pos_sb[:], pos_d, writes=[B["pos"]])
            pass

            bada4 = xv(32768, 3 * D, F32)[0:BPC, :]
            ada4 = xv(45056, 3 * D, F32)[0:BPC, :]
            dma("sp", bada4, bada_d[0].partition_broadcast(BPC), writes=[B["bada4"]])
            stg = [xv(0, 8 * 512, F32).rearrange("p (c n) -> p c n", c=8),
                   xv(16384, 8 * 512, F32).rearrange("p (c n) -> p c n", c=8)]
            b_stg = [Buf(), Buf()]
            wada3 = wada_d.rearrange("(c p) n -> p c n", p=128)
            for n in range(6):
                s_ = n % 2
                dma("sp", stg[s_], wada3[:, :, n * 512:(n + 1) * 512], writes=[b_stg[s_]])
                for kc in range(8):
                    op("pe", lambda e, s_=s_, kc=kc, n=n: e.matmul(banks[n % 2][0:BPC, :], cT[:, kc, :], stg[s_][:, kc, :],
                                                                  start=(kc == 0), stop=(kc == 7)),
                       reads=[B["cT"], b_stg[s_]], writes=[bB[n % 2]], inc=(kc == 7))
                op("dve", lambda e, n=n: e.tensor_tensor(ada4[:, n * 512:(n + 1) * 512], banks[n % 2][0:BPC, :],
                                                         bada4[:, n * 512:(n + 1) * 512], ALU.add),
                   reads=[bB[n % 2], B["bada4"]], writes=[B["ada4"]])
            dma("sp", ada_d, ada4, reads=[B["ada4"]], writes=[B["ada_d"]])
            P.barrier()

            sA = xv(0, 3 * 768, F32).rearrange("p (c n) -> p c n", c=3)
            sBv = xv(16384, 2 * 1024, F32).rearrange("p (c n) -> p c n", c=2)
            b_sA, b_sB = Buf(), Buf()
            dma("sp", sA, wuq_d.rearrange("(c p) n -> p c n", p=128), writes=[b_sA])
            dma("sp", sBv, wukv_d.rearrange("(c p) n -> p c n", p=128), writes=[b_sB])
            for c in range(3):
                src3 = sA[:, c, :].rearrange("p (h n) -> p h n", h=8)
                dst3 = wuq[:, c, :].rearrange("p (h n) -> p h n", h=8)
                for d0, d1, s0, s1 in ((0, 96, 0, 96), (96, 112, 80, 96), (112, 128, 64, 80)):
                    op("dve", lambda e, c=c, src3=src3, dst3=dst3, d0=d0, d1=d1, s0=s0, s1=s1: e.tensor_scalar(
                        dst3[:, :, d0:d1], src3[:, :, s0:s1], qlw[:, c:c + 1], None, ALU.mult),
                       reads=[b_sA, B["qlw"]], writes=[B["wuq"]])
            for c in range(2):
                op("dve", lambda e, c=c: e.tensor_scalar(wukv[:, c, :], sBv[:, c, :], kvlw[:, c:c + 1], None, ALU.mult),
                   reads=[b_sB, B["kvlw"]], writes=[B["wukv"]])
            P.barrier()
            sW = [xv(0, 4 * 1024, F32).rearrange("p (c n) -> p c n", c=4),
                  xv(16384, 4 * 1024, F32).rearrange("p (c n) -> p c n", c=4)]
            b_sW = [Buf(), Buf()]
            jobs = [(wba_d.rearrange("(c p) n -> p c n", p=128), wba, 0, "wba"),
                    (wbb_d.rearrange("(c p) n -> p c n", p=128), wbb, 0, "wbb"),
                    (wout_d.rearrange("(c p) n -> p c n", p=128)[:, 0:4, :], wout, 0, "wout"),
                    (wout_d.rearrange("(c p) n -> p c n", p=128)[:, 4:8, :], wout, 4, "wout")]
            for i, (src, dst, c0, key) in enumerate(jobs):
                s_ = i % 2
                dma("sp", sW[s_], src, writes=[b_sW[s_]])
                eng = "dve" if s_ == 0 else "pool"
                op(eng, lambda e, s_=s_, dst=dst, c0=c0: e.tensor_copy(dst[:, c0:c0 + 4, :], sW[s_]),
                   reads=[b_sW[s_]], writes=[B[key]])
            P.barrier()
            HALF = D_IN // 2
            sI = [xv(0, HALF, F32), xv(16384, HALF, F32)]
            sO = [xv(32768, HALF, BF16), xv(40960, HALF, BF16)]
            b_sI = [Buf(), Buf()]
            b_sO = [Buf(), Buf()]
            i = 0
            for kc in range(8):
                for hf in range(2):
                    s_ = i % 2
                    dma("sp", sI[s_], win_d[kc * 128:(kc + 1) * 128, hf * HALF:(hf + 1) * HALF], writes=[b_sI[s_]])
                    eng = ("dve", "pool", "act")[i % 3]
                    if eng == "act":
                        op(eng, lambda e, s_=s_: e.copy(sO[s_], sI[s_]), reads=[b_sI[s_]], writes=[b_sO[s_]])
                    else:
                        op(eng, lambda e, s_=s_: e.tensor_copy(sO[s_], sI[s_]), reads=[b_sI[s_]], writes=[b_sO[s_]])
                    dma("sp", wbf3[:, kc, hf * HALF:(hf + 1) * HALF], sO[s_], reads=[b_sO[s_]], writes=[B["wbf_d"]])
                    i += 1
            P.barrier()

            stage_check(1)
            cqT = xv(0, 3 * S, BF16).rearrange("p (c n) -> p c n", c=3)
            ckvT = xv(12288, 2 * S, BF16).rearrange("p (c n) -> p c n", c=2)
            b_cqT = [Buf() for _ in range(NT)]
            xt = [xv(20480, D, F32), xv(24576, D, F32)]
            junk = xv(28672, D, BF16)
            xn = [xv(30720, D, BF16), xv(32768, D, BF16)]
            wg1 = xv(34816, 8 * 680, BF16).rearrange("p (c n) -> p c n", c=8)
            cqb = [xv(45696, 640, BF16), xv(46976, 640, BF16)]
            htmp = [xv(48256, D, F32), xv(52352, D, F32)]
            ang = xv(60544, NT * 16, F32).rearrange("p (t n) -> p t n", t=NT)
            angk = xv(61568, NT * 16, F32).rearrange("p (t n) -> p t n", t=NT)
            angi = xv(62592, NT * 16, I32).rearrange("p (t n) -> p t n", t=NT)
            krt = xv(56448, NT * 32, F32).rearrange("p (t n) -> p t n", t=NT)
            krs = xv(58496, NT * 32, F32).rearrange("p (t n) -> p t n", t=NT)
            krm = [xv(60544 + 1024 * k_, NT * 16, F32).rearrange("p (t n) -> p t n", t=NT) for k_ in range(4)]
            QKT = xv(20480, 4 * S, BF16).rearrange("p (a n) -> p a n", a=4)
            Vt = xv(36864, NT * 192, BF16).rearrange("p (t n) -> p t n", t=NT)
            gate2 = xv(43008, S, F32)
            wpair = xv(51200, 8 * 512, BF16).rearrange("p (c n) -> p c n", c=8)
            WB = 59392
            sq = [xv(WB, 512, F32), xv(WB + 2048, 512, F32)]
            tmpr = [xv(WB + 4096, 128, F32), xv(WB + 4608, 128, F32)]
            tmpr2 = [xv(WB + 5120, 128, F32), xv(WB + 5632, 128, F32)]
            wmg = xv(0, 8 * 2048, BF16).rearrange("p (c n) -> p c n", c=8)
            mT = xv(32768, 8 * 512, BF16).rearrange("p (c n) -> p c n", c=8)
            ta = [xv(40960, 512, F32), xv(43008, 512, F32)]
            m12 = [xv(45056, 512, F32), xv(47104, 512, F32)]
            xr = [xv(49152, D, F32), xv(53248, D, F32)]
            res = [xv(57344, D, F32), xv(61440, D, F32)]
            st6 = [sb([128, 8], F32) for _ in range(3)]
            rs6 = [sb([128, 8], F32) for _ in range(3)]
            sq.append(sb([128, 512], F32)[:])
            tmpr.append(sb([128, 128], F32)[:])
            tmpr2.append(sb([128, 128], F32)[:])
            qk = [sb([128, 384], BF16) for _ in range(3)]
            rd = ovl1[:, 0:512]
            tmpo = ovl1[:, 512:1024]
            tg = sb([128, 512], F32)
            b_rd, b_tmpo, b_tg = Buf(), Buf(), Buf()

            def bc2(ap2, n):
                return ap2.unsqueeze(2).broadcast_to([128, ap2.shape[1], n])

            def bch(ap2, h):
                return ap2.unsqueeze(1).broadcast_to([128, h, ap2.shape[1]])

            for b in range(nseq):
                _TagList.tag[0] = "s%d.p1" % b
                dma("sp", sh_col[:], ada_d[b, 0:D].rearrange("(c p) -> p c", p=128), reads=[B["ada_d"]],
                    writes=[B["cols"]], allow_slow_non_contiguous=True)
                dma("sp", sc_col[:], ada_d[b, D:2 * D].rearrange("(c p) -> p c", p=128), reads=[B["ada_d"]],
                    writes=[B["cols"]], allow_slow_non_contiguous=True)
                op("dve", lambda e: e.scalar_tensor_tensor(A_col[:], sc_col[:], 1.0, normw[:], ALU.add, ALU.mult),
                   reads=[B["cols"], B["normw"]], writes=[B["A_col"]])
                stage_check(11)
                op("dve", lambda e, b=b: e.tensor_copy(posf[:], pos_sb[:, b, :]), reads=[B["pos"]], writes=[B["posf"]])
                op("dve", lambda e: e.tensor_tensor(ang, bc2(posf[:], 16), bch(invf[:], NT), ALU.mult),
                   reads=[B["posf"], B["gains"]], writes=[B["ang"]])

                def reduce_angle(dst_key_unused=None):
                    op("dve", lambda e: e.tensor_scalar(angk, ang, 1.0 / (2.0 * math.pi), None, ALU.mult),
                       reads=[B["ang"]], writes=[B["angk"]])
                    op("dve", lambda e: e.tensor_copy(angi, angk), reads=[B["angk"]], writes=[B["angi"]])
                    op("dve", lambda e: e.tensor_copy(angk, angi), reads=[B["angi"]], writes=[B["angk"]])
                    op("dve", lambda e: e.scalar_tensor_tensor(ang, angk, -TWO_PI_HI, ang, ALU.mult, ALU.add),
                       reads=[B["angk"], B["ang"]], writes=[B["ang"]])
                    op("dve", lambda e: e.scalar_tensor_tensor(ang, angk, -TWO_PI_LO, ang, ALU.mult, ALU.add),
                       reads=[B["angk"], B["ang"]], writes=[B["ang"]])
                    op("dve", lambda e: e.tensor_scalar(ang, ang, math.pi, -math.pi, ALU.min, ALU.max),
                       reads=[B["ang"]], writes=[B["ang"]])

                reduce_angle()
                op("act", lambda e: e.activation(sint[:], ang, AF.Sin), reads=[B["ang"]], writes=[B["sint"]])
                op("dve", lambda e: e.tensor_scalar(ang, ang, 0.5 * math.pi, None, ALU.add),
                   reads=[B["ang"], B["sint"]], writes=[B["ang"]])
                reduce_angle()
                op("act", lambda e: e.activation(cost[:], ang, AF.Sin), reads=[B["ang"]], writes=[B["cost"]])
                G4 = GCS[:].rearrange("p t (k n) -> p t k n", k=4)
                for k_, tab, key in ((0, cost, "cost"), (1, cost, "cost"), (2, sint, "sint"), (3, sint, "sint")):
                    op("dve", lambda e, k_=k_, tab=tab: e.tensor_tensor(
                        G4[:, :, k_, :], tab[:], bch(g_qra_s[:, k_ * 16:(k_ + 1) * 16], NT), ALU.mult),
                       reads=[B[key], B["gains"]], writes=[B["GCS"]])

                stage_check(12)
                b_wg1 = Buf()
                dma("sp", wg1[:, :, 0:672], wbf3[:, :, 0:672], reads=[B["wbf_d"]], writes=[b_wg1])
                dma("sp", wg1[:, :, 672:680], wbf3[:, :, C_FB:C_FB + 8], reads=[B["wbf_d"]], writes=[b_wg1])

                b_xt = [Buf(), Buf()]
                b_junk = Buf()
                b_xn = [Buf(), Buf()]
                b_cqb = [Buf(), Buf()]
                b_htmp = [Buf(), Buf()]
                def p1(t, part):
                  s_ = t % 2
                  ts = slice(t * 128, (t + 1) * 128)
                  gA, gB = 2 + s_, 4 + s_
                  tb = 6 + s_
                  if part == 1:
                    dma("sp", xt[s_], x_d[b, ts, :], writes=[b_xt[s_]])
                    op("act", lambda e, s_=s_: e.activation(junk, xt[s_], AF.Square, accum_out=ssx[s_][:]),
                       reads=[b_xt[s_]], writes=[b_junk, b_ssx[s_]])
                    op("act", lambda e, s_=s_: e.activation(rsx[s_][:], ssx[s_][:], AF.Ln, bias=EPS, scale=1.0 / D),
                       reads=[b_ssx[s_]], writes=[b_rsx[s_]])
                    op("act", lambda e, s_=s_: e.activation(rsx[s_][:], rsx[s_][:], AF.Exp, scale=-0.5),
                       reads=[b_rsx[s_]], writes=[b_rsx[s_]])
                    op("dve", lambda e, s_=s_: e.tensor_scalar(xn[s_], xt[s_], rsx[s_][:], None, ALU.mult),
                       reads=[b_xt[s_], b_rsx[s_]], writes=[b_xn[s_]])
                  pb = s_
                  if part == 2:
                    for c in range(8):
                        op("pe", lambda e, c=c, s_=s_, pb=pb: e.transpose(bankbf[pb][:, c * 128:(c + 1) * 128],
                                                                          xn[s_][:, c * 128:(c + 1) * 128], ident[:]),
                           reads=[b_xn[s_], B["ident"]], writes=[bB[pb]], inc=(c == 7))
                    op("dve", lambda e, s_=s_, pb=pb: e.tensor_tensor(
                        htmp[s_].rearrange("p (c n) -> p c n", c=8), bankbf[pb].rearrange("p (c n) -> p c n", c=8),
                        bc2(A_col[:], 128), ALU.mult),
                       reads=[bB[pb], B["A_col"]], writes=[b_htmp[s_]])
                    op("pool", lambda e, s_=s_, ts=ts: e.tensor_tensor(
                        hT[:, :, ts], htmp[s_].rearrange("p (c n) -> p c n", c=8), bc2(sh_col[:], 128), ALU.add),
                       reads=[b_htmp[s_], B["cols"]], writes=[b_hT[t]])
                  if part == 3:
                    for c in range(8):
                        op("pe", lambda e, c=c, ts=ts, gA=gA: e.matmul(banks[gA][:, 0:384], hT[:, c, ts], wg1[:, c, 0:384],
                                                                       start=(c == 0), stop=(c == 7)),
                           reads=[b_hT[t], b_wg1], writes=[bB[gA]], inc=(c == 7))
                    for c in range(8):
                        op("pe", lambda e, c=c, ts=ts, gB=gB: e.matmul(banks[gB][:, 0:296], hT[:, c, ts], wg1[:, c, 384:680],
                                                                       start=(c == 0), stop=(c == 7)),
                           reads=[b_hT[t], b_wg1], writes=[bB[gB]], inc=(c == 7))

                    op("dve", lambda e, gA=gA, s_=s_: e.tensor_copy(cqb[s_][:, 0:384], banks[gA][:, 0:384]),
                       reads=[bB[gA]], writes=[b_cqb[s_]])
                    op("dve", lambda e, gB=gB, s_=s_: e.tensor_copy(cqb[s_][:, 384:640], banks[gB][:, 0:256]),
                       reads=[bB[gB]], writes=[b_cqb[s_]])
                    op("act", lambda e, gA=gA, t=t: e.activation(junk[:, 0:384], banks[gA][:, 0:384], AF.Square,
                                                                 accum_out=ssq[:, t:t + 1]),
                       reads=[bB[gA]], writes=[b_junk, B["ssq"]])
                    op("act", lambda e, gB=gB, t=t: e.activation(junk[:, 0:256], banks[gB][:, 0:256], AF.Square,
                                                                 accum_out=sskv[:, t:t + 1]),
                       reads=[bB[gB]], writes=[b_junk, B["sskv"]])
                    op("act", lambda e, gB=gB, t=t: e.copy(kfraw[:, t, :], banks[gB][:, 256:296]),
                       reads=[bB[gB]], writes=[B["kfraw"]])
                  if part == 4:
                    for c in range(5):
                        op("pe", lambda e, c=c, s_=s_, tb=tb: e.transpose(bankbf[tb][:, c * 128:(c + 1) * 128],
                                                                          cqb[s_][:, c * 128:(c + 1) * 128], ident[:]),
                           reads=[b_cqb[s_], B["ident"]], writes=[bB[tb]], inc=(c == 4))
                    op("dve", lambda e, tb=tb, ts=ts: e.tensor_copy(
                        cqT[:, :, ts], bankbf[tb][:, 0:384].rearrange("p (c n) -> p c n", c=3)),
                       reads=[bB[tb]], writes=[b_cqT[t]])
                    op("act", lambda e, tb=tb, ts=ts: e.copy(
                        ckvT[:, :, ts], bankbf[tb][:, 384:640].rearrange("p (c n) -> p c n", c=2)),
                       reads=[bB[tb]], writes=[b_cqT[t]])


                for t in range(NT + 3):
                    if t < NT:
                        p1(t, 1)
                    if 1 <= t < NT + 1:
                        p1(t - 1, 2)
                    if 2 <= t < NT + 2:
                        p1(t - 2, 3)
                    if t >= 3:
                        p1(t - 3, 4)

                stage_check(2)
                _TagList.tag[0] = "s%d.p1c" % b
                op("dve", lambda e: e.tensor_scalar(epsq[:], ssq[:], EPS / 384.0, EPS * EPS, ALU.mult, ALU.add),
                   reads=[B["ssq"]], writes=[B["epsq"]])
                op("dve", lambda e: e.tensor_scalar(epskv[:], sskv[:], EPS / 256.0, EPS * EPS, ALU.mult, ALU.add),
                   reads=[B["sskv"]], writes=[B["epskv"]])
                op("act", lambda e: e.activation(rstdkv[:], sskv[:], AF.Ln, bias=EPS, scale=1.0 / 256.0),
                   reads=[B["sskv"]], writes=[B["rstdkv"]])
                op("act", lambda e: e.activation(rstdkv[:], rstdkv[:], AF.Exp, scale=-0.5),
                   reads=[B["rstdkv"]], writes=[B["rstdkv"]])
                op("dve", lambda e: e.tensor_tensor(krs, kfraw[:, :, 0:32], kfraw[:, :, 0:32], ALU.mult),
                   reads=[B["kfraw"]], writes=[B["krs"]])
                op("dve", lambda e: e.tensor_reduce(krss[:], krs, AX.X, ALU.add), reads=[B["krs"]], writes=[B["krss"]])
                op("act", lambda e: e.activation(krss[:], krss[:], AF.Ln, bias=EPS, scale=1.0 / 32.0),
                   reads=[B["krss"]], writes=[B["krss"]])
                op("act", lambda e: e.activation(krss[:], krss[:], AF.Exp, scale=-0.5),
                   reads=[B["krss"]], writes=[B["krss"]])
                op("dve", lambda e: e.tensor_tensor(krt, kfraw[:, :, 0:32], bc2(krss[:], 32), ALU.mult),
                   reads=[B["kfraw"], B["krss"]], writes=[B["krt"]])
                op("dve", lambda e: e.tensor_tensor(krt, krt, bch(g_kra[:], NT), ALU.mult),
                   reads=[B["krt"], B["gains"]], writes=[B["krt"]])
                x1, x2 = krt[:, :, 0:16], krt[:, :, 16:32]
                op("dve", lambda e: e.tensor_tensor(krm[0], x1, cost[:], ALU.mult), reads=[B["krt"], B["cost"]], writes=[B["krm"]])
                op("dve", lambda e: e.tensor_tensor(krm[1], x2, sint[:], ALU.mult), reads=[B["krt"], B["sint"]], writes=[B["krm"]])
                op("dve", lambda e: e.tensor_tensor(krm[2], x2, cost[:], ALU.mult), reads=[B["krt"], B["cost"]], writes=[B["krm"]])
                op("dve", lambda e: e.tensor_tensor(krm[3], x1, sint[:], ALU.mult), reads=[B["krt"], B["sint"]], writes=[B["krm"]])
                op("dve", lambda e: e.tensor_tensor(krr[:, :, 0:16], krm[0], krm[1], ALU.subtract),
                   reads=[B["krm"]], writes=[B["krr"]])
                op("dve", lambda e: e.tensor_tensor(krr[:, :, 16:32], krm[2], krm[3], ALU.add),
                   reads=[B["krm"]], writes=[B["krr"]])
                op("dve", lambda e: e.tensor_tensor(spt[:], kfraw[:, :, 32:40], bch(bf_bc[:], NT), ALU.add),
                   reads=[B["kfraw"], B["gains"]], writes=[B["spt"]])
                op("act", lambda e: e.activation(spt[:], spt[:], AF.Exp, scale=-1.0), reads=[B["spt"]], writes=[B["spt"]])
                op("act", lambda e: e.activation(spt[:], spt[:], AF.Ln, bias=1.0), reads=[B["spt"]], writes=[B["spt"]])
                spt2 = spt[:].rearrange("p t h -> p (t h)")
                op("pe", lambda e: e.matmul(banks[0][:, 0:128], tri[:], spt2, start=True, stop=True),
                   reads=[B["tri"], B["spt"]], writes=[bB[0]])
                op("pe", lambda e: e.matmul(banks[1][:, 0:128], ones[:], spt2, start=True, stop=True),
                   reads=[B["ones"], B["spt"]], writes=[bB[1]])
                op("dve", lambda e: e.tensor_copy(Wf[:], banks[0][:, 0:128]), reads=[bB[0]], writes=[B["Wf"]])
                op("act", lambda e: e.copy(Wr[:], banks[1][:, 0:128]), reads=[bB[1]], writes=[B["Wr"]])
                scanA = (Wr[:].rearrange("p (t h) -> p t h", t=NT), "Wr")
                scanB = (spt[:], "spt")
                for d_ in (1, 2, 4, 8):
                    (A_, ka), (B_, kb) = scanA, scanB
                    op("dve", lambda e, A_=A_, B_=B_, d_=d_: e.tensor_copy(B_[:, 0:d_, :], A_[:, 0:d_, :]),
                       reads=[B[ka]], writes=[B[kb]])
                    op("dve", lambda e, A_=A_, B_=B_, d_=d_: e.tensor_tensor(B_[:, d_:NT, :], A_[:, d_:NT, :], A_[:, 0:NT - d_, :], ALU.add),
                       reads=[B[ka]], writes=[B[kb]])
                    scanA, scanB = scanB, scanA
                Wf3_ = Wf[:].rearrange("p (t h) -> p t h", t=NT)
                op("dve", lambda e: e.tensor_tensor(Wf3_[:, 1:NT, :], Wf3_[:, 1:NT, :], scanA[0][:, 0:NT - 1, :], ALU.add),
                   reads=[B["Wf"], B[scanA[1]]], writes=[B["Wf"]])
                Wf3 = Wf[:].rearrange("p (t h) -> p t h", t=NT)
                Wr3 = Wr[:].rearrange("p (t h) -> p t h", t=NT)
                op("dve", lambda e: e.tensor_copy(Ws[:, :, :, 0], Wf3), reads=[B["Wf"]], writes=[B["Ws"]])
                op("dve", lambda e: e.tensor_tensor(Wr3, Wf3, Ws[:, :, :, 0], ALU.subtract),
                   reads=[B["Wf"], B["Ws"]], writes=[B["Wr"]])
                op("dve", lambda e: e.tensor_copy(Ws[:, :, :, 1], Wr3), reads=[B["Wr"]], writes=[B["Ws"]])
                op("dve", lambda e: e.tensor_tensor(Wr3, Wr3, Ws[:, :, :, 1], ALU.subtract),
                   reads=[B["Wr"], B["Ws"]], writes=[B["Wr"]])
                op("dve", lambda e: e.tensor_copy(Ws[:, :, :, 2], Wr3), reads=[B["Wr"]], writes=[B["Ws"]])
                op("dve", lambda e: e.tensor_scalar(nWs[:], Ws[:], -1.0, None, ALU.mult), reads=[B["Ws"]], writes=[B["nWs"]])
                P.barrier()

                stage_check(3)
                op("pool", lambda e: e.memset(Vt[:, :, 64:128], 2.0), writes=[B["vtwos"]])
                op("pool", lambda e: e.memset(QKT[96:128, :, :], 0.0), writes=[B["qkz"]])
                for job in range(8):
                    _TagList.tag[0] = "s%d.j%d.proj" % (b, job)
                    mla = job < 4
                    hp = job % 4
                    Kd = 96 if mla else 70
                    ychunk = hp if mla else 4 + hp
                    b_wp = Buf()
                    if mla:
                        dma("sp", wpair[:, :, 0:128], wbf3[:, :, C_GA + hp * 128:C_GA + (hp + 1) * 128],
                            reads=[B["wbf_d"]], writes=[b_wp])
                    else:
                        for k_, c0 in enumerate((C_QB, C_KB, C_VB, C_GB)):
                            dma("sp", wpair[:, :, k_ * 128:(k_ + 1) * 128], wbf3[:, :, c0 + hp * 128:c0 + (hp + 1) * 128],
                                reads=[B["wbf_d"]], writes=[b_wp])
                    b_QKT = [Buf() for _ in range(NT)]
                    if job == 4:
                        op("pool", lambda e: e.memset(QKT[64:96, :, :], 0.0), writes=b_QKT + [B["qkz"]])
                    b_Vt = [Buf() for _ in range(NT)]
                    b_sq = [Buf(), Buf(), Buf()]
                    b_st = [Buf(), Buf(), Buf()]
                    b_rs = [Buf(), Buf(), Buf()]
                    b_tn = [Buf(), Buf(), Buf()]
                    b_tr = [Buf(), Buf(), Buf()]
                    b_rm = [Buf(), Buf(), Buf()]
                    b_qc = [Buf(), Buf(), Buf()]
                    b_kc = [Buf(), Buf(), Buf()]
                    b_g2 = [Buf() for _ in range(4)]
                    if not mla:
                        for s_ in range(3):
                            qk4 = qk[s_][:, 0:280].rearrange("p (a n) -> p a n", a=4)
                            op("pool", lambda e, qk4=qk4: e.memset(qk4[:, 0:2, 67:70], 1.0), writes=[b_qc[s_]])
                            op("pool", lambda e, qk4=qk4: e.memset(qk4[:, 2:4, 64:67], 1.0), writes=[b_qc[s_]])

                    def tile(t, part):
                        s_ = t % 3
                        ts = slice(t * 128, (t + 1) * 128)
                        pp = s_
                        tb = 6 + (t % 2)
                        vdst = Vt[:, t, :].rearrange("p (a n) -> p a n", a=3)[:, 0:3:2, :]
                        if mla:
                            pq = banks[pp][:, 0:256].rearrange("p (h n) -> p h n", h=2)
                            pkv = banks[pp][:, 256:512].rearrange("p (h n) -> p h n", h=2)
                            qk4 = qk[s_][:, 0:384].rearrange("p (a n) -> p a n", a=4)
                            rsq = rs6[s_][:, 0:4].rearrange("p (h k) -> p h k", k=2)
                            rsk = rs6[s_][:, 4:8].rearrange("p (h k) -> p h k", k=2)
                            tr3 = tmpr[s_].rearrange("p (h n) -> p h n", h=2)
                            tr23 = tmpr2[s_].rearrange("p (h n) -> p h n", h=2)
                            if part == 0:
                                for c in range(3):
                                    op("pe", lambda e, c=c: e.matmul(
                                        banks[pp][:, 0:256], cqT[:, c, ts], wuq[:, c, hp * 256:(hp + 1) * 256],
                                        start=(c == 0), stop=(c == 2)),
                                       reads=[b_cqT[t], B["wuq"]], writes=[bB[pp]], inc=False)
                                for c in range(2):
                                    op("pe", lambda e, c=c: e.matmul(
                                        banks[pp][:, 256:512], ckvT[:, c, ts], wukv[:, c, hp * 256:(hp + 1) * 256],
                                        start=(c == 0), stop=(c == 1)),
                                       reads=[b_cqT[t], B["wukv"]], writes=[bB[pp]], inc=(c == 1))
                                op("act", lambda e: e.activation(sq[s_], banks[pp][:, :], AF.Square),
                                   reads=[bB[pp]], writes=[b_sq[s_]])
                                op("dve", lambda e: e.tensor_reduce(st6[s_][:, 0:8], sq[s_].rearrange("p (a n) -> p a n", a=8),
                                                                    AX.X, ALU.add),
                                   reads=[b_sq[s_]], writes=[b_st[s_]])
                                op("act", lambda e: e.activation(rs6[s_][:, 0:4], st6[s_][:, 0:4], AF.Ln,
                                                                 bias=epsq[:, t:t + 1], scale=1.0 / 64.0),
                                   reads=[b_st[s_], B["epsq"]], writes=[b_rs[s_]])
                                op("act", lambda e: e.activation(rs6[s_][:, 4:8], st6[s_][:, 4:8], AF.Ln,
                                                                 bias=epskv[:, t:t + 1], scale=1.0 / 64.0),
                                   reads=[b_st[s_], B["epskv"]], writes=[b_rs[s_]])
                                op("act", lambda e: e.activation(rs6[s_][:, 0:8], rs6[s_][:, 0:8], AF.Exp, scale=-0.5),
                                   reads=[b_rs[s_]], writes=[b_rs[s_]])
                            elif part == 1:
                                op("dve", lambda e: e.tensor_tensor(qk4[:, 0:2, 0:64], pq[:, :, 0:64],
                                                                    rsq[:, :, 0:1].broadcast_to([128, 2, 64]), ALU.mult),
                                   reads=[bB[pp], b_rs[s_]], writes=[b_qc[s_]])
                                op("dve", lambda e: e.tensor_tensor(qk4[:, 2:4, 0:64], pkv[:, :, 0:64],
                                                                    rsk[:, :, 0:1].broadcast_to([128, 2, 64]), ALU.mult),
                                   reads=[bB[pp], b_rs[s_]], writes=[b_qc[s_]])
                                op("dve", lambda e: e.tensor_tensor(tr3, pq[:, :, 64:128],
                                                                    rsq[:, :, 1:2].broadcast_to([128, 2, 64]), ALU.mult),
                                   reads=[bB[pp], b_rs[s_]], writes=[b_tr[s_]])
                                op("pool", lambda e: e.tensor_tensor(tr23, tr3, bch(GCS[:, t, :], 2), ALU.mult),
                                   reads=[b_tr[s_], B["GCS"]], writes=[b_rm[s_]])
                                op("pool", lambda e: e.tensor_tensor(qk4[:, 0:2, 64:96], tr23[:, :, 0:32], tr23[:, :, 32:64], ALU.add),
                                   reads=[b_rm[s_]], writes=[b_qc[s_]])
                                op("pool", lambda e: e.tensor_copy(qk4[:, 2:4, 64:96], bch(krr[:, t, :], 2)),
                                   reads=[B["krr"]], writes=[b_qc[s_]])
                                op("act", lambda e: e.activation(
                                    vdst, pkv[:, :, 64:128], AF.Identity,
                                    scale=rstdkv[:, t:t + 1]),
                                   reads=[bB[pp], B["rstdkv"]], writes=[b_Vt[t]])
                            gc = 0
                        else:
                            p4 = banks[pp][:, 0:256].rearrange("p (a n) -> p a n", a=4)
                            qk4 = qk[s_][:, 0:280].rearrange("p (a n) -> p a n", a=4)
                            if part == 0:
                                for c in range(8):
                                    op("pe", lambda e, c=c: e.matmul(
                                        banks[pp][:, 0:384], hT[:, c, ts], wpair[:, c, 0:384], start=(c == 0), stop=(c == 7)),
                                       reads=[b_hT[t], b_wp], writes=[bB[pp]], inc=(c == 7))
                                op("act", lambda e: e.activation(sq[s_][:, 0:256], banks[pp][:, 0:256], AF.Square),
                                   reads=[bB[pp]], writes=[b_sq[s_]])
                                op("dve", lambda e: e.tensor_reduce(st6[s_][:, 0:4],
                                                                    sq[s_][:, 0:256].rearrange("p (a n) -> p a n", a=4), AX.X, ALU.add),
                                   reads=[b_sq[s_]], writes=[b_st[s_]])
                                op("act", lambda e: e.activation(rs6[s_][:, 0:4], st6[s_][:, 0:4], AF.Ln, bias=EPS,
                                                                 scale=1.0 / 64.0),
                                   reads=[b_st[s_]], writes=[b_rs[s_]])
                                op("act", lambda e: e.activation(rs6[s_][:, 0:4], rs6[s_][:, 0:4], AF.Exp, scale=-0.5),
                                   reads=[b_rs[s_]], writes=[b_rs[s_]])
                            elif part == 1:
                                op("dve", lambda e: e.tensor_tensor(qk4[:, :, 0:64], p4, bc2(rs6[s_][:, 0:4], 64), ALU.mult),
                                   reads=[bB[pp], b_rs[s_]], writes=[b_qc[s_]])
                                op("pool", lambda e: e.tensor_copy(qk4[:, 0:2, 64:67], nWs[:, t, 2 * hp:2 * hp + 2, :]),
                                   reads=[B["nWs"]], writes=[b_qc[s_]])
                                op("pool", lambda e: e.tensor_copy(qk4[:, 2:4, 67:70], Ws[:, t, 2 * hp:2 * hp + 2, :]),
                                   reads=[B["Ws"]], writes=[b_qc[s_]])
                                op("act", lambda e: e.copy(vdst, banks[pp][:, 256:384].rearrange("p (h n) -> p h n", h=2)),
                                   reads=[bB[pp]], writes=[b_Vt[t]])
                            gc = 2
                        if part == 2:
                            for a in range(4):
                                op("pe", lambda e, a=a: e.transpose(bankbf[tb][0:Kd, a * 128:(a + 1) * 128], qk4[:, a, :], ident[:]),
                                   reads=[b_qc[s_], B["ident"]], writes=[bB[tb]], inc=(a == 3))
                            op("dve", lambda e: e.tensor_scalar(
                                QKT[0:Kd, 0:2, ts], bankbf[tb][0:Kd, 0:256].rearrange("p (a n) -> p a n", a=2),
                                gcols[0:Kd, gc:gc + 1], None, ALU.mult),
                               reads=[bB[tb], B["gcols"]], writes=[b_QKT[t]])
                            op("dve", lambda e: e.tensor_scalar(
                                QKT[0:Kd, 2:4, ts], bankbf[tb][0:Kd, 256:512].rearrange("p (a n) -> p a n", a=2),
                                gcols[0:Kd, gc + 1:gc + 2], None, ALU.mult),
                               reads=[bB[tb], B["gcols"]], writes=[b_QKT[t]])

                    def proj_step(k):
                        if k < NT:
                            tile(k, 0)
                        if 1 <= k < NT + 1:
                            tile(k - 1, 1)
                        if 2 <= k < NT + 2:
                            tile(k - 2, 2)

                    gc0 = 0 if mla else 384

                    def gate(g):
                        gs = slice(g * 512, (g + 1) * 512)
                        gbk = 4 + (g % 2)
                        for c in range(8):
                            op("pe", lambda e, c=c: e.matmul(
                                banks[gbk][:, :], wpair[:, c, gc0:gc0 + 128], hT[:, c, gs], start=(c == 0), stop=(c == 7)),
                               reads=[b_wp] + b_hT[4 * g:4 * g + 4], writes=[bB[gbk]], inc=(c == 7))
                        op("act", lambda e: e.activation(tg[:], banks[gbk][:, :], AF.Tanh, scale=0.5),
                           reads=[bB[gbk]], writes=[b_tg])
                        op("dve", lambda e: e.scalar_tensor_tensor(
                            gate2[:, gs], tg[:], 1.0, banks[gbk][:, :], ALU.add, ALU.mult),
                           reads=[b_tg, bB[gbk]], writes=[b_g2[g]])

                    maskb = mmaskb if mla else fmaskb

                    def issue_s(g, j):
                        N = 512 if j < 4 * g else 512 - (j - 4 * g) * 128
                        qc0 = g * 512 + 512 - N
                        sb0 = 2 * (j % 2)
                        for hh in range(2):
                            kT = QKT[:, 2 + hh, j * 128:(j + 1) * 128]
                            rd_ = [b_QKT[j], B["qkz"]] + b_QKT[qc0 // 128:4 * g + 4]
                            if j >= 4 * g:
                                op("pe", lambda e, hh=hh: e.matmul(banks[sb0 + hh][:, 0:128], ident[:], maskb[:],
                                                                   start=True, stop=False),
                                   reads=[B["ident"], B["maskb"]], writes=[bB[sb0 + hh]], inc=False)
                                op("pe", lambda e, hh=hh, kT=kT: e.matmul(banks[sb0 + hh][:, 0:128], kT, QKT[:, hh, qc0:qc0 + 128],
                                                                          start=False, stop=True),
                                   reads=rd_, writes=[bB[sb0 + hh]], inc=(hh == 1 and N == 128))
                                if N > 128:
                                    op("pe", lambda e, hh=hh, kT=kT: e.matmul(banks[sb0 + hh][:, 128:N], kT,
                                                                              QKT[:, hh, qc0 + 128:qc0 + N], start=True, stop=True),
                                       reads=rd_, writes=[bB[sb0 + hh]], inc=(hh == 1))
                            else:
                                op("pe", lambda e, hh=hh, kT=kT: e.matmul(banks[sb0 + hh][:, 0:N], kT,
                                                                          QKT[:, hh, qc0:qc0 + N], start=True, stop=True),
                                   reads=rd_, writes=[bB[sb0 + hh]], inc=(hh == 1))
                        s2 = psall[:, sb0 * 512:(sb0 + 2) * 512].rearrange("p (h n) -> p h n", h=2)
                        ps_ = j % 2
                        op("act", lambda e: e.activation(pt[ps_][:, :, 0:N], s2[:, :, 0:N], AF.Exp),
                           reads=[bB[sb0], bB[sb0 + 1]], writes=[b_pt[ps_]])

                    def issue_pv(g, j):
                        N = 512 if j < 4 * g else 512 - (j - 4 * g) * 128
                        ps_ = j % 2
                        for hh in range(2):
                            op("pe", lambda e, hh=hh: e.matmul(banks[4 + hh][:, 512 - N:512], Vt[:, j, hh * 64:hh * 64 + 128],
                                                               pt[ps_][:, hh, 0:N], start=(j == 0), stop=(j == 4 * g + 3)),
                               reads=[b_Vt[j], b_pt[ps_], B["vtwos"]], writes=[bB[4 + hh]], inc=(hh == 1))

                    def group_end_a(g):
                        lo, hi = slice(0, 64), slice(64, 128)
                        op("dve", lambda e: e.tensor_copy(tmpo[lo, :], banks[4][lo, :]), reads=[bB[4]], writes=[b_tmpo])
                        op("dve", lambda e: e.tensor_copy(rd[lo, :], banks[4][hi, :]), reads=[bB[4]], writes=[b_rd])
                        op("dve", lambda e: e.tensor_copy(tmpo[hi, :], banks[5][hi, :]), reads=[bB[5]], writes=[b_tmpo])
                        op("dve", lambda e: e.tensor_copy(rd[hi, :], banks[5][lo, :]), reads=[bB[5]], writes=[b_rd])

                    def group_end_b(g):
                        gs = slice(g * 512, (g + 1) * 512)
                        op("act", lambda e: e.activation(rd, rd, AF.Ln), reads=[b_rd], writes=[b_rd])
                        op("act", lambda e: e.activation(rd, rd, AF.Exp, scale=-1.0), reads=[b_rd], writes=[b_rd])
                        op("dve", lambda e: e.tensor_tensor(tmpo, tmpo, rd, ALU.mult),
                           reads=[b_tmpo, b_rd], writes=[b_tmpo])
                        op("pool", lambda e: e.tensor_tensor(yT[:, ychunk, gs], tmpo, gate2[:, gs], ALU.mult),
                           reads=[b_tmpo, b_g2[g]], writes=[b_yT[ychunk][g]])

                    for k in range(NT + 2):
                        proj_step(k)
                    _TagList.tag[0] = "s%d.j%d.gate" % (b, job)
                    for g in range(4):
                        gate(g)
                    if job == 0:
                        stage_check(4)
                    _TagList.tag[0] = "s%d.j%d.attn" % (b, job)
                    for g in range(4):
                        nst = 4 * g + 4
                        for i in range(nst + 1):
                            if i < nst:
                                issue_s(g, i)
                            if i >= 1:
                                issue_pv(g, i - 1)
                            if i == 2 and g > 0:
                                group_end_b(g - 1)
                        group_end_a(g)
                    group_end_b(3)
                    P.barrier()
                    if job == 0:
                        stage_check(5)
                    if job == 4:
                        stage_check(6)

                stage_check(7)
                _TagList.tag[0] = "s%d.p4" % b
                b_wmg = Buf()
                dma("sp", gate_bc[:], ada_d[b, 2 * D:3 * D].partition_broadcast(128), reads=[B["ada_d"]],
                    writes=[B["gate_bc"]])
                op("dve", lambda e: e.tensor_scalar(gate_bc[:], gate_bc[:], 0.5, None, ALU.mult),
                   reads=[B["gate_bc"]], writes=[B["gate_bc"]])
                for c in range(8):
                    dma("sp", wmg[:, c, :], wbf3[:, c, C_MA:C_MA + 2048], reads=[B["wbf_d"]], writes=[b_wmg])
                b_mT = [Buf() for _ in range(8)]
                b_ta = [Buf(), Buf()]
                b_m12 = [Buf(), Buf()]
                b_xr = [Buf(), Buf()]
                b_res = [Buf(), Buf()]
                k4 = 0
                for g in range(4):
                    gs = slice(g * 512, (g + 1) * 512)
                    for mc in range(8):
                        bs = (k4 % 2) * 4
                        k4 += 1
                        ms = slice(mc * 128, (mc + 1) * 128)
                        for c in range(4):
                            op("pe", lambda e, c=c, bs=bs, ms=ms, gs=gs: e.matmul(banks[bs][:, :], wba[:, c, ms], yT[:, c, gs],
                                                                                  start=(c == 0), stop=(c == 3)),
                               reads=[B["wba"], b_yT[c][g]], writes=[bB[bs]], inc=(c == 3))
                        for c in range(4):
                            op("pe", lambda e, c=c, bs=bs, ms=ms, gs=gs: e.matmul(banks[bs + 1][:, :], wbb[:, c, ms], yT[:, 4 + c, gs],
                                                                                  start=(c == 0), stop=(c == 3)),
                               reads=[B["wbb"], b_yT[4 + c][g]], writes=[bB[bs + 1]], inc=(c == 3))
                        for c in range(8):
                            op("pe", lambda e, c=c, bs=bs, ms=ms, gs=gs: e.matmul(banks[bs + 2][:, :], wmg[:, c, ms], hT[:, c, gs],
                                                                                  start=(c == 0), stop=(c == 7)),
                               reads=[b_wmg] + b_hT[4 * g:4 * g + 4], writes=[bB[bs + 2]], inc=(c == 7))
                        for c in range(8):
                            op("pe", lambda e, c=c, bs=bs, mc=mc, gs=gs: e.matmul(
                                banks[bs + 3][:, :], wmg[:, c, 1024 + mc * 128:1024 + (mc + 1) * 128], hT[:, c, gs],
                                start=(c == 0), stop=(c == 7)),
                               reads=[b_wmg] + b_hT[4 * g:4 * g + 4], writes=[bB[bs + 3]], inc=(c == 7))
                        op("act", lambda e, bs=bs: e.activation(ta[0], banks[bs + 2][:, :], AF.Tanh, scale=0.5),
                           reads=[bB[bs + 2]], writes=[b_ta[0]])
                        op("act", lambda e, bs=bs: e.activation(ta[1], banks[bs + 3][:, :], AF.Tanh, scale=0.5),
                           reads=[bB[bs + 3]], writes=[b_ta[1]])
                        op("dve", lambda e, bs=bs: e.scalar_tensor_tensor(m12[0], ta[0], 1.0, banks[bs][:, :], ALU.add, ALU.mult),
                           reads=[b_ta[0], bB[bs]], writes=[b_m12[0]])
                        op("dve", lambda e, bs=bs: e.scalar_tensor_tensor(m12[1], ta[1], 1.0, banks[bs + 1][:, :], ALU.add, ALU.mult),
                           reads=[b_ta[1], bB[bs + 1]], writes=[b_m12[1]])
                        op("pool", lambda e, mc=mc: e.tensor_tensor(mT[:, mc, :], m12[0], m12[1], ALU.add),
                           reads=[b_m12[0], b_m12[1]], writes=[b_mT[mc]])
                    for tt in range(4):
                        t = 4 * g + tt
                        s_ = t % 2
                        ts = slice(t * 128, (t + 1) * 128)
                        bs = (k4 % 2) * 4
                        k4 += 1
                        dma("sp", xr[s_], x_d[b, ts, :], writes=[b_xr[s_]])
                        for hf in range(2):
                            for c in range(8):
                                op("pe", lambda e, c=c, bs=bs, hf=hf, tt=tt: e.matmul(
                                    banks[bs + hf][:, :], mT[:, c, tt * 128:(tt + 1) * 128], wout[:, c, hf * 512:(hf + 1) * 512],
                                    start=(c == 0), stop=(c == 7)),
                                   reads=[b_mT[c], B["wout"]], writes=[bB[bs + hf]], inc=(c == 7))
                        for hf in range(2):
                            hs = slice(hf * 512, (hf + 1) * 512)
                            op("dve", lambda e, bs=bs, hf=hf, hs=hs, s_=s_: e.tensor_tensor(
                                res[s_][:, hs], banks[bs + hf][:, :], gate_bc[:, hs], ALU.mult),
                               reads=[bB[bs + hf], B["gate_bc"]], writes=[b_res[s_]])
                        op("pool", lambda e, s_=s_: e.tensor_tensor(res[s_], res[s_], xr[s_], ALU.add),
                           reads=[b_res[s_], b_xr[s_]], writes=[b_res[s_]])
                        dma("sp", out_d[b, ts, :], res[s_], reads=[b_res[s_]])
                P.barrier()

        except _Stop:
            pass
        P.finish()
        _DBG["sbuf_remaining"] = nc.sbuf_bytes_remaining
        _DBG["ops"] = {n: len(e.ops) for n, e in P.E.items()}
        _DBG["tags"] = {n: list(e.ops.tags) for n, e in P.E.items()}
        P.emit(st)
    return nc


_NC_CACHE = {}


def _consts():
    ident = np.eye(128, dtype=np.float32)
    tri = np.triu(np.ones((128, 128), np.float32))
    kk = np.arange(128)[:, None]
    qq = np.arange(128)[None, :]
    fmask = np.where(kk <= qq, 0.0, NEG).astype(np.float32)
    mmask = np.where((kk // 64) <= (qq // 64), 0.0, NEG).astype(np.float32)
    invf = (np.float32(10000.0) ** (-(np.arange(0, 32, 2, dtype=np.float32)) / np.float32(32))).astype(np.float32)
    return ident, tri, fmask, mmask, invf.reshape(1, 16)


def kernel(x, c, positions, w_ada, b_ada, norm_w, w_in, b_f, q_lora_norm_w, kv_lora_norm_w, w_uq, w_ukv,
           qn_nope_a, qn_rope_a, kn_nope_a, kn_rope_a, qn_b, kn_b, w_branch_a, w_branch_b, w_out):
    f = lambda a: np.ascontiguousarray(np.asarray(a, dtype=np.float32))
    x = f(x)
    c = f(c)
    positions = np.ascontiguousarray(np.asarray(positions, dtype=np.int32))
    ident, tri, fmask, mmask, invf = _consts()
    shared = {
        "w_ada": f(w_ada)[0], "b_ada": f(b_ada)[0].reshape(1, -1),
        "normw": np.ascontiguousarray(f(norm_w)[0].reshape(8, 128).T),
        "w_in": f(w_in)[0], "b_f": f(b_f)[0].reshape(1, 8),
        "qlw": np.ascontiguousarray(f(q_lora_norm_w)[0].reshape(3, 128).T),
        "kvlw": np.ascontiguousarray(f(kv_lora_norm_w)[0].reshape(2, 128).T),
        "w_uq": f(w_uq)[0], "w_ukv": f(w_ukv)[0],
        "g_qna": f(qn_nope_a)[0].reshape(1, -1), "g_qra": f(qn_rope_a)[0].reshape(1, -1),
        "g_kna": f(kn_nope_a)[0].reshape(1, -1), "g_kra": f(kn_rope_a)[0].reshape(1, -1),
        "g_qb": f(qn_b)[0].reshape(1, -1), "g_kb": f(kn_b)[0].reshape(1, -1),
        "w_ba": f(w_branch_a)[0], "w_bb": f(w_branch_b)[0], "w_out": f(w_out)[0],
        "ident": ident, "tri": tri, "fmask": fmask, "mmask": mmask, "invf": invf,
    }
    gcols = np.ones((128, 4), np.float32)
    gcols[0:64, 0] = f(qn_nope_a)[0]
    gcols[0:64, 1] = f(kn_nope_a)[0]
    gcols[0:64, 2] = f(qn_b)[0]
    gcols[0:64, 3] = f(kn_b)[0]
    shared["gcols"] = gcols
    in_maps = []
    for i in range(NCORES):
        bs = slice(i * BPC, (i + 1) * BPC)
        m = dict(shared)
        m["x"] = x[bs]
        m["cT"] = np.ascontiguousarray(c[bs].reshape(BPC, 8, 128).transpose(2, 1, 0))
        m["pos"] = np.ascontiguousarray(positions[bs].reshape(BPC, NT, 128).transpose(2, 0, 1))
        in_maps.append(m)
    if "nc" not in _NC_CACHE:
        _NC_CACHE["nc"] = build_nc()
    res = run_bass_kernel_spmd(_NC_CACHE["nc"], in_maps, core_ids=list(range(NCORES)))
    return np.concatenate([r["out"] for r in res.results], axis=0).astype(np.float32)
```

```python
import math
from contextlib import ExitStack

import numpy as np
import concourse.bass as bass
import concourse.mybir as mybir
from concourse.bass_utils import run_bass_kernel_spmd

F32 = mybir.dt.float32
BF16 = mybir.dt.bfloat16
I32 = mybir.dt.int32
AF = mybir.ActivationFunctionType
ALU = mybir.AluOpType
AX = mybir.AxisListType

NCORES = 8
BPC = 4
S = 2048
D = 1024
NT = 16
D_IN = 5288
EPS = 1e-6
C_CQ, C_CKV, C_KR, C_GA, C_QB, C_KB, C_VB, C_FB, C_GB, C_MA, C_MB = (
    0, 384, 640, 672, 1184, 1696, 2208, 2720, 2728, 3240, 4264)
TWO_PI_HI = 6.28125
TWO_PI_LO = 2.0 * math.pi - 6.28125
NEG = -30000.0


class Buf:
    __slots__ = ("w", "r", "excl")

    def __init__(self, excl=False):
        self.w = None
        self.r = {}
        self.excl = excl


class _TagList(list):
    tag = [""]

    def append(self, x):
        list.append(self, x)
        self.tags.append(_TagList.tag[0])


class _Eng:
    def __init__(self, name, semkey):
        self.name = name
        self.semkey = semkey
        self.count = 0
        self.waited = {}
        self.ops = _TagList()
        self.ops.tags = []
        self.dma_n = 0
        self.dma_vals = {}


class _Rec:
    def __init__(self):
        self.calls = []

    def __getattr__(self, name):
        def f(*a, **k):
            self.calls.append((name, a, k))
        return f


class Prog:
    ENGS = ("pe", "act", "dve", "pool", "sp")
    NDMA = {"sp": 8, "act": 2, "pool": 2}

    def __init__(self, nc):
        self.nc = nc
        self.E = {n: _Eng(n, ("eng", n)) for n in self.ENGS}
        self.semkeys = [e.semkey for e in self.E.values()]
        for q, k in self.NDMA.items():
            for i in range(k):
                self.semkeys.append(("dma", q, i))
                self.E[q].dma_vals[i] = 0

    def _wait(self, E, k, v):
        if E.waited.get(k, 0) < v:
            E.ops.append(("wait", k, v))
            E.waited[k] = v

    def _need(self, E, reads, writes):
        need = {}
        for b in reads:
            if b.w is not None and need.get(b.w[0], 0) < b.w[1]:
                need[b.w[0]] = b.w[1]
        for b in writes:
            if b.w is not None and need.get(b.w[0], 0) < b.w[1]:
                need[b.w[0]] = b.w[1]
            for k, v in b.r.items():
                if need.get(k, 0) < v:
                    need[k] = v
        for k, v in need.items():
            if k == E.semkey:
                if E.name == "pe" or v > E.count:
                    continue
            self._wait(E, k, v)

    def _mark(self, tok, reads, writes):
        for b in writes:
            b.w = tok
            b.r = {}
        for b in reads:
            if b.r.get(tok[0], 0) < tok[1]:
                b.r[tok[0]] = tok[1]

    def op(self, eng, fn, reads=(), writes=(), inc=True):
        E = self.E[eng]
        if any(b.excl for b in reads):
            writes = list(writes) + [b for b in reads if b.excl]
            reads = [b for b in reads if not b.excl]
        self._need(E, reads, writes)
        tok = (E.semkey, E.count + 1)
        rec = _Rec()
        fn(rec)
        E.ops.append(("inst", rec.calls[0], inc))
        if inc:
            E.count += 1
        self._mark(tok, reads, writes)

    def dma(self, q, out, in_, reads=(), writes=(), **kw):
        E = self.E[q]
        slot = E.dma_n % self.NDMA[q]
        E.dma_n += 1
        k = ("dma", q, slot)
        prev = E.dma_vals[slot]
        if prev > 0:
            self._wait(E, k, prev)
        self._need(E, reads, writes)
        E.dma_vals[slot] = prev + 16
        E.ops.append(("dma", out, in_, k, kw))
        self._mark((k, prev + 16), reads, writes)

    def barrier(self):
        toks = []
        for n in ("pe", "act", "dve", "pool"):
            e = self.E[n]
            if e.count > 0:
                toks.append((e.semkey, e.count))
        for q in self.NDMA:
            for slot, v in self.E[q].dma_vals.items():
                if v > 0:
                    toks.append((("dma", q, slot), v))
        for n in self.ENGS:
            E = self.E[n]
            for k, v in toks:
                if k != E.semkey:
                    self._wait(E, k, v)

    def finish(self):
        self.barrier()

    def emit(self, stack):
        nc = self.nc
        sems = {}
        for k in self.semkeys:
            sems[k] = stack.enter_context(nc.semaphore("s_" + "_".join(str(x) for x in k)))
        block = stack.enter_context(nc.Block())

        def run(E):
            def body(e):
                own = sems[E.semkey]
                for o in E.ops:
                    if o[0] == "wait":
                        e.wait_ge(sems[o[1]], o[2])
                    elif o[0] == "inst":
                        name, a, k = o[1]
                        ins = getattr(e, name)(*a, **k)
                        if o[2]:
                            ins.then_inc(own, 1)
                    else:
                        e.dma_start(out=o[1], in_=o[2], **o[4]).then_inc(sems[o[3]], 16)
            return body

        block.tensor(run(self.E["pe"]))
        block.scalar(run(self.E["act"]))
        block.vector(run(self.E["dve"]))
        block.gpsimd(run(self.E["pool"]))
        block.sync(run(self.E["sp"]))


_DBG = {}


class _Stop(Exception):
    pass


def build_nc(nseq=BPC, stage=99):
    def stage_check(n):
        if stage == n:
            raise _Stop()

    nc = bass.Bass("TRN2", target_bir_lowering=False)
    din = lambda n, s, dt=F32: nc.dram_tensor(n, list(s), dt, kind="ExternalInput").ap()
    x_d = din("x", [BPC, S, D])
    cT_d = din("cT", [128, 8, BPC])
    pos_d = din("pos", [128, BPC, NT], I32)
    wada_d = din("w_ada", [D, 3 * D])
    bada_d = din("b_ada", [1, 3 * D])
    normw_d = din("normw", [128, 8])
    win_d = din("w_in", [D, D_IN])
    bf_d = din("b_f", [1, 8])
    qlw_d = din("qlw", [128, 3])
    kvlw_d = din("kvlw", [128, 2])
    wuq_d = din("w_uq", [384, 768])
    wukv_d = din("w_ukv", [256, 1024])
    gqna_d = din("g_qna", [1, 64])
    gqra_d = din("g_qra", [1, 32])
    gkna_d = din("g_kna", [1, 64])
    gkra_d = din("g_kra", [1, 32])
    gqb_d = din("g_qb", [1, 64])
    gkb_d = din("g_kb", [1, 64])
    wba_d = din("w_ba", [512, D])
    wbb_d = din("w_bb", [512, D])
    wout_d = din("w_out", [D, D])
    ident_d = din("ident", [128, 128])
    tri_d = din("tri", [128, 128])
    fmask_d = din("fmask", [128, 128])
    mmask_d = din("mmask", [128, 128])
    invf_d = din("invf", [1, 16])
    gcols_d = din("gcols", [128, 4])
    out_d = nc.dram_tensor("out", [BPC, S, D], F32, kind="ExternalOutput").ap()
    wbf_d = nc.dram_tensor("wbf_scr", [128, 8 * D_IN], BF16).ap()
    wbf3 = wbf_d.rearrange("p (c n) -> p c n", c=8)
    ada_d = nc.dram_tensor("ada_scr", [BPC, 3 * D], F32).ap()

    P = Prog(nc)
    op, dma = P.op, P.dma
    with ExitStack() as st:
        try:
            _n = [0]

            def sb(shape, dt):
                _n[0] += 1
                return st.enter_context(nc.sbuf_tensor("sb%d" % _n[0], list(shape), dt))

            ident = sb([128, 128], BF16)
            identf = sb([128, 128], F32)
            tri = sb([128, 128], F32)
            ones = sb([128, 128], F32)
            negones = sb([128, 128], F32)
            twos = sb([128, 128], BF16)
            fmask = sb([128, 128], F32)
            mmask = sb([128, 128], F32)
            fmaskb = sb([128, 128], BF16)
            mmaskb = sb([128, 128], BF16)
            g_qna = sb([128, 64], F32)
            g_qra = sb([128, 32], F32)
            g_kna = sb([128, 64], F32)
            g_kra = sb([128, 32], F32)
            g_qb = sb([128, 64], F32)
            g_kb = sb([128, 64], F32)
            bf_bc = sb([128, 8], F32)
            invf = sb([128, 16], F32)
            cT = sb([128, 8, BPC], F32)
            normw = sb([128, 8], F32)
            qlw = sb([128, 3], F32)
            kvlw = sb([128, 2], F32)
            pos_sb = sb([128, BPC, NT], I32)
            wuq = sb([128, 3, 1024], BF16)
            wukv = sb([128, 2, 1024], BF16)
            wba = sb([128, 4, D], BF16)
            wbb = sb([128, 4, D], BF16)
            wout = sb([128, 8, D], BF16)
            hT = sb([128, 8, S], BF16)
            yT = sb([128, 8, S], BF16)
            ovl1 = sb([128, D], F32)
            gate_bc = ovl1
            sc_col = sb([128, 8], F32)
            sh_col = sb([128, 8], F32)
            A_col = sb([128, 8], F32)
            posf = sb([128, NT], F32)
            GCS = sb([128, NT, 64], F32)
            gcols = sb([128, 4], F32)
            g_qra_s = sb([128, 64], F32)
            sint = sb([128, NT, 16], F32)
            cost = sb([128, NT, 16], F32)
            ssq = sb([128, NT], F32)
            sskv = sb([128, NT], F32)
            epsq = sb([128, NT], F32)
            epskv = sb([128, NT], F32)
            rstdkv = sb([128, NT], F32)
            kfraw = sb([128, NT, 40], F32)
            krss = sb([128, NT], F32)
            krr = sb([128, NT, 32], BF16)
            spt = sb([128, NT, 8], F32)
            Wf = sb([128, NT * 8], F32)
            Wr = sb([128, NT * 8], F32)
            Ws = sb([128, NT, 8, 3], BF16)
            nWs = sb([128, NT, 8, 3], BF16)
            Ctab = sb([128, 48], F32)
            ssx = [sb([128, 1], F32) for _ in range(2)]
            rsx = [sb([128, 1], F32) for _ in range(2)]
            pt = [sb([128, 2, 512], BF16) for _ in range(2)]
            X = sb([128, 32768], BF16)

            def xv(off, n, dt):
                if dt == BF16:
                    return X[:, off // 2: off // 2 + n]
                return X[:, off // 2: off // 2 + 2 * n].bitcast(dt)

            psall = st.enter_context(nc.psum_tensor("psall", [128, 4096], F32))
            banks = [psall[:, i * 512:(i + 1) * 512] for i in range(8)]
            bB = [Buf(excl=True) for _ in range(8)]
            bankbf = [b.bitcast(BF16) for b in banks]

            B = {k: Buf() for k in (
                "ident", "identf", "tri", "ones", "negones", "twos", "fmask", "mmask", "maskb", "vtwos", "qkz", "gains", "bf", "invf", "cT",
                "normw", "qlw", "kvlw", "pos", "bada4", "ada4", "ada_d", "wuq", "wukv", "wba", "wbb", "wout", "wbf_d",
                "gate_bc", "cols", "A_col", "posf", "ang", "angk", "angi", "sint", "cost", "ssq", "sskv", "epsq", "epskv",
                "rstdkv", "GCS", "gcols", "kfraw", "krt", "krs", "krss", "krm", "krr", "spt", "Wf", "Wr", "Ws", "nWs", "Ctab")}
            b_hT = [Buf() for _ in range(NT)]
            b_yT = [[Buf() for _ in range(4)] for _ in range(8)]
            b_pt = [Buf() for _ in range(2)]
            b_ssx = [Buf(), Buf()]
            b_rsx = [Buf(), Buf()]

            _TagList.tag[0] = "setup"
            dma("sp", identf[:], ident_d, writes=[B["identf"]])
            op("dve", lambda e: e.tensor_copy(ident[:], identf[:]), reads=[B["identf"]], writes=[B["ident"]])
            dma("sp", tri[:], tri_d, writes=[B["tri"]])
            dma("sp", fmask[:], fmask_d, writes=[B["fmask"]])
            dma("sp", mmask[:], mmask_d, writes=[B["mmask"]])
            op("dve", lambda e: e.tensor_copy(fmaskb[:], fmask[:]), reads=[B["fmask"]], writes=[B["maskb"]])
            op("dve", lambda e: e.tensor_copy(mmaskb[:], mmask[:]), reads=[B["mmask"]], writes=[B["maskb"]])
            op("pool", lambda e: e.memset(ones[:], 1.0), writes=[B["ones"]])
            op("pool", lambda e: e.memset(negones[:], -1.0), writes=[B["negones"]])
            op("pool", lambda e: e.memset(twos[:], 2.0), writes=[B["twos"]])
            for t_, d_ in ((g_qna, gqna_d), (g_qra, gqra_d), (g_kna, gkna_d), (g_kra, gkra_d), (g_qb, gqb_d),
                           (g_kb, gkb_d), (bf_bc, bf_d), (invf, invf_d)):
                dma("sp", t_[:], d_[0].partition_broadcast(128), writes=[B["gains"]])
            op("dve", lambda e: e.tensor_scalar(g_qna[:], g_qna[:], 1.0 / math.sqrt(96.0), None, ALU.mult),
               reads=[B["gains"]], writes=[B["gains"]])
            op("dve", lambda e: e.tensor_scalar(g_qra[:], g_qra[:], 1.0 / math.sqrt(96.0), None, ALU.mult),
               reads=[B["gains"]], writes=[B["gains"]])
            op("dve", lambda e: e.tensor_scalar(g_qb[:], g_qb[:], 0.125, None, ALU.mult),
               reads=[B["gains"]], writes=[B["gains"]])
            dma("sp", gcols[:], gcols_d, writes=[B["gcols"]])
            op("dve", lambda e: e.tensor_scalar(gcols[0:64, 0:1], gcols[0:64, 0:1], 1.0 / math.sqrt(96.0), None, ALU.mult),
               reads=[B["gcols"]], writes=[B["gcols"]])
            op("dve", lambda e: e.tensor_scalar(gcols[0:64, 2:3], gcols[0:64, 2:3], 0.125, None, ALU.mult),
               reads=[B["gcols"]], writes=[B["gcols"]])
            op("dve", lambda e: e.tensor_copy(g_qra_s[:, 0:32], g_qra[:]), reads=[B["gains"]], writes=[B["gains"]])
            op("dve", lambda e: e.tensor_scalar(g_qra_s[:, 32:48], g_qra[:, 16:32], -1.0, None, ALU.mult),
               reads=[B["gains"]], writes=[B["gains"]])
            op("dve", lambda e: e.tensor_copy(g_qra_s[:, 48:64], g_qra[:, 0:16]), reads=[B["gains"]], writes=[B["gains"]])
            dma("sp", cT[:], cT_d, writes=[B["cT"]])
            dma("sp", normw[:], normw_d, writes=[B["normw"]])
            dma("sp", qlw[:], qlw_d, writes=[B["qlw"]])
            dma("sp", kvlw[:], kvlw_d, writes=[B["kvlw"]])
            dma("sp", pos_sb[:], pos_d, writes=[B["pos"]])
            pass

            bada4 = xv(32768, 3 * D, F32)[0:BPC, :]
            ada4 = xv(45056, 3 * D, F32)[0:BPC, :]
            dma("sp", bada4, bada_d[0].partition_broadcast(BPC), writes=[B["bada4"]])
            stg = [xv(0, 8 * 512, F32).rearrange("p (c n) -> p c n", c=8),
                   xv(16384, 8 * 512, F32).rearrange("p (c n) -> p c n", c=8)]
            b_stg = [Buf(), Buf()]
            wada3 = wada_d.rearrange("(c p) n -> p c n", p=128)
            for n in range(6):
                s_ = n % 2
                dma("sp", stg[s_], wada3[:, :, n * 512:(n + 1) * 512], writes=[b_stg[s_]])
                for kc in range(8):
                    op("pe", lambda e, s_=s_, kc=kc, n=n: e.matmul(banks[n % 2][0:BPC, :], cT[:, kc, :], stg[s_][:, kc, :],
                                                                  start=(kc == 0), stop=(kc == 7)),
                       reads=[B["cT"], b_stg[s_]], writes=[bB[n % 2]], inc=(kc == 7))
                op("dve", lambda e, n=n: e.tensor_tensor(ada4[:, n * 512:(n + 1) * 512], banks[n % 2][0:BPC, :],
                                                         bada4[:, n * 512:(n + 1) * 512], ALU.add),
                   reads=[bB[n % 2], B["bada4"]], writes=[B["ada4"]])
            dma("sp", ada_d, ada4, reads=[B["ada4"]], writes=[B["ada_d"]])
            P.barrier()

            sA = xv(0, 3 * 768, F32).rearrange("p (c n) -> p c n", c=3)
            sBv = xv(16384, 2 * 1024, F32).rearrange("p (c n) -> p c n", c=2)
            b_sA, b_sB = Buf(), Buf()
            dma("sp", sA, wuq_d.rearrange("(c p) n -> p c n", p=128), writes=[b_sA])
            dma("sp", sBv, wukv_d.rearrange("(c p) n -> p c n", p=128), writes=[b_sB])
            for c in range(3):
                src3 = sA[:, c, :].rearrange("p (h n) -> p h n", h=8)
                dst3 = wuq[:, c, :].rearrange("p (h n) -> p h n", h=8)
                for d0, d1, s0, s1 in ((0, 96, 0, 96), (96, 112, 80, 96), (112, 128, 64, 80)):
                    op("dve", lambda e, c=c, src3=src3, dst3=dst3, d0=d0, d1=d1, s0=s0, s1=s1: e.tensor_scalar(
                        dst3[:, :, d0:d1], src3[:, :, s0:s1], qlw[:, c:c + 1], None, ALU.mult),
                       reads=[b_sA, B["qlw"]], writes=[B["wuq"]])
            for c in range(2):
                op("dve", lambda e, c=c: e.tensor_scalar(wukv[:, c, :], sBv[:, c, :], kvlw[:, c:c + 1], None, ALU.mult),
                   reads=[b_sB, B["kvlw"]], writes=[B["wukv"]])
            P.barrier()
            sW = [xv(0, 4 * 1024, F32).rearrange("p (c n) -> p c n", c=4),
                  xv(16384, 4 * 1024, F32).rearrange("p (c n) -> p c n", c=4)]
            b_sW = [Buf(), Buf()]
            jobs = [(wba_d.rearrange("(c p) n -> p c n", p=128), wba, 0, "wba"),
                    (wbb_d.rearrange("(c p) n -> p c n", p=128), wbb, 0, "wbb"),
                    (wout_d.rearrange("(c p) n -> p c n", p=128)[:, 0:4, :], wout, 0, "wout"),
                    (wout_d.rearrange("(c p) n -> p c n", p=128)[:, 4:8, :], wout, 4, "wout")]
            for i, (src, dst, c0, key) in enumerate(jobs):
                s_ = i % 2
                dma("sp", sW[s_], src, writes=[b_sW[s_]])
                if s_ == 0:
                    op("dve", lambda e, s_=s_, dst=dst, c0=c0: e.tensor_copy(dst[:, c0:c0 + 4, :], sW[s_]),
                       reads=[b_sW[s_]], writes=[B[key]])
                else:
                    op("act", lambda e, s_=s_, dst=dst, c0=c0: e.copy(dst[:, c0:c0 + 4, :], sW[s_]),
                       reads=[b_sW[s_]], writes=[B[key]])
            P.barrier()
            HALF = D_IN // 2
            sI = [xv(0, HALF, F32), xv(16384, HALF, F32)]
            sO = [xv(32768, HALF, BF16), xv(40960, HALF, BF16)]
            b_sI = [Buf(), Buf()]
            b_sO = [Buf(), Buf()]
            i = 0
            for kc in range(8):
                for hf in range(2):
                    s_ = i % 2
                    dma("sp", sI[s_], win_d[kc * 128:(kc + 1) * 128, hf * HALF:(hf + 1) * HALF], writes=[b_sI[s_]])
                    eng = ("dve", "act")[i % 2]
                    if eng == "act":
                        op(eng, lambda e, s_=s_: e.copy(sO[s_], sI[s_]), reads=[b_sI[s_]], writes=[b_sO[s_]])
                    else:
                        op(eng, lambda e, s_=s_: e.tensor_copy(sO[s_], sI[s_]), reads=[b_sI[s_]], writes=[b_sO[s_]])
                    dma("sp", wbf3[:, kc, hf * HALF:(hf + 1) * HALF], sO[s_], reads=[b_sO[s_]], writes=[B["wbf_d"]])
                    i += 1
            P.barrier()

            stage_check(1)
            cqT = xv(0, 3 * S, BF16).rearrange("p (c n) -> p c n", c=3)
            ckvT = xv(12288, 2 * S, BF16).rearrange("p (c n) -> p c n", c=2)
            b_cqT = [Buf() for _ in range(NT)]
            xt = [xv(20480, D, F32), xv(24576, D, F32)]
            junk = xv(28672, D, BF16)
            xn = [xv(30720, D, BF16), xv(32768, D, BF16)]
            wg1 = xv(34816, 8 * 680, BF16).rearrange("p (c n) -> p c n", c=8)
            cqb = [xv(45696, 640, BF16), xv(46976, 640, BF16)]
            htmp = [xv(48256, D, F32), xv(52352, D, F32)]
            ang = xv(60544, NT * 16, F32).rearrange("p (t n) -> p t n", t=NT)
            angk = xv(61568, NT * 16, F32).rearrange("p (t n) -> p t n", t=NT)
            angi = xv(62592, NT * 16, I32).rearrange("p (t n) -> p t n", t=NT)
            krt = xv(56448, NT * 32, F32).rearrange("p (t n) -> p t n", t=NT)
            krs = xv(58496, NT * 32, F32).rearrange("p (t n) -> p t n", t=NT)
            krm = [xv(60544 + 1024 * k_, NT * 16, F32).rearrange("p (t n) -> p t n", t=NT) for k_ in range(4)]
            QKT = xv(20480, 4 * S, BF16).rearrange("p (a n) -> p a n", a=4)
            Vt = xv(36864, NT * 192, BF16).rearrange("p (t n) -> p t n", t=NT)
            gate2 = xv(43008, S, F32)
            wpair = xv(51200, 8 * 512, BF16).rearrange("p (c n) -> p c n", c=8)
            WB = 59392
            sq = [xv(WB, 512, F32), xv(WB + 2048, 512, F32)]
            tmpr = [xv(WB + 4096, 128, F32), xv(WB + 4608, 128, F32)]
            tmpr2 = [xv(WB + 5120, 128, F32), xv(WB + 5632, 128, F32)]
            wmg = xv(0, 8 * 2048, BF16).rearrange("p (c n) -> p c n", c=8)
            mT = xv(32768, 8 * 512, BF16).rearrange("p (c n) -> p c n", c=8)
            ta = [xv(40960, 512, F32), xv(43008, 512, F32)]
            m12 = [xv(45056, 512, F32), xv(47104, 512, F32)]
            xr = [xv(49152, D, F32), xv(53248, D, F32)]
            res = [xv(57344, D, F32), xv(61440, D, F32)]
            st6 = [sb([128, 8], F32) for _ in range(3)]
            rs6 = [sb([128, 8], F32) for _ in range(3)]
            sq.append(sb([128, 512], F32)[:])
            tmpr.append(sb([128, 128], F32)[:])
            tmpr2.append(sb([128, 128], F32)[:])
            qk = [sb([128, 384], BF16) for _ in range(3)]
            rd = ovl1[:, 0:512]
            tmpo = ovl1[:, 512:1024]
            tg = sb([128, 512], F32)
            b_rd, b_tmpo, b_tg = Buf(), Buf(), Buf()

            def bc2(ap2, n):
                return ap2.unsqueeze(2).broadcast_to([128, ap2.shape[1], n])

            def bch(ap2, h):
                return ap2.unsqueeze(1).broadcast_to([128, h, ap2.shape[1]])

            for b in range(nseq):
                _TagList.tag[0] = "s%d.p1" % b
                dma("sp", sh_col[:], ada_d[b, 0:D].rearrange("(c p) -> p c", p=128), reads=[B["ada_d"]],
                    writes=[B["cols"]], allow_slow_non_contiguous=True)
                dma("sp", sc_col[:], ada_d[b, D:2 * D].rearrange("(c p) -> p c", p=128), reads=[B["ada_d"]],
                    writes=[B["cols"]], allow_slow_non_contiguous=True)
                op("dve", lambda e: e.scalar_tensor_tensor(A_col[:], sc_col[:], 1.0, normw[:], ALU.add, ALU.mult),
                   reads=[B["cols"], B["normw"]], writes=[B["A_col"]])
                stage_check(11)
                op("dve", lambda e, b=b: e.tensor_copy(posf[:], pos_sb[:, b, :]), reads=[B["pos"]], writes=[B["posf"]])
                op("dve", lambda e: e.tensor_tensor(ang, bc2(posf[:], 16), bch(invf[:], NT), ALU.mult),
                   reads=[B["posf"], B["gains"]], writes=[B["ang"]])

                def reduce_angle(dst_key_unused=None):
                    op("dve", lambda e: e.tensor_scalar(angk, ang, 1.0 / (2.0 * math.pi), None, ALU.mult),
                       reads=[B["ang"]], writes=[B["angk"]])
                    op("dve", lambda e: e.tensor_copy(angi, angk), reads=[B["angk"]], writes=[B["angi"]])
                    op("dve", lambda e: e.tensor_copy(angk, angi), reads=[B["angi"]], writes=[B["angk"]])
                    op("dve", lambda e: e.scalar_tensor_tensor(ang, angk, -TWO_PI_HI, ang, ALU.mult, ALU.add),
                       reads=[B["angk"], B["ang"]], writes=[B["ang"]])
                    op("dve", lambda e: e.scalar_tensor_tensor(ang, angk, -TWO_PI_LO, ang, ALU.mult, ALU.add),
                       reads=[B["angk"], B["ang"]], writes=[B["ang"]])
                    op("dve", lambda e: e.tensor_scalar(ang, ang, math.pi, -math.pi, ALU.min, ALU.max),
                       reads=[B["ang"]], writes=[B["ang"]])

                reduce_angle()
                op("act", lambda e: e.activation(sint[:], ang, AF.Sin), reads=[B["ang"]], writes=[B["sint"]])
                op("dve", lambda e: e.tensor_scalar(ang, ang, 0.5 * math.pi, None, ALU.add),
                   reads=[B["ang"], B["sint"]], writes=[B["ang"]])
                reduce_angle()
                op("act", lambda e: e.activation(cost[:], ang, AF.Sin), reads=[B["ang"]], writes=[B["cost"]])
                G4 = GCS[:].rearrange("p t (k n) -> p t k n", k=4)
                for k_, tab, key in ((0, cost, "cost"), (1, cost, "cost"), (2, sint, "sint"), (3, sint, "sint")):
                    op("dve", lambda e, k_=k_, tab=tab: e.tensor_tensor(
                        G4[:, :, k_, :], tab[:], bch(g_qra_s[:, k_ * 16:(k_ + 1) * 16], NT), ALU.mult),
                       reads=[B[key], B["gains"]], writes=[B["GCS"]])

                stage_check(12)
                b_wg1 = Buf()
                dma("sp", wg1[:, :, 0:672], wbf3[:, :, 0:672], reads=[B["wbf_d"]], writes=[b_wg1])
                dma("sp", wg1[:, :, 672:680], wbf3[:, :, C_FB:C_FB + 8], reads=[B["wbf_d"]], writes=[b_wg1])

                b_xt = [Buf(), Buf()]
                b_junk = Buf()
                b_xn = [Buf(), Buf()]
                b_cqb = [Buf(), Buf()]
                b_htmp = [Buf(), Buf()]
                def p1(t, part):
                  s_ = t % 2
                  ts = slice(t * 128, (t + 1) * 128)
                  gA, gB = 2 + s_, 4 + s_
                  tb = 6 + s_
                  if part == 1:
                    dma("sp", xt[s_], x_d[b, ts, :], writes=[b_xt[s_]])
                    op("act", lambda e, s_=s_: e.activation(junk, xt[s_], AF.Square, accum_out=ssx[s_][:]),
                       reads=[b_xt[s_]], writes=[b_junk, b_ssx[s_]])
                    op("act", lambda e, s_=s_: e.activation(rsx[s_][:], ssx[s_][:], AF.Ln, bias=EPS, scale=1.0 / D),
                       reads=[b_ssx[s_]], writes=[b_rsx[s_]])
                    op("act", lambda e, s_=s_: e.activation(rsx[s_][:], rsx[s_][:], AF.Exp, scale=-0.5),
                       reads=[b_rsx[s_]], writes=[b_rsx[s_]])
                    op("dve", lambda e, s_=s_: e.tensor_scalar(xn[s_], xt[s_], rsx[s_][:], None, ALU.mult),
                       reads=[b_xt[s_], b_rsx[s_]], writes=[b_xn[s_]])
                  pb = s_
                  if part == 2:
                    for c in range(8):
                        op("pe", lambda e, c=c, s_=s_, pb=pb: e.transpose(bankbf[pb][:, c * 128:(c + 1) * 128],
                                                                          xn[s_][:, c * 128:(c + 1) * 128], ident[:]),
                           reads=[b_xn[s_], B["ident"]], writes=[bB[pb]], inc=(c == 7))
                    op("dve", lambda e, s_=s_, pb=pb: e.tensor_tensor(
                        htmp[s_].rearrange("p (c n) -> p c n", c=8), bankbf[pb].rearrange("p (c n) -> p c n", c=8),
                        bc2(A_col[:], 128), ALU.mult),
                       reads=[bB[pb], B["A_col"]], writes=[b_htmp[s_]])
                    op("pool", lambda e, s_=s_, ts=ts: e.tensor_tensor(
                        hT[:, :, ts], htmp[s_].rearrange("p (c n) -> p c n", c=8), bc2(sh_col[:], 128), ALU.add),
                       reads=[b_htmp[s_], B["cols"]], writes=[b_hT[t]])
                  if part == 3:
                    for c in range(8):
                        op("pe", lambda e, c=c, ts=ts, gA=gA: e.matmul(banks[gA][:, 0:384], hT[:, c, ts], wg1[:, c, 0:384],
                                                                       start=(c == 0), stop=(c == 7)),
                           reads=[b_hT[t], b_wg1], writes=[bB[gA]], inc=(c == 7))
                    for c in range(8):
                        op("pe", lambda e, c=c, ts=ts, gB=gB: e.matmul(banks[gB][:, 0:296], hT[:, c, ts], wg1[:, c, 384:680],
                                                                       start=(c == 0), stop=(c == 7)),
                           reads=[b_hT[t], b_wg1], writes=[bB[gB]], inc=(c == 7))

                    op("act", lambda e, gA=gA, t=t: e.activation(junk[:, 0:384], banks[gA][:, 0:384], AF.Square,
                                                                 accum_out=ssq[:, t:t + 1]),
                       reads=[bB[gA]], writes=[b_junk, B["ssq"]])
                    op("act", lambda e, gB=gB, t=t: e.activation(junk[:, 0:256], banks[gB][:, 0:256], AF.Square,
                                                                 accum_out=sskv[:, t:t + 1]),
                       reads=[bB[gB]], writes=[b_junk, B["sskv"]])
                    op("dve", lambda e, gA=gA, s_=s_: e.tensor_copy(cqb[s_][:, 0:384], banks[gA][:, 0:384]),
                       reads=[bB[gA]], writes=[b_cqb[s_]])
                    op("dve", lambda e, gB=gB, s_=s_: e.tensor_copy(cqb[s_][:, 384:640], banks[gB][:, 0:256]),
                       reads=[bB[gB]], writes=[b_cqb[s_]])
                    op("act", lambda e, gB=gB, t=t: e.copy(kfraw[:, t, :], banks[gB][:, 256:296]),
                       reads=[bB[gB]], writes=[B["kfraw"]])
                  if part == 4:
                    for c in range(5):
                        op("pe", lambda e, c=c, s_=s_, tb=tb: e.transpose(bankbf[tb][:, c * 128:(c + 1) * 128],
                                                                          cqb[s_][:, c * 128:(c + 1) * 128], ident[:]),
                           reads=[b_cqb[s_], B["ident"]], writes=[bB[tb]], inc=(c == 4))
                    op("dve", lambda e, tb=tb, ts=ts: e.tensor_copy(
                        cqT[:, :, ts], bankbf[tb][:, 0:384].rearrange("p (c n) -> p c n", c=3)),
                       reads=[bB[tb]], writes=[b_cqT[t]])
                    op("act", lambda e, tb=tb, ts=ts: e.copy(
                        ckvT[:, :, ts], bankbf[tb][:, 384:640].rearrange("p (c n) -> p c n", c=2)),
                       reads=[bB[tb]], writes=[b_cqT[t]])


                for t in range(NT + 3):
                    if t < NT:
                        p1(t, 1)
                    if 1 <= t < NT + 1:
                        p1(t - 1, 2)
                    if 2 <= t < NT + 2:
                        p1(t - 2, 3)
                    if t >= 3:
                        p1(t - 3, 4)

                stage_check(2)
                _TagList.tag[0] = "s%d.p1c" % b
                op("dve", lambda e: e.tensor_scalar(epsq[:], ssq[:], EPS / 384.0, EPS * EPS, ALU.mult, ALU.add),
                   reads=[B["ssq"]], writes=[B["epsq"]])
                op("dve", lambda e: e.tensor_scalar(epskv[:], sskv[:], EPS / 256.0, EPS * EPS, ALU.mult, ALU.add),
                   reads=[B["sskv"]], writes=[B["epskv"]])
                op("act", lambda e: e.activation(rstdkv[:], sskv[:], AF.Ln, bias=EPS, scale=1.0 / 256.0),
                   reads=[B["sskv"]], writes=[B["rstdkv"]])
                op("act", lambda e: e.activation(rstdkv[:], rstdkv[:], AF.Exp, scale=-0.5),
                   reads=[B["rstdkv"]], writes=[B["rstdkv"]])
                op("dve", lambda e: e.tensor_tensor(krs, kfraw[:, :, 0:32], kfraw[:, :, 0:32], ALU.mult),
                   reads=[B["kfraw"]], writes=[B["krs"]])
                op("dve", lambda e: e.tensor_reduce(krss[:], krs, AX.X, ALU.add), reads=[B["krs"]], writes=[B["krss"]])
                op("act", lambda e: e.activation(krss[:], krss[:], AF.Ln, bias=EPS, scale=1.0 / 32.0),
                   reads=[B["krss"]], writes=[B["krss"]])
                op("act", lambda e: e.activation(krss[:], krss[:], AF.Exp, scale=-0.5),
                   reads=[B["krss"]], writes=[B["krss"]])
                op("dve", lambda e: e.tensor_tensor(krt, kfraw[:, :, 0:32], bc2(krss[:], 32), ALU.mult),
                   reads=[B["kfraw"], B["krss"]], writes=[B["krt"]])
                op("dve", lambda e: e.tensor_tensor(krt, krt, bch(g_kra[:], NT), ALU.mult),
                   reads=[B["krt"], B["gains"]], writes=[B["krt"]])
                x1, x2 = krt[:, :, 0:16], krt[:, :, 16:32]
                op("dve", lambda e: e.tensor_tensor(krm[0], x1, cost[:], ALU.mult), reads=[B["krt"], B["cost"]], writes=[B["krm"]])
                op("dve", lambda e: e.tensor_tensor(krm[1], x2, sint[:], ALU.mult), reads=[B["krt"], B["sint"]], writes=[B["krm"]])
                op("dve", lambda e: e.tensor_tensor(krm[2], x2, cost[:], ALU.mult), reads=[B["krt"], B["cost"]], writes=[B["krm"]])
                op("dve", lambda e: e.tensor_tensor(krm[3], x1, sint[:], ALU.mult), reads=[B["krt"], B["sint"]], writes=[B["krm"]])
                op("dve", lambda e: e.tensor_tensor(krr[:, :, 0:16], krm[0], krm[1], ALU.subtract),
                   reads=[B["krm"]], writes=[B["krr"]])
                op("dve", lambda e: e.tensor_tensor(krr[:, :, 16:32], krm[2], krm[3], ALU.add),
                   reads=[B["krm"]], writes=[B["krr"]])
                op("dve", lambda e: e.tensor_tensor(spt[:], kfraw[:, :, 32:40], bch(bf_bc[:], NT), ALU.add),
                   reads=[B["kfraw"], B["gains"]], writes=[B["spt"]])
                op("act", lambda e: e.activation(spt[:], spt[:], AF.Exp, scale=-1.0), reads=[B["spt"]], writes=[B["spt"]])
                op("act", lambda e: e.activation(spt[:], spt[:], AF.Ln, bias=1.0), reads=[B["spt"]], writes=[B["spt"]])
                spt2 = spt[:].rearrange("p t h -> p (t h)")
                op("pe", lambda e: e.matmul(banks[0][:, 0:128], tri[:], spt2, start=True, stop=True),
                   reads=[B["tri"], B["spt"]], writes=[bB[0]])
                op("pe", lambda e: e.matmul(banks[1][:, 0:128], ones[:], spt2, start=True, stop=True),
                   reads=[B["ones"], B["spt"]], writes=[bB[1]])
                op("dve", lambda e: e.tensor_copy(Wf[:], banks[0][:, 0:128]), reads=[bB[0]], writes=[B["Wf"]])
                op("act", lambda e: e.copy(Wr[:], banks[1][:, 0:128]), reads=[bB[1]], writes=[B["Wr"]])
                scanA = (Wr[:].rearrange("p (t h) -> p t h", t=NT), "Wr")
                scanB = (spt[:], "spt")
                for d_ in (1, 2, 4, 8):
                    (A_, ka), (B_, kb) = scanA, scanB
                    op("dve", lambda e, A_=A_, B_=B_, d_=d_: e.tensor_copy(B_[:, 0:d_, :], A_[:, 0:d_, :]),
                       reads=[B[ka]], writes=[B[kb]])
                    op("dve", lambda e, A_=A_, B_=B_, d_=d_: e.tensor_tensor(B_[:, d_:NT, :], A_[:, d_:NT, :], A_[:, 0:NT - d_, :], ALU.add),
                       reads=[B[ka]], writes=[B[kb]])
                    scanA, scanB = scanB, scanA
                Wf3_ = Wf[:].rearrange("p (t h) -> p t h", t=NT)
                op("dve", lambda e: e.tensor_tensor(Wf3_[:, 1:NT, :], Wf3_[:, 1:NT, :], scanA[0][:, 0:NT - 1, :], ALU.add),
                   reads=[B["Wf"], B[scanA[1]]], writes=[B["Wf"]])
                Wf3 = Wf[:].rearrange("p (t h) -> p t h", t=NT)
                Wr3 = Wr[:].rearrange("p (t h) -> p t h", t=NT)
                op("dve", lambda e: e.tensor_copy(Ws[:, :, :, 0], Wf3), reads=[B["Wf"]], writes=[B["Ws"]])
                op("dve", lambda e: e.tensor_tensor(Wr3, Wf3, Ws[:, :, :, 0], ALU.subtract),
                   reads=[B["Wf"], B["Ws"]], writes=[B["Wr"]])
                op("dve", lambda e: e.tensor_copy(Ws[:, :, :, 1], Wr3), reads=[B["Wr"]], writes=[B["Ws"]])
                op("dve", lambda e: e.tensor_tensor(Wr3, Wr3, Ws[:, :, :, 1], ALU.subtract),
                   reads=[B["Wr"], B["Ws"]], writes=[B["Wr"]])
                op("dve", lambda e: e.tensor_copy(Ws[:, :, :, 2], Wr3), reads=[B["Wr"]], writes=[B["Ws"]])
                op("dve", lambda e: e.tensor_scalar(nWs[:], Ws[:], -1.0, None, ALU.mult), reads=[B["Ws"]], writes=[B["nWs"]])
                P.barrier()

                stage_check(3)
                op("pool", lambda e: e.memset(Vt[:, :, 64:128], 2.0), writes=[B["vtwos"]])
                op("pool", lambda e: e.memset(QKT[96:128, :, :], 0.0), writes=[B["qkz"]])
                for job in range(8):
                    _TagList.tag[0] = "s%d.j%d.proj" % (b, job)
                    mla = job < 4
                    hp = job % 4
                    Kd = 96 if mla else 70
                    ychunk = hp if mla else 4 + hp
                    b_wp = Buf()
                    if mla:
                        dma("sp", wpair[:, :, 0:128], wbf3[:, :, C_GA + hp * 128:C_GA + (hp + 1) * 128],
                            reads=[B["wbf_d"]], writes=[b_wp])
                    else:
                        for k_, c0 in enumerate((C_QB, C_KB, C_VB, C_GB)):
                            dma("sp", wpair[:, :, k_ * 128:(k_ + 1) * 128], wbf3[:, :, c0 + hp * 128:c0 + (hp + 1) * 128],
                                reads=[B["wbf_d"]], writes=[b_wp])
                    b_QKT = [Buf() for _ in range(NT)]
                    if job == 4:
                        op("pool", lambda e: e.memset(QKT[64:96, :, :], 0.0), writes=b_QKT + [B["qkz"]])
                    b_Vt = [Buf() for _ in range(NT)]
                    b_sq = [Buf(), Buf(), Buf()]
                    b_st = [Buf(), Buf(), Buf()]
                    b_rs = [Buf(), Buf(), Buf()]
                    b_tn = [Buf(), Buf(), Buf()]
                    b_tr = [Buf(), Buf(), Buf()]
                    b_rm = [Buf(), Buf(), Buf()]
                    b_qc = [Buf(), Buf(), Buf()]
                    b_kc = [Buf(), Buf(), Buf()]
                    b_g2 = [Buf() for _ in range(4)]
                    if not mla:
                        for s_ in range(3):
                            qk4 = qk[s_][:, 0:280].rearrange("p (a n) -> p a n", a=4)
                            op("pool", lambda e, qk4=qk4: e.memset(qk4[:, 0:2, 67:70], 1.0), writes=[b_qc[s_]])
                            op("pool", lambda e, qk4=qk4: e.memset(qk4[:, 2:4, 64:67], 1.0), writes=[b_qc[s_]])

                    def tile(t, part):
                        s_ = t % 3
                        ts = slice(t * 128, (t + 1) * 128)
                        pp = s_
                        tb = 6 + (t % 2)
                        vdst = Vt[:, t, :].rearrange("p (a n) -> p a n", a=3)[:, 0:3:2, :]
                        if mla:
                            pq = banks[pp][:, 0:256].rearrange("p (h n) -> p h n", h=2)
                            pkv = banks[pp][:, 256:512].rearrange("p (h n) -> p h n", h=2)
                            qk4 = qk[s_][:, 0:384].rearrange("p (a n) -> p a n", a=4)
                            rsq = rs6[s_][:, 0:4].rearrange("p (h k) -> p h k", k=2)
                            rsk = rs6[s_][:, 4:8].rearrange("p (h k) -> p h k", k=2)
                            tr3 = tmpr[s_].rearrange("p (h n) -> p h n", h=2)
                            tr23 = tmpr2[s_].rearrange("p (h n) -> p h n", h=2)
                            if part == 0:
                                for c in range(3):
                                    op("pe", lambda e, c=c: e.matmul(
                                        banks[pp][:, 0:256], cqT[:, c, ts], wuq[:, c, hp * 256:(hp + 1) * 256],
                                        start=(c == 0), stop=(c == 2)),
                                       reads=[b_cqT[t], B["wuq"]], writes=[bB[pp]], inc=False)
                                for c in range(2):
                                    op("pe", lambda e, c=c: e.matmul(
                                        banks[pp][:, 256:512], ckvT[:, c, ts], wukv[:, c, hp * 256:(hp + 1) * 256],
                                        start=(c == 0), stop=(c == 1)),
                                       reads=[b_cqT[t], B["wukv"]], writes=[bB[pp]], inc=(c == 1))
                                op("act", lambda e: e.activation(sq[s_], banks[pp][:, :], AF.Square),
                                   reads=[bB[pp]], writes=[b_sq[s_]])
                                op("dve", lambda e: e.tensor_reduce(st6[s_][:, 0:8], sq[s_].rearrange("p (a n) -> p a n", a=8),
                                                                    AX.X, ALU.add),
                                   reads=[b_sq[s_]], writes=[b_st[s_]])
                                op("act", lambda e: e.activation(rs6[s_][:, 0:4], st6[s_][:, 0:4], AF.Ln,
                                                                 bias=epsq[:, t:t + 1], scale=1.0 / 64.0),
                                   reads=[b_st[s_], B["epsq"]], writes=[b_rs[s_]])
                                op("act", lambda e: e.activation(rs6[s_][:, 4:8], st6[s_][:, 4:8], AF.Ln,
                                                                 bias=epskv[:, t:t + 1], scale=1.0 / 64.0),
                                   reads=[b_st[s_], B["epskv"]], writes=[b_rs[s_]])
                                op("act", lambda e: e.activation(rs6[s_][:, 0:8], rs6[s_][:, 0:8], AF.Exp, scale=-0.5),
                                   reads=[b_rs[s_]], writes=[b_rs[s_]])
                            elif part == 1:
                                op("dve", lambda e: e.tensor_tensor(qk4[:, 0:2, 0:64], pq[:, :, 0:64],
                                                                    rsq[:, :, 0:1].broadcast_to([128, 2, 64]), ALU.mult),
                                   reads=[bB[pp], b_rs[s_]], writes=[b_qc[s_]])
                                op("dve", lambda e: e.tensor_tensor(qk4[:, 2:4, 0:64], pkv[:, :, 0:64],
                                                                    rsk[:, :, 0:1].broadcast_to([128, 2, 64]), ALU.mult),
                                   reads=[bB[pp], b_rs[s_]], writes=[b_qc[s_]])
                                op("dve", lambda e: e.tensor_tensor(tr3, pq[:, :, 64:128],
                                                                    rsq[:, :, 1:2].broadcast_to([128, 2, 64]), ALU.mult),
                                   reads=[bB[pp], b_rs[s_]], writes=[b_tr[s_]])
                                op("pool", lambda e: e.tensor_tensor(tr23, tr3, bch(GCS[:, t, :], 2), ALU.mult),
                                   reads=[b_tr[s_], B["GCS"]], writes=[b_rm[s_]])
                                op("pool", lambda e: e.tensor_tensor(qk4[:, 0:2, 64:96], tr23[:, :, 0:32], tr23[:, :, 32:64], ALU.add),
                                   reads=[b_rm[s_]], writes=[b_qc[s_]])
                                op("pool", lambda e: e.tensor_copy(qk4[:, 2:4, 64:96], bch(krr[:, t, :], 2)),
                                   reads=[B["krr"]], writes=[b_qc[s_]])
                                op("act", lambda e: e.activation(
                                    vdst, pkv[:, :, 64:128], AF.Identity,
                                    scale=rstdkv[:, t:t + 1]),
                                   reads=[bB[pp], B["rstdkv"]], writes=[b_Vt[t]])
                            gc = 0
                        else:
                            p4 = banks[pp][:, 0:256].rearrange("p (a n) -> p a n", a=4)
                            qk4 = qk[s_][:, 0:280].rearrange("p (a n) -> p a n", a=4)
                            if part == 0:
                                for c in range(8):
                                    op("pe", lambda e, c=c: e.matmul(
                                        banks[pp][:, 0:384], hT[:, c, ts], wpair[:, c, 0:384], start=(c == 0), stop=(c == 7)),
                                       reads=[b_hT[t], b_wp], writes=[bB[pp]], inc=(c == 7))
                                op("act", lambda e: e.activation(sq[s_][:, 0:256], banks[pp][:, 0:256], AF.Square),
                                   reads=[bB[pp]], writes=[b_sq[s_]])
                                op("dve", lambda e: e.tensor_reduce(st6[s_][:, 0:4],
                                                                    sq[s_][:, 0:256].rearrange("p (a n) -> p a n", a=4), AX.X, ALU.add),
                                   reads=[b_sq[s_]], writes=[b_st[s_]])
                                op("act", lambda e: e.activation(rs6[s_][:, 0:4], st6[s_][:, 0:4], AF.Ln, bias=EPS,
                                                                 scale=1.0 / 64.0),
                                   reads=[b_st[s_]], writes=[b_rs[s_]])
                                op("act", lambda e: e.activation(rs6[s_][:, 0:4], rs6[s_][:, 0:4], AF.Exp, scale=-0.5),
                                   reads=[b_rs[s_]], writes=[b_rs[s_]])
                            elif part == 1:
                                op("dve", lambda e: e.tensor_tensor(qk4[:, :, 0:64], p4, bc2(rs6[s_][:, 0:4], 64), ALU.mult),
                                   reads=[bB[pp], b_rs[s_]], writes=[b_qc[s_]])
                                op("pool", lambda e: e.tensor_copy(qk4[:, 0:2, 64:67], nWs[:, t, 2 * hp:2 * hp + 2, :]),
                                   reads=[B["nWs"]], writes=[b_qc[s_]])
                                op("pool", lambda e: e.tensor_copy(qk4[:, 2:4, 67:70], Ws[:, t, 2 * hp:2 * hp + 2, :]),
                                   reads=[B["Ws"]], writes=[b_qc[s_]])
                                op("act", lambda e: e.copy(vdst, banks[pp][:, 256:384].rearrange("p (h n) -> p h n", h=2)),
                                   reads=[bB[pp]], writes=[b_Vt[t]])
                            gc = 2
                        if part == 2:
                            for a in range(4):
                                op("pe", lambda e, a=a: e.transpose(bankbf[tb][0:Kd, a * 128:(a + 1) * 128], qk4[:, a, :], ident[:]),
                                   reads=[b_qc[s_], B["ident"]], writes=[bB[tb]], inc=(a == 3))
                            op("dve", lambda e: e.tensor_scalar(
                                QKT[0:Kd, 0:2, ts], bankbf[tb][0:Kd, 0:256].rearrange("p (a n) -> p a n", a=2),
                                gcols[0:Kd, gc:gc + 1], None, ALU.mult),
                               reads=[bB[tb], B["gcols"]], writes=[b_QKT[t]])
                            op("dve", lambda e: e.tensor_scalar(
                                QKT[0:Kd, 2:4, ts], bankbf[tb][0:Kd, 256:512].rearrange("p (a n) -> p a n", a=2),
                                gcols[0:Kd, gc + 1:gc + 2], None, ALU.mult),
                               reads=[bB[tb], B["gcols"]], writes=[b_QKT[t]])

                    def proj_step(k):
                        if k < NT:
                            tile(k, 0)
                        if 1 <= k < NT + 1:
                            tile(k - 1, 1)
                        if 2 <= k < NT + 2:
                            tile(k - 2, 2)

                    gc0 = 0 if mla else 384

                    def gate(g):
                        gs = slice(g * 512, (g + 1) * 512)
                        gbk = 4 + (g % 2)
                        for c in range(8):
                            op("pe", lambda e, c=c: e.matmul(
                                banks[gbk][:, :], wpair[:, c, gc0:gc0 + 128], hT[:, c, gs], start=(c == 0), stop=(c == 7)),
                               reads=[b_wp] + b_hT[4 * g:4 * g + 4], writes=[bB[gbk]], inc=(c == 7))
                        op("act", lambda e: e.activation(tg[:], banks[gbk][:, :], AF.Tanh, scale=0.5),
                           reads=[bB[gbk]], writes=[b_tg])
                        op("dve", lambda e: e.scalar_tensor_tensor(
                            gate2[:, gs], tg[:], 1.0, banks[gbk][:, :], ALU.add, ALU.mult),
                           reads=[b_tg, bB[gbk]], writes=[b_g2[g]])

                    maskb = mmaskb if mla else fmaskb

                    def issue_s(g, j):
                        N = 512 if j < 4 * g else 512 - (j - 4 * g) * 128
                        qc0 = g * 512 + 512 - N
                        sb0 = 2 * (j % 2)
                        for hh in range(2):
                            kT = QKT[:, 2 + hh, j * 128:(j + 1) * 128]
                            rd_ = [b_QKT[j], B["qkz"]] + b_QKT[qc0 // 128:4 * g + 4]
                            if j >= 4 * g:
                                op("pe", lambda e, hh=hh: e.matmul(banks[sb0 + hh][:, 0:128], ident[:], maskb[:],
                                                                   start=True, stop=False),
                                   reads=[B["ident"], B["maskb"]], writes=[bB[sb0 + hh]], inc=False)
                                op("pe", lambda e, hh=hh, kT=kT: e.matmul(banks[sb0 + hh][:, 0:128], kT, QKT[:, hh, qc0:qc0 + 128],
                                                                          start=False, stop=True),
                                   reads=rd_, writes=[bB[sb0 + hh]], inc=(hh == 1 and N == 128))
                                if N > 128:
                                    op("pe", lambda e, hh=hh, kT=kT: e.matmul(banks[sb0 + hh][:, 128:N], kT,
                                                                              QKT[:, hh, qc0 + 128:qc0 + N], start=True, stop=True),
                                       reads=rd_, writes=[bB[sb0 + hh]], inc=(hh == 1))
                            else:
                                op("pe", lambda e, hh=hh, kT=kT: e.matmul(banks[sb0 + hh][:, 0:N], kT,
                                                                          QKT[:, hh, qc0:qc0 + N], start=True, stop=True),
                                   reads=rd_, writes=[bB[sb0 + hh]], inc=(hh == 1))
                        s2 = psall[:, sb0 * 512:(sb0 + 2) * 512].rearrange("p (h n) -> p h n", h=2)
                        ps_ = j % 2
                        op("act", lambda e: e.activation(pt[ps_][:, :, 0:N], s2[:, :, 0:N], AF.Exp),
                           reads=[bB[sb0], bB[sb0 + 1]], writes=[b_pt[ps_]])

                    def issue_pv(g, j):
                        N = 512 if j < 4 * g else 512 - (j - 4 * g) * 128
                        ps_ = j % 2
                        for hh in range(2):
                            op("pe", lambda e, hh=hh: e.matmul(banks[4 + hh][:, 512 - N:512], Vt[:, j, hh * 64:hh * 64 + 128],
                                                               pt[ps_][:, hh, 0:N], start=(j == 0), stop=(j == 4 * g + 3)),
                               reads=[b_Vt[j], b_pt[ps_], B["vtwos"]], writes=[bB[4 + hh]], inc=(hh == 1))

                    def group_end_a(g):
                        lo, hi = slice(0, 64), slice(64, 128)
                        op("dve", lambda e: e.tensor_copy(tmpo[lo, :], banks[4][lo, :]), reads=[bB[4]], writes=[b_tmpo])
                        op("dve", lambda e: e.tensor_copy(rd[lo, :], banks[4][hi, :]), reads=[bB[4]], writes=[b_rd])
                        op("dve", lambda e: e.tensor_copy(tmpo[hi, :], banks[5][hi, :]), reads=[bB[5]], writes=[b_tmpo])
                        op("dve", lambda e: e.tensor_copy(rd[hi, :], banks[5][lo, :]), reads=[bB[5]], writes=[b_rd])

                    def group_end_b(g):
                        gs = slice(g * 512, (g + 1) * 512)
                        op("act", lambda e: e.activation(rd, rd, AF.Ln), reads=[b_rd], writes=[b_rd])
                        op("act", lambda e: e.activation(rd, rd, AF.Exp, scale=-1.0), reads=[b_rd], writes=[b_rd])
                        op("dve", lambda e: e.tensor_tensor(tmpo, tmpo, rd, ALU.mult),
                           reads=[b_tmpo, b_rd], writes=[b_tmpo])
                        op("pool", lambda e: e.tensor_tensor(yT[:, ychunk, gs], tmpo, gate2[:, gs], ALU.mult),
                           reads=[b_tmpo, b_g2[g]], writes=[b_yT[ychunk][g]])

                    for k in range(NT + 2):
                        proj_step(k)
                    _TagList.tag[0] = "s%d.j%d.gate" % (b, job)
                    for g in range(4):
                        gate(g)
                    if job == 0:
                        stage_check(4)
                    _TagList.tag[0] = "s%d.j%d.attn" % (b, job)
                    for g in range(4):
                        nst = 4 * g + 4
                        for i in range(nst + 1):
                            if i < nst:
                                issue_s(g, i)
                            if i >= 1:
                                issue_pv(g, i - 1)
                            if i == 2 and g > 0:
                                group_end_b(g - 1)
                        group_end_a(g)
                    group_end_b(3)
                    P.barrier()
                    if job == 0:
                        stage_check(5)
                    if job == 4:
                        stage_check(6)

                stage_check(7)
                _TagList.tag[0] = "s%d.p4" % b
                b_wmg = Buf()
                dma("sp", gate_bc[:], ada_d[b, 2 * D:3 * D].partition_broadcast(128), reads=[B["ada_d"]],
                    writes=[B["gate_bc"]])
                op("dve", lambda e: e.tensor_scalar(gate_bc[:], gate_bc[:], 0.5, None, ALU.mult),
                   reads=[B["gate_bc"]], writes=[B["gate_bc"]])
                for c in range(8):
                    dma("sp", wmg[:, c, :], wbf3[:, c, C_MA:C_MA + 2048], reads=[B["wbf_d"]], writes=[b_wmg])
                b_mT = [Buf() for _ in range(8)]
                b_ta = [Buf(), Buf()]
                b_m12 = [Buf(), Buf()]
                b_xr = [Buf(), Buf()]
                b_res = [Buf(), Buf()]
                k4 = 0
                for g in range(4):
                    gs = slice(g * 512, (g + 1) * 512)
                    for mc in range(8):
                        bs = (k4 % 2) * 4
                        k4 += 1
                        ms = slice(mc * 128, (mc + 1) * 128)
                        for c in range(4):
                            op("pe", lambda e, c=c, bs=bs, ms=ms, gs=gs: e.matmul(banks[bs][:, :], wba[:, c, ms], yT[:, c, gs],
                                                                                  start=(c == 0), stop=(c == 3)),
                               reads=[B["wba"], b_yT[c][g]], writes=[bB[bs]], inc=(c == 3))
                        for c in range(4):
                            op("pe", lambda e, c=c, bs=bs, ms=ms, gs=gs: e.matmul(banks[bs + 1][:, :], wbb[:, c, ms], yT[:, 4 + c, gs],
                                                                                  start=(c == 0), stop=(c == 3)),
                               reads=[B["wbb"], b_yT[4 + c][g]], writes=[bB[bs + 1]], inc=(c == 3))
                        for c in range(8):
                            op("pe", lambda e, c=c, bs=bs, ms=ms, gs=gs: e.matmul(banks[bs + 2][:, :], wmg[:, c, ms], hT[:, c, gs],
                                                                                  start=(c == 0), stop=(c == 7)),
                               reads=[b_wmg] + b_hT[4 * g:4 * g + 4], writes=[bB[bs + 2]], inc=(c == 7))
                        for c in range(8):
                            op("pe", lambda e, c=c, bs=bs, mc=mc, gs=gs: e.matmul(
                                banks[bs + 3][:, :], wmg[:, c, 1024 + mc * 128:1024 + (mc + 1) * 128], hT[:, c, gs],
                                start=(c == 0), stop=(c == 7)),
                               reads=[b_wmg] + b_hT[4 * g:4 * g + 4], writes=[bB[bs + 3]], inc=(c == 7))
                        op("act", lambda e, bs=bs: e.activation(ta[0], banks[bs + 2][:, :], AF.Tanh, scale=0.5),
                           reads=[bB[bs + 2]], writes=[b_ta[0]])
                        op("act", lambda e, bs=bs: e.activation(ta[1], banks[bs + 3][:, :], AF.Tanh, scale=0.5),
                           reads=[bB[bs + 3]], writes=[b_ta[1]])
                        op("dve", lambda e, bs=bs: e.scalar_tensor_tensor(m12[0], ta[0], 1.0, banks[bs][:, :], ALU.add, ALU.mult),
                           reads=[b_ta[0], bB[bs]], writes=[b_m12[0]])
                        op("dve", lambda e, bs=bs: e.scalar_tensor_tensor(m12[1], ta[1], 1.0, banks[bs + 1][:, :], ALU.add, ALU.mult),
                           reads=[b_ta[1], bB[bs + 1]], writes=[b_m12[1]])
                        op("pool", lambda e, mc=mc: e.tensor_tensor(mT[:, mc, :], m12[0], m12[1], ALU.add),
                           reads=[b_m12[0], b_m12[1]], writes=[b_mT[mc]])
                    for tt in range(4):
                        t = 4 * g + tt
                        s_ = t % 2
                        ts = slice(t * 128, (t + 1) * 128)
                        bs = (k4 % 2) * 4
                        k4 += 1
                        dma("sp", xr[s_], x_d[b, ts, :], writes=[b_xr[s_]])
                        for hf in range(2):
                            for c in range(8):
                                op("pe", lambda e, c=c, bs=bs, hf=hf, tt=tt: e.matmul(
                                    banks[bs + hf][:, :], mT[:, c, tt * 128:(tt + 1) * 128], wout[:, c, hf * 512:(hf + 1) * 512],
                                    start=(c == 0), stop=(c == 7)),
                                   reads=[b_mT[c], B["wout"]], writes=[bB[bs + hf]], inc=(c == 7))
                        for hf in range(2):
                            hs = slice(hf * 512, (hf + 1) * 512)
                            op("dve", lambda e, bs=bs, hf=hf, hs=hs, s_=s_: e.tensor_tensor(
                                res[s_][:, hs], banks[bs + hf][:, :], gate_bc[:, hs], ALU.mult),
                               reads=[bB[bs + hf], B["gate_bc"]], writes=[b_res[s_]])
                        op("pool", lambda e, s_=s_: e.tensor_tensor(res[s_], res[s_], xr[s_], ALU.add),
                           reads=[b_res[s_], b_xr[s_]], writes=[b_res[s_]])
                        dma("sp", out_d[b, ts, :], res[s_], reads=[b_res[s_]])
                P.barrier()

        except _Stop:
            pass
        P.finish()
        _DBG["sbuf_remaining"] = nc.sbuf_bytes_remaining
        _DBG["ops"] = {n: len(e.ops) for n, e in P.E.items()}
        _DBG["tags"] = {n: list(e.ops.tags) for n, e in P.E.items()}
        P.emit(st)
    return nc


_NC_CACHE = {}


def _consts():
    ident = np.eye(128, dtype=np.float32)
    tri = np.triu(np.ones((128, 128), np.float32))
    kk = np.arange(128)[:, None]
    qq = np.arange(128)[None, :]
    fmask = np.where(kk <= qq, 0.0, NEG).astype(np.float32)
    mmask = np.where((kk // 64) <= (qq // 64), 0.0, NEG).astype(np.float32)
    invf = (np.float32(10000.0) ** (-(np.arange(0, 32, 2, dtype=np.float32)) / np.float32(32))).astype(np.float32)
    return ident, tri, fmask, mmask, invf.reshape(1, 16)


def kernel(x, c, positions, w_ada, b_ada, norm_w, w_in, b_f, q_lora_norm_w, kv_lora_norm_w, w_uq, w_ukv,
           qn_nope_a, qn_rope_a, kn_nope_a, kn_rope_a, qn_b, kn_b, w_branch_a, w_branch_b, w_out):
    f = lambda a: np.ascontiguousarray(np.asarray(a, dtype=np.float32))
    x = f(x)
    c = f(c)
    positions = np.ascontiguousarray(np.asarray(positions, dtype=np.int32))
    ident, tri, fmask, mmask, invf = _consts()
    shared = {
        "w_ada": f(w_ada)[0], "b_ada": f(b_ada)[0].reshape(1, -1),
        "normw": np.ascontiguousarray(f(norm_w)[0].reshape(8, 128).T),
        "w_in": f(w_in)[0], "b_f": f(b_f)[0].reshape(1, 8),
        "qlw": np.ascontiguousarray(f(q_lora_norm_w)[0].reshape(3, 128).T),
        "kvlw": np.ascontiguousarray(f(kv_lora_norm_w)[0].reshape(2, 128).T),
        "w_uq": f(w_uq)[0], "w_ukv": f(w_ukv)[0],
        "g_qna": f(qn_nope_a)[0].reshape(1, -1), "g_qra": f(qn_rope_a)[0].reshape(1, -1),
        "g_kna": f(kn_nope_a)[0].reshape(1, -1), "g_kra": f(kn_rope_a)[0].reshape(1, -1),
        "g_qb": f(qn_b)[0].reshape(1, -1), "g_kb": f(kn_b)[0].reshape(1, -1),
        "w_ba": f(w_branch_a)[0], "w_bb": f(w_branch_b)[0], "w_out": f(w_out)[0],
        "ident": ident, "tri": tri, "fmask": fmask, "mmask": mmask, "invf": invf,
    }
    gcols = np.ones((128, 4), np.float32)
    gcols[0:64, 0] = f(qn_nope_a)[0]
    gcols[0:64, 1] = f(kn_nope_a)[0]
    gcols[0:64, 2] = f(qn_b)[0]
    gcols[0:64, 3] = f(kn_b)[0]
    shared["gcols"] = gcols
    in_maps = []
    for i in range(NCORES):
        bs = slice(i * BPC, (i + 1) * BPC)
        m = dict(shared)
        m["x"] = x[bs]
        m["cT"] = np.ascontiguousarray(c[bs].reshape(BPC, 8, 128).transpose(2, 1, 0))
        m["pos"] = np.ascontiguousarray(positions[bs].reshape(BPC, NT, 128).transpose(2, 0, 1))
        in_maps.append(m)
    if "nc" not in _NC_CACHE:
        _NC_CACHE["nc"] = build_nc()
    res = run_bass_kernel_spmd(_NC_CACHE["nc"], in_maps, core_ids=list(range(NCORES)))
    return np.concatenate([r["out"] for r in res.results], axis=0).astype(np.float32)
```

```python
import math
from contextlib import ExitStack

import numpy as np
import concourse.bass as bass
import concourse.mybir as mybir
from concourse.bass_utils import run_bass_kernel_spmd

F32 = mybir.dt.float32
BF16 = mybir.dt.bfloat16
I32 = mybir.dt.int32
AF = mybir.ActivationFunctionType
ALU = mybir.AluOpType
AX = mybir.AxisListType

NCORES = 8
BPC = 4
S = 2048
D = 1024
NT = 16
D_IN = 5288
EPS = 1e-6
C_CQ, C_CKV, C_KR, C_GA, C_QB, C_KB, C_VB, C_FB, C_GB, C_MA, C_MB = (
    0, 384, 640, 672, 1184, 1696, 2208, 2720, 2728, 3240, 4264)
TWO_PI_HI = 6.28125
TWO_PI_LO = 2.0 * math.pi - 6.28125
NEG = -30000.0


class Buf:
    __slots__ = ("w", "r", "excl")

    def __init__(self, excl=False):
        self.w = None
        self.r = {}
        self.excl = excl


class _TagList(list):
    tag = [""]

    def append(self, x):
        list.append(self, x)
        self.tags.append(_TagList.tag[0])


class _Eng:
    def __init__(self, name, semkey):
        self.name = name
        self.semkey = semkey
        self.count = 0
        self.waited = {}
        self.ops = _TagList()
        self.ops.tags = []
        self.dma_n = 0
        self.dma_vals = {}


class _Rec:
    def __init__(self):
        self.calls = []

    def __getattr__(self, name):
        def f(*a, **k):
            self.calls.append((name, a, k))
        return f


class Prog:
    ENGS = ("pe", "act", "dve", "pool", "sp")
    NDMA = {"sp": 8, "act": 2, "pool": 2}

    def __init__(self, nc):
        self.nc = nc
        self.E = {n: _Eng(n, ("eng", n)) for n in self.ENGS}
        self.semkeys = [e.semkey for e in self.E.values()]
        for q, k in self.NDMA.items():
            for i in range(k):
                self.semkeys.append(("dma", q, i))
                self.E[q].dma_vals[i] = 0

    def _wait(self, E, k, v):
        if E.waited.get(k, 0) < v:
            E.ops.append(("wait", k, v))
            E.waited[k] = v

    def _need(self, E, reads, writes):
        need = {}
        for b in reads:
            if b.w is not None and need.get(b.w[0], 0) < b.w[1]:
                need[b.w[0]] = b.w[1]
        for b in writes:
            if b.w is not None and need.get(b.w[0], 0) < b.w[1]:
                need[b.w[0]] = b.w[1]
            for k, v in b.r.items():
                if need.get(k, 0) < v:
                    need[k] = v
        for k, v in need.items():
            if k == E.semkey:
                if E.name == "pe" or v > E.count:
                    continue
            self._wait(E, k, v)

    def _mark(self, tok, reads, writes):
        for b in writes:
            b.w = tok
            b.r = {}
        for b in reads:
            if b.r.get(tok[0], 0) < tok[1]:
                b.r[tok[0]] = tok[1]

    def op(self, eng, fn, reads=(), writes=(), inc=True):
        E = self.E[eng]
        if any(b.excl for b in reads):
            writes = list(writes) + [b for b in reads if b.excl]
            reads = [b for b in reads if not b.excl]
        self._need(E, reads, writes)
        tok = (E.semkey, E.count + 1)
        rec = _Rec()
        fn(rec)
        E.ops.append(("inst", rec.calls[0], inc))
        if inc:
            E.count += 1
        self._mark(tok, reads, writes)

    def dma(self, q, out, in_, reads=(), writes=(), **kw):
        E = self.E[q]
        slot = E.dma_n % self.NDMA[q]
        E.dma_n += 1
        k = ("dma", q, slot)
        prev = E.dma_vals[slot]
        if prev > 0:
            self._wait(E, k, prev)
        self._need(E, reads, writes)
        E.dma_vals[slot] = prev + 16
        E.ops.append(("dma", out, in_, k, kw))
        self._mark((k, prev + 16), reads, writes)

    def barrier(self):
        toks = []
        for n in ("pe", "act", "dve", "pool"):
            e = self.E[n]
            if e.count > 0:
                toks.append((e.semkey, e.count))
        for q in self.NDMA:
            for slot, v in self.E[q].dma_vals.items():
                if v > 0:
                    toks.append((("dma", q, slot), v))
        for n in self.ENGS:
            E = self.E[n]
            for k, v in toks:
                if k != E.semkey:
                    self._wait(E, k, v)

    def finish(self):
        self.barrier()

    def emit(self, stack):
        nc = self.nc
        sems = {}
        for k in self.semkeys:
            sems[k] = stack.enter_context(nc.semaphore("s_" + "_".join(str(x) for x in k)))
        block = stack.enter_context(nc.Block())

        def run(E):
            def body(e):
                own = sems[E.semkey]
                for o in E.ops:
                    if o[0] == "wait":
                        e.wait_ge(sems[o[1]], o[2])
                    elif o[0] == "inst":
                        name, a, k = o[1]
                        ins = getattr(e, name)(*a, **k)
                        if o[2]:
                            ins.then_inc(own, 1)
                    else:
                        e.dma_start(out=o[1], in_=o[2], **o[4]).then_inc(sems[o[3]], 16)
            return body

        block.tensor(run(self.E["pe"]))
        block.scalar(run(self.E["act"]))
        block.vector(run(self.E["dve"]))
        block.gpsimd(run(self.E["pool"]))
        block.sync(run(self.E["sp"]))


_DBG = {}


class _Stop(Exception):
    pass


def build_nc(nseq=BPC, stage=99):
    def stage_check(n):
        if stage == n:
            raise _Stop()

    nc = bass.Bass("TRN2", target_bir_lowering=False)
    din = lambda n, s, dt=F32: nc.dram_tensor(n, list(s), dt, kind="ExternalInput").ap()
    x_d = din("x", [BPC, S, D])
    cT_d = din("cT", [128, 8, BPC])
    pos_d = din("pos", [128, BPC, NT], I32)
    wada_d = din("w_ada", [D, 3 * D])
    bada_d = din("b_ada", [1, 3 * D])
    normw_d = din("normw", [128, 8])
    win_d = din("w_in", [D, D_IN])
    bf_d = din("b_f", [1, 8])
    qlw_d = din("qlw", [128, 3])
    kvlw_d = din("kvlw", [128, 2])
    wuq_d = din("w_uq", [384, 768])
    wukv_d = din("w_ukv", [256, 1024])
    gqna_d = din("g_qna", [1, 64])
    gqra_d = din("g_qra", [1, 32])
    gkna_d = din("g_kna", [1, 64])
    gkra_d = din("g_kra", [1, 32])
    gqb_d = din("g_qb", [1, 64])
    gkb_d = din("g_kb", [1, 64])
    wba_d = din("w_ba", [512, D])
    wbb_d = din("w_bb", [512, D])
    wout_d = din("w_out", [D, D])
    ident_d = din("ident", [128, 128])
    tri_d = din("tri", [128, 128])
    fmask_d = din("fmask", [128, 128])
    mmask_d = din("mmask", [128, 128])
    invf_d = din("invf", [1, 16])
    gcols_d = din("gcols", [128, 4])
    out_d = nc.dram_tensor("out", [BPC, S, D], F32, kind="ExternalOutput").ap()
    wbf_d = nc.dram_tensor("wbf_scr", [128, 8 * D_IN], BF16).ap()
    wbf3 = wbf_d.rearrange("p (c n) -> p c n", c=8)
    ada_d = nc.dram_tensor("ada_scr", [BPC, 3 * D], F32).ap()

    P = Prog(nc)
    op, dma = P.op, P.dma
    with ExitStack() as st:
        try:
            _n = [0]

            def sb(shape, dt):
                _n[0] += 1
                return st.enter_context(nc.sbuf_tensor("sb%d" % _n[0], list(shape), dt))

            ident = sb([128, 128], BF16)
            identf = sb([128, 128], F32)
            tri = sb([128, 128], F32)
            ones = sb([128, 128], F32)
            negones = sb([128, 128], F32)
            twos = sb([128, 128], BF16)
            fmask = sb([128, 128], F32)
            mmask = sb([128, 128], F32)
            fmaskb = sb([128, 128], BF16)
            mmaskb = sb([128, 128], BF16)
            g_qna = sb([128, 64], F32)
            g_qra = sb([128, 32], F32)
            g_kna = sb([128, 64], F32)
            g_kra = sb([128, 32], F32)
            g_qb = sb([128, 64], F32)
            g_kb = sb([128, 64], F32)
            bf_bc = sb([128, 8], F32)
            invf = sb([128, 16], F32)
            cT = sb([128, 8, BPC], F32)
            normw = sb([128, 8], F32)
            qlw = sb([128, 3], F32)
            kvlw = sb([128, 2], F32)
            pos_sb = sb([128, BPC, NT], I32)
            wuq = sb([128, 3, 1024], BF16)
            wukv = sb([128, 2, 1024], BF16)
            wba = sb([128, 4, D], BF16)
            wbb = sb([128, 4, D], BF16)
            wout = sb([128, 8, D], BF16)
            hT = sb([128, 8, S], BF16)
            yT = sb([128, 8, S], BF16)
            ovl1 = sb([128, D], F32)
            gate_bc = ovl1
            sc_col = sb([128, 8], F32)
            sh_col = sb([128, 8], F32)
            A_col = sb([128, 8], F32)
            posf = sb([128, NT], F32)
            GCS = sb([128, NT, 64], F32)
            gcols = sb([128, 4], F32)
            g_qra_s = sb([128, 64], F32)
            sint = sb([128, NT, 16], F32)
            cost = sb([128, NT, 16], F32)
            ssq = sb([128, NT], F32)
            sskv = sb([128, NT], F32)
            epsq = sb([128, NT], F32)
            epskv = sb([128, NT], F32)
            rstdkv = sb([128, NT], F32)
            kfraw = sb([128, NT, 40], F32)
            krss = sb([128, NT], F32)
            krr = sb([128, NT, 32], BF16)
            spt = sb([128, NT, 8], F32)
            Wf = sb([128, NT * 8], F32)
            Wr = sb([128, NT * 8], F32)
            Ws = sb([128, NT, 8, 3], BF16)
            nWs = sb([128, NT, 8, 3], BF16)
            Ctab = sb([128, 48], F32)
            ssx = [sb([128, 1], F32) for _ in range(2)]
            rsx = [sb([128, 1], F32) for _ in range(2)]
            pt = [sb([128, 2, 512], BF16) for _ in range(2)]
            X = sb([128, 32768], BF16)

            def xv(off, n, dt):
                if dt == BF16:
                    return X[:, off // 2: off // 2 + n]
                return X[:, off // 2: off // 2 + 2 * n].bitcast(dt)

            psall = st.enter_context(nc.psum_tensor("psall", [128, 4096], F32))
            banks = [psall[:, i * 512:(i + 1) * 512] for i in range(8)]
            bB = [Buf(excl=True) for _ in range(8)]
            bankbf = [b.bitcast(BF16) for b in banks]

            B = {k: Buf() for k in (
                "ident", "identf", "tri", "ones", "negones", "twos", "fmask", "mmask", "maskb", "vtwos", "qkz", "gains", "bf", "invf", "cT",
                "normw", "qlw", "kvlw", "pos", "bada4", "ada4", "ada_d", "wuq", "wukv", "wba", "wbb", "wout", "wbf_d",
                "gate_bc", "cols", "A_col", "posf", "ang", "angk", "angi", "sint", "cost", "ssq", "sskv", "epsq", "epskv",
                "rstdkv", "GCS", "gcols", "kfraw", "krt", "krs", "krss", "krm", "krr", "spt", "Wf", "Wr", "Ws", "nWs", "Ctab")}
            b_hT = [Buf() for _ in range(NT)]
            b_yT = [[Buf() for _ in range(4)] for _ in range(8)]
            b_pt = [Buf() for _ in range(2)]
            b_ssx = [Buf(), Buf()]
            b_rsx = [Buf(), Buf()]

            _TagList.tag[0] = "setup"
            dma("sp", identf[:], ident_d, writes=[B["identf"]])
            op("dve", lambda e: e.tensor_copy(ident[:], identf[:]), reads=[B["identf"]], writes=[B["ident"]])
            dma("sp", tri[:], tri_d, writes=[B["tri"]])
            dma("sp", fmask[:], fmask_d, writes=[B["fmask"]])
            dma("sp", mmask[:], mmask_d, writes=[B["mmask"]])
            op("dve", lambda e: e.tensor_copy(fmaskb[:], fmask[:]), reads=[B["fmask"]], writes=[B["maskb"]])
            op("dve", lambda e: e.tensor_copy(mmaskb[:], mmask[:]), reads=[B["mmask"]], writes=[B["maskb"]])
            op("pool", lambda e: e.memset(ones[:], 1.0), writes=[B["ones"]])
            op("pool", lambda e: e.memset(negones[:], -1.0), writes=[B["negones"]])
            op("pool", lambda e: e.memset(twos[:], 2.0), writes=[B["twos"]])
            for t_, d_ in ((g_qna, gqna_d), (g_qra, gqra_d), (g_kna, gkna_d), (g_kra, gkra_d), (g_qb, gqb_d),
                           (g_kb, gkb_d), (bf_bc, bf_d), (invf, invf_d)):
                dma("sp", t_[:], d_[0].partition_broadcast(128), writes=[B["gains"]])
            op("dve", lambda e: e.tensor_scalar(g_qna[:], g_qna[:], 1.0 / math.sqrt(96.0), None, ALU.mult),
               reads=[B["gains"]], writes=[B["gains"]])
            op("dve", lambda e: e.tensor_scalar(g_qra[:], g_qra[:], 1.0 / math.sqrt(96.0), None, ALU.mult),
               reads=[B["gains"]], writes=[B["gains"]])
            op("dve", lambda e: e.tensor_scalar(g_qb[:], g_qb[:], 0.125, None, ALU.mult),
               reads=[B["gains"]], writes=[B["gains"]])
            dma("sp", gcols[:], gcols_d, writes=[B["gcols"]])
            op("dve", lambda e: e.tensor_scalar(gcols[0:64, 0:1], gcols[0:64, 0:1], 1.0 / math.sqrt(96.0), None, ALU.mult),
               reads=[B["gcols"]], writes=[B["gcols"]])
            op("dve", lambda e: e.tensor_scalar(gcols[0:64, 2:3], gcols[0:64, 2:3], 0.125, None, ALU.mult),
               reads=[B["gcols"]], writes=[B["gcols"]])
            op("dve", lambda e: e.tensor_copy(g_qra_s[:, 0:32], g_qra[:]), reads=[B["gains"]], writes=[B["gains"]])
            op("dve", lambda e: e.tensor_scalar(g_qra_s[:, 32:48], g_qra[:, 16:32], -1.0, None, ALU.mult),
               reads=[B["gains"]], writes=[B["gains"]])
            op("dve", lambda e: e.tensor_copy(g_qra_s[:, 48:64], g_qra[:, 0:16]), reads=[B["gains"]], writes=[B["gains"]])
            dma("sp", cT[:], cT_d, writes=[B["cT"]])
            dma("sp", normw[:], normw_d, writes=[B["normw"]])
            dma("sp", qlw[:], qlw_d, writes=[B["qlw"]])
            dma("sp", kvlw[:], kvlw_d, writes=[B["kvlw"]])
            dma("sp", pos_sb[:], pos_d, writes=[B["pos"]])
            pass

            bada4 = xv(32768, 3 * D, F32)[0:BPC, :]
            ada4 = xv(45056, 3 * D, F32)[0:BPC, :]
            dma("sp", bada4, bada_d[0].partition_broadcast(BPC), writes=[B["bada4"]])
            stg = [xv(0, 8 * 512, F32).rearrange("p (c n) -> p c n", c=8),
                   xv(16384, 8 * 512, F32).rearrange("p (c n) -> p c n", c=8)]
            b_stg = [Buf(), Buf()]
            wada3 = wada_d.rearrange("(c p) n -> p c n", p=128)
            for n in range(6):
                s_ = n % 2
                dma("sp", stg[s_], wada3[:, :, n * 512:(n + 1) * 512], writes=[b_stg[s_]])
                for kc in range(8):
                    op("pe", lambda e, s_=s_, kc=kc, n=n: e.matmul(banks[n % 2][0:BPC, :], cT[:, kc, :], stg[s_][:, kc, :],
                                                                  start=(kc == 0), stop=(kc == 7)),
                       reads=[B["cT"], b_stg[s_]], writes=[bB[n % 2]], inc=(kc == 7))
                op("dve", lambda e, n=n: e.tensor_tensor(ada4[:, n * 512:(n + 1) * 512], banks[n % 2][0:BPC, :],
                                                         bada4[:, n * 512:(n + 1) * 512], ALU.add),
                   reads=[bB[n % 2], B["bada4"]], writes=[B["ada4"]])
            dma("sp", ada_d, ada4, reads=[B["ada4"]], writes=[B["ada_d"]])
            P.barrier()

            sA = xv(0, 3 * 768, F32).rearrange("p (c n) -> p c n", c=3)
            sBv = xv(16384, 2 * 1024, F32).rearrange("p (c n) -> p c n", c=2)
            b_sA, b_sB = Buf(), Buf()
            dma("sp", sA, wuq_d.rearrange("(c p) n -> p c n", p=128), writes=[b_sA])
            dma("sp", sBv, wukv_d.rearrange("(c p) n -> p c n", p=128), writes=[b_sB])
            for c in range(3):
                src3 = sA[:, c, :].rearrange("p (h n) -> p h n", h=8)
                dst3 = wuq[:, c, :].rearrange("p (h n) -> p h n", h=8)
                for d0, d1, s0, s1 in ((0, 96, 0, 96), (96, 112, 80, 96), (112, 128, 64, 80)):
                    op("dve", lambda e, c=c, src3=src3, dst3=dst3, d0=d0, d1=d1, s0=s0, s1=s1: e.tensor_scalar(
                        dst3[:, :, d0:d1], src3[:, :, s0:s1], qlw[:, c:c + 1], None, ALU.mult),
                       reads=[b_sA, B["qlw"]], writes=[B["wuq"]])
            for c in range(2):
                op("dve", lambda e, c=c: e.tensor_scalar(wukv[:, c, :], sBv[:, c, :], kvlw[:, c:c + 1], None, ALU.mult),
                   reads=[b_sB, B["kvlw"]], writes=[B["wukv"]])
            P.barrier()
            sW = [xv(0, 4 * 1024, F32).rearrange("p (c n) -> p c n", c=4),
                  xv(16384, 4 * 1024, F32).rearrange("p (c n) -> p c n", c=4)]
            b_sW = [Buf(), Buf()]
            jobs = [(wba_d.rearrange("(c p) n -> p c n", p=128), wba, 0, "wba"),
                    (wbb_d.rearrange("(c p) n -> p c n", p=128), wbb, 0, "wbb"),
                    (wout_d.rearrange("(c p) n -> p c n", p=128)[:, 0:4, :], wout, 0, "wout"),
                    (wout_d.rearrange("(c p) n -> p c n", p=128)[:, 4:8, :], wout, 4, "wout")]
            for i, (src, dst, c0, key) in enumerate(jobs):
                s_ = i % 2
                dma("sp", sW[s_], src, writes=[b_sW[s_]])
                eng = "dve" if s_ == 0 else "pool"
                op(eng, lambda e, s_=s_, dst=dst, c0=c0: e.tensor_copy(dst[:, c0:c0 + 4, :], sW[s_]),
                   reads=[b_sW[s_]], writes=[B[key]])
            P.barrier()
            HALF = D_IN // 2
            sI = [xv(0, HALF, F32), xv(16384, HALF, F32)]
            sO = [xv(32768, HALF, BF16), xv(40960, HALF, BF16)]
            b_sI = [Buf(), Buf()]
            b_sO = [Buf(), Buf()]
            i = 0
            for kc in range(8):
                for hf in range(2):
                    s_ = i % 2
                    dma("sp", sI[s_], win_d[kc * 128:(kc + 1) * 128, hf * HALF:(hf + 1) * HALF], writes=[b_sI[s_]])
                    eng = ("dve", "pool", "act")[i % 3]
                    if eng == "act":
                        op(eng, lambda e, s_=s_: e.copy(sO[s_], sI[s_]), reads=[b_sI[s_]], writes=[b_sO[s_]])
                    else:
                        op(eng, lambda e, s_=s_: e.tensor_copy(sO[s_], sI[s_]), reads=[b_sI[s_]], writes=[b_sO[s_]])
                    dma("sp", wbf3[:, kc, hf * HALF:(hf + 1) * HALF], sO[s_], reads=[b_sO[s_]], writes=[B["wbf_d"]])
                    i += 1
            P.barrier()

            stage_check(1)
            cqT = xv(0, 3 * S, BF16).rearrange("p (c n) -> p c n", c=3)
            ckvT = xv(12288, 2 * S, BF16).rearrange("p (c n) -> p c n", c=2)
            b_cqT = [Buf() for _ in range(NT)]
            xt = [xv(20480, D, F32), xv(24576, D, F32)]
            junk = xv(28672, D, BF16)
            xn = [xv(30720, D, BF16), xv(32768, D, BF16)]
            wg1 = xv(34816, 8 * 680, BF16).rearrange("p (c n) -> p c n", c=8)
            cqb = [xv(45696, 640, BF16), xv(46976, 640, BF16)]
            htmp = [xv(48256, D, F32), xv(52352, D, F32)]
            ang = xv(60544, NT * 16, F32).rearrange("p (t n) -> p t n", t=NT)
            angk = xv(61568, NT * 16, F32).rearrange("p (t n) -> p t n", t=NT)
            angi = xv(62592, NT * 16, I32).rearrange("p (t n) -> p t n", t=NT)
            krt = xv(56448, NT * 32, F32).rearrange("p (t n) -> p t n", t=NT)
            krs = xv(58496, NT * 32, F32).rearrange("p (t n) -> p t n", t=NT)
            krm = [xv(60544 + 1024 * k_, NT * 16, F32).rearrange("p (t n) -> p t n", t=NT) for k_ in range(4)]
            QKT = xv(20480, 4 * S, BF16).rearrange("p (a n) -> p a n", a=4)
            Vt = xv(36864, NT * 192, BF16).rearrange("p (t n) -> p t n", t=NT)
            gate2 = xv(43008, S, F32)
            wpair = xv(51200, 8 * 512, BF16).rearrange("p (c n) -> p c n", c=8)
            WB = 59392
            sq = [xv(WB, 512, F32), xv(WB + 2048, 512, F32)]
            tmpr = [xv(WB + 4096, 128, F32), xv(WB + 4608, 128, F32)]
            tmpr2 = [xv(WB + 5120, 128, F32), xv(WB + 5632, 128, F32)]
            wmg = xv(0, 8 * 2048, BF16).rearrange("p (c n) -> p c n", c=8)
            mT = xv(32768, 8 * 512, BF16).rearrange("p (c n) -> p c n", c=8)
            ta = [xv(40960, 512, F32), xv(43008, 512, F32)]
            m12 = [xv(45056, 512, F32), xv(47104, 512, F32)]
            xr = [xv(49152, D, F32), xv(53248, D, F32)]
            res = [xv(57344, D, F32), xv(61440, D, F32)]
            st6 = [sb([128, 8], F32) for _ in range(3)]
            rs6 = [sb([128, 8], F32) for _ in range(3)]
            sq.append(sb([128, 512], F32)[:])
            tmpr.append(sb([128, 128], F32)[:])
            tmpr2.append(sb([128, 128], F32)[:])
            qk = [sb([128, 384], BF16) for _ in range(3)]
            rd = ovl1[:, 0:512]
            tmpo = ovl1[:, 512:1024]
            tg = sb([128, 512], F32)
            b_rd, b_tmpo, b_tg = Buf(), Buf(), Buf()

            def bc2(ap2, n):
                return ap2.unsqueeze(2).broadcast_to([128, ap2.shape[1], n])

            def bch(ap2, h):
                return ap2.unsqueeze(1).broadcast_to([128, h, ap2.shape[1]])

            for b in range(nseq):
                _TagList.tag[0] = "s%d.p1" % b
                dma("sp", sh_col[:], ada_d[b, 0:D].rearrange("(c p) -> p c", p=128), reads=[B["ada_d"]],
                    writes=[B["cols"]], allow_slow_non_contiguous=True)
                dma("sp", sc_col[:], ada_d[b, D:2 * D].rearrange("(c p) -> p c", p=128), reads=[B["ada_d"]],
                    writes=[B["cols"]], allow_slow_non_contiguous=True)
                op("dve", lambda e: e.scalar_tensor_tensor(A_col[:], sc_col[:], 1.0, normw[:], ALU.add, ALU.mult),
                   reads=[B["cols"], B["normw"]], writes=[B["A_col"]])
                stage_check(11)
                op("dve", lambda e, b=b: e.tensor_copy(posf[:], pos_sb[:, b, :]), reads=[B["pos"]], writes=[B["posf"]])
                op("dve", lambda e: e.tensor_tensor(ang, bc2(posf[:], 16), bch(invf[:], NT), ALU.mult),
                   reads=[B["posf"], B["gains"]], writes=[B["ang"]])

                def reduce_angle(dst_key_unused=None):
                    op("dve", lambda e: e.tensor_scalar(angk, ang, 1.0 / (2.0 * math.pi), None, ALU.mult),
                       reads=[B["ang"]], writes=[B["angk"]])
                    op("dve", lambda e: e.tensor_copy(angi, angk), reads=[B["angk"]], writes=[B["angi"]])
                    op("dve", lambda e: e.tensor_copy(angk, angi), reads=[B["angi"]], writes=[B["angk"]])
                    op("dve", lambda e: e.scalar_tensor_tensor(ang, angk, -TWO_PI_HI, ang, ALU.mult, ALU.add),
                       reads=[B["angk"], B["ang"]], writes=[B["ang"]])
                    op("dve", lambda e: e.scalar_tensor_tensor(ang, angk, -TWO_PI_LO, ang, ALU.mult, ALU.add),
                       reads=[B["angk"], B["ang"]], writes=[B["ang"]])
                    op("dve", lambda e: e.tensor_scalar(ang, ang, math.pi, -math.pi, ALU.min, ALU.max),
                       reads=[B["ang"]], writes=[B["ang"]])

                reduce_angle()
                op("act", lambda e: e.activation(sint[:], ang, AF.Sin), reads=[B["ang"]], writes=[B["sint"]])
                op("dve", lambda e: e.tensor_scalar(ang, ang, 0.5 * math.pi, None, ALU.add),
                   reads=[B["ang"], B["sint"]], writes=[B["ang"]])
                reduce_angle()
                op("act", lambda e: e.activation(cost[:], ang, AF.Sin), reads=[B["ang"]], writes=[B["cost"]])
                G4 = GCS[:].rearrange("p t (k n) -> p t k n", k=4)
                for k_, tab, key in ((0, cost, "cost"), (1, cost, "cost"), (2, sint, "sint"), (3, sint, "sint")):
                    op("dve", lambda e, k_=k_, tab=tab: e.tensor_tensor(
                        G4[:, :, k_, :], tab[:], bch(g_qra_s[:, k_ * 16:(k_ + 1) * 16], NT), ALU.mult),
                       reads=[B[key], B["gains"]], writes=[B["GCS"]])

                stage_check(12)
                b_wg1 = Buf()
                dma("sp", wg1[:, :, 0:672], wbf3[:, :, 0:672], reads=[B["wbf_d"]], writes=[b_wg1])
                dma("sp", wg1[:, :, 672:680], wbf3[:, :, C_FB:C_FB + 8], reads=[B["wbf_d"]], writes=[b_wg1])

                b_xt = [Buf(), Buf()]
                b_junk = Buf()
                b_xn = [Buf(), Buf()]
                b_cqb = [Buf(), Buf()]
                b_htmp = [Buf(), Buf()]
                def p1(t, part):
                  s_ = t % 2
                  ts = slice(t * 128, (t + 1) * 128)
                  gA, gB = 2 + s_, 4 + s_
                  tb = 6 + s_
                  if part == 1:
                    dma("sp", xt[s_], x_d[b, ts, :], writes=[b_xt[s_]])
                    op("act", lambda e, s_=s_: e.activation(junk, xt[s_], AF.Square, accum_out=ssx[s_][:]),
                       reads=[b_xt[s_]], writes=[b_junk, b_ssx[s_]])
                    op("act", lambda e, s_=s_: e.activation(rsx[s_][:], ssx[s_][:], AF.Ln, bias=EPS, scale=1.0 / D),
                       reads=[b_ssx[s_]], writes=[b_rsx[s_]])
                    op("act", lambda e, s_=s_: e.activation(rsx[s_][:], rsx[s_][:], AF.Exp, scale=-0.5),
                       reads=[b_rsx[s_]], writes=[b_rsx[s_]])
                    op("dve", lambda e, s_=s_: e.tensor_scalar(xn[s_], xt[s_], rsx[s_][:], None, ALU.mult),
                       reads=[b_xt[s_], b_rsx[s_]], writes=[b_xn[s_]])
                  pb = s_
                  if part == 2:
                    for c in range(8):
                        op("pe", lambda e, c=c, s_=s_, pb=pb: e.transpose(bankbf[pb][:, c * 128:(c + 1) * 128],
                                                                          xn[s_][:, c * 128:(c + 1) * 128], ident[:]),
                           reads=[b_xn[s_], B["ident"]], writes=[bB[pb]], inc=(c == 7))
                    op("dve", lambda e, s_=s_, pb=pb: e.tensor_tensor(
                        htmp[s_].rearrange("p (c n) -> p c n", c=8), bankbf[pb].rearrange("p (c n) -> p c n", c=8),
                        bc2(A_col[:], 128), ALU.mult),
                       reads=[bB[pb], B["A_col"]], writes=[b_htmp[s_]])
                    op("pool", lambda e, s_=s_, ts=ts: e.tensor_tensor(
                        hT[:, :, ts], htmp[s_].rearrange("p (c n) -> p c n", c=8), bc2(sh_col[:], 128), ALU.add),
                       reads=[b_htmp[s_], B["cols"]], writes=[b_hT[t]])
                  if part == 3:
                    for c in range(8):
                        op("pe", lambda e, c=c, ts=ts, gA=gA: e.matmul(banks[gA][:, 0:384], hT[:, c, ts], wg1[:, c, 0:384],
                                                                       start=(c == 0), stop=(c == 7)),
                           reads=[b_hT[t], b_wg1], writes=[bB[gA]], inc=(c == 7))
                    for c in range(8):
                        op("pe", lambda e, c=c, ts=ts, gB=gB: e.matmul(banks[gB][:, 0:296], hT[:, c, ts], wg1[:, c, 384:680],
                                                                       start=(c == 0), stop=(c == 7)),
                           reads=[b_hT[t], b_wg1], writes=[bB[gB]], inc=(c == 7))

                    op("act", lambda e, gA=gA, t=t: e.activation(junk[:, 0:384], banks[gA][:, 0:384], AF.Square,
                                                                 accum_out=ssq[:, t:t + 1]),
                       reads=[bB[gA]], writes=[b_junk, B["ssq"]])
                    op("act", lambda e, gB=gB, t=t: e.activation(junk[:, 0:256], banks[gB][:, 0:256], AF.Square,
                                                                 accum_out=sskv[:, t:t + 1]),
                       reads=[bB[gB]], writes=[b_junk, B["sskv"]])
                    op("dve", lambda e, gA=gA, s_=s_: e.tensor_copy(cqb[s_][:, 0:384], banks[gA][:, 0:384]),
                       reads=[bB[gA]], writes=[b_cqb[s_]])
                    op("dve", lambda e, gB=gB, s_=s_: e.tensor_copy(cqb[s_][:, 384:640], banks[gB][:, 0:256]),
                       reads=[bB[gB]], writes=[b_cqb[s_]])
                    op("act", lambda e, gB=gB, t=t: e.copy(kfraw[:, t, :], banks[gB][:, 256:296]),
                       reads=[bB[gB]], writes=[B["kfraw"]])
                  if part == 4:
                    for c in range(5):
                        op("pe", lambda e, c=c, s_=s_, tb=tb: e.transpose(bankbf[tb][:, c * 128:(c + 1) * 128],
                                                                          cqb[s_][:, c * 128:(c + 1) * 128], ident[:]),
                           reads=[b_cqb[s_], B["ident"]], writes=[bB[tb]], inc=(c == 4))
                    op("dve", lambda e, tb=tb, ts=ts: e.tensor_copy(
                        cqT[:, :, ts], bankbf[tb][:, 0:384].rearrange("p (c n) -> p c n", c=3)),
                       reads=[bB[tb]], writes=[b_cqT[t]])
                    op("act", lambda e, tb=tb, ts=ts: e.copy(
                        ckvT[:, :, ts], bankbf[tb][:, 384:640].rearrange("p (c n) -> p c n", c=2)),
                       reads=[bB[tb]], writes=[b_cqT[t]])


                for t in range(NT + 3):
                    if t < NT:
                        p1(t, 1)
                    if 1 <= t < NT + 1:
                        p1(t - 1, 2)
                    if 2 <= t < NT + 2:
                        p1(t - 2, 3)
                    if t >= 3:
                        p1(t - 3, 4)

                stage_check(2)
                _TagList.tag[0] = "s%d.p1c" % b
                op("dve", lambda e: e.tensor_scalar(epsq[:], ssq[:], EPS / 384.0, EPS * EPS, ALU.mult, ALU.add),
                   reads=[B["ssq"]], writes=[B["epsq"]])
                op("dve", lambda e: e.tensor_scalar(epskv[:], sskv[:], EPS / 256.0, EPS * EPS, ALU.mult, ALU.add),
                   reads=[B["sskv"]], writes=[B["epskv"]])
                op("act", lambda e: e.activation(rstdkv[:], sskv[:], AF.Ln, bias=EPS, scale=1.0 / 256.0),
                   reads=[B["sskv"]], writes=[B["rstdkv"]])
                op("act", lambda e: e.activation(rstdkv[:], rstdkv[:], AF.Exp, scale=-0.5),
                   reads=[B["rstdkv"]], writes=[B["rstdkv"]])
                op("dve", lambda e: e.tensor_tensor(krs, kfraw[:, :, 0:32], kfraw[:, :, 0:32], ALU.mult),
                   reads=[B["kfraw"]], writes=[B["krs"]])
                op("dve", lambda e: e.tensor_reduce(krss[:], krs, AX.X, ALU.add), reads=[B["krs"]], writes=[B["krss"]])
                op("act", lambda e: e.activation(krss[:], krss[:], AF.Ln, bias=EPS, scale=1.0 / 32.0),
                   reads=[B["krss"]], writes=[B["krss"]])
                op("act", lambda e: e.activation(krss[:], krss[:], AF.Exp, scale=-0.5),
                   reads=[B["krss"]], writes=[B["krss"]])
                op("dve", lambda e: e.tensor_tensor(krt, kfraw[:, :, 0:32], bc2(krss[:], 32), ALU.mult),
                   reads=[B["kfraw"], B["krss"]], writes=[B["krt"]])
                op("dve", lambda e: e.tensor_tensor(krt, krt, bch(g_kra[:], NT), ALU.mult),
                   reads=[B["krt"], B["gains"]], writes=[B["krt"]])
                x1, x2 = krt[:, :, 0:16], krt[:, :, 16:32]
                op("dve", lambda e: e.tensor_tensor(krm[0], x1, cost[:], ALU.mult), reads=[B["krt"], B["cost"]], writes=[B["krm"]])
                op("dve", lambda e: e.tensor_tensor(krm[1], x2, sint[:], ALU.mult), reads=[B["krt"], B["sint"]], writes=[B["krm"]])
                op("dve", lambda e: e.tensor_tensor(krm[2], x2, cost[:], ALU.mult), reads=[B["krt"], B["cost"]], writes=[B["krm"]])
                op("dve", lambda e: e.tensor_tensor(krm[3], x1, sint[:], ALU.mult), reads=[B["krt"], B["sint"]], writes=[B["krm"]])
                op("dve", lambda e: e.tensor_tensor(krr[:, :, 0:16], krm[0], krm[1], ALU.subtract),
                   reads=[B["krm"]], writes=[B["krr"]])
                op("dve", lambda e: e.tensor_tensor(krr[:, :, 16:32], krm[2], krm[3], ALU.add),
                   reads=[B["krm"]], writes=[B["krr"]])
                op("dve", lambda e: e.tensor_tensor(spt[:], kfraw[:, :, 32:40], bch(bf_bc[:], NT), ALU.add),
                   reads=[B["kfraw"], B["gains"]], writes=[B["spt"]])
                op("act", lambda e: e.activation(spt[:], spt[:], AF.Exp, scale=-1.0), reads=[B["spt"]], writes=[B["spt"]])
                op("act", lambda e: e.activation(spt[:], spt[:], AF.Ln, bias=1.0), reads=[B["spt"]], writes=[B["spt"]])
                spt2 = spt[:].rearrange("p t h -> p (t h)")
                op("pe", lambda e: e.matmul(banks[0][:, 0:128], tri[:], spt2, start=True, stop=True),
                   reads=[B["tri"], B["spt"]], writes=[bB[0]])
                op("pe", lambda e: e.matmul(banks[1][:, 0:128], ones[:], spt2, start=True, stop=True),
                   reads=[B["ones"], B["spt"]], writes=[bB[1]])
                op("dve", lambda e: e.tensor_copy(Wf[:], banks[0][:, 0:128]), reads=[bB[0]], writes=[B["Wf"]])
                op("act", lambda e: e.copy(Wr[:], banks[1][:, 0:128]), reads=[bB[1]], writes=[B["Wr"]])
                scanA = (Wr[:].rearrange("p (t h) -> p t h", t=NT), "Wr")
                scanB = (spt[:], "spt")
                for d_ in (1, 2, 4, 8):
                    (A_, ka), (B_, kb) = scanA, scanB
                    op("dve", lambda e, A_=A_, B_=B_, d_=d_: e.tensor_copy(B_[:, 0:d_, :], A_[:, 0:d_, :]),
                       reads=[B[ka]], writes=[B[kb]])
                    op("dve", lambda e, A_=A_, B_=B_, d_=d_: e.tensor_tensor(B_[:, d_:NT, :], A_[:, d_:NT, :], A_[:, 0:NT - d_, :], ALU.add),
                       reads=[B[ka]], writes=[B[kb]])
                    scanA, scanB = scanB, scanA
                Wf3_ = Wf[:].rearrange("p (t h) -> p t h", t=NT)
                op("dve", lambda e: e.tensor_tensor(Wf3_[:, 1:NT, :], Wf3_[:, 1:NT, :], scanA[0][:, 0:NT - 1, :], ALU.add),
                   reads=[B["Wf"], B[scanA[1]]], writes=[B["Wf"]])
                Wf3 = Wf[:].rearrange("p (t h) -> p t h", t=NT)
                Wr3 = Wr[:].rearrange("p (t h) -> p t h", t=NT)
                op("dve", lambda e: e.tensor_copy(Ws[:, :, :, 0], Wf3), reads=[B["Wf"]], writes=[B["Ws"]])
                op("dve", lambda e: e.tensor_tensor(Wr3, Wf3, Ws[:, :, :, 0], ALU.subtract),
                   reads=[B["Wf"], B["Ws"]], writes=[B["Wr"]])
                op("dve", lambda e: e.tensor_copy(Ws[:, :, :, 1], Wr3), reads=[B["Wr"]], writes=[B["Ws"]])
                op("dve", lambda e: e.tensor_tensor(Wr3, Wr3, Ws[:, :, :, 1], ALU.subtract),
                   reads=[B["Wr"], B["Ws"]], writes=[B["Wr"]])
                op("dve", lambda e: e.tensor_copy(Ws[:, :, :, 2], Wr3), reads=[B["Wr"]], writes=[B["Ws"]])
                op("dve", lambda e: e.tensor_scalar(nWs[:], Ws[:], -1.0, None, ALU.mult), reads=[B["Ws"]], writes=[B["nWs"]])
                P.barrier()

                stage_check(3)
                op("pool", lambda e: e.memset(Vt[:, :, 64:128], 2.0), writes=[B["vtwos"]])
                op("pool", lambda e: e.memset(QKT[96:128, :, :], 0.0), writes=[B["qkz"]])
                b_wp = Buf()

                def load_wpair(jb):
                    hp_ = jb % 4
                    if jb < 4:
                        dma("sp", wpair[:, :, 0:128], wbf3[:, :, C_GA + hp_ * 128:C_GA + (hp_ + 1) * 128],
                            reads=[B["wbf_d"]], writes=[b_wp])
                    else:
                        for k_, c0 in enumerate((C_QB, C_KB, C_VB, C_GB)):
                            dma("sp", wpair[:, :, k_ * 128:(k_ + 1) * 128], wbf3[:, :, c0 + hp_ * 128:c0 + (hp_ + 1) * 128],
                                reads=[B["wbf_d"]], writes=[b_wp])

                for job in range(8):
                    _TagList.tag[0] = "s%d.j%d.proj" % (b, job)
                    mla = job < 4
                    hp = job % 4
                    Kd = 96 if mla else 70
                    ychunk = hp if mla else 4 + hp
                    if job == 0:
                        load_wpair(0)
                    b_QKT = [Buf() for _ in range(NT)]
                    if job == 4:
                        op("pool", lambda e: e.memset(QKT[64:96, :, :], 0.0), writes=b_QKT + [B["qkz"]])
                    b_Vt = [Buf() for _ in range(NT)]
                    b_sq = [Buf(), Buf(), Buf()]
                    b_st = [Buf(), Buf(), Buf()]
                    b_rs = [Buf(), Buf(), Buf()]
                    b_tn = [Buf(), Buf(), Buf()]
                    b_tr = [Buf(), Buf(), Buf()]
                    b_rm = [Buf(), Buf(), Buf()]
                    b_qc = [Buf(), Buf(), Buf()]
                    b_kc = [Buf(), Buf(), Buf()]
                    b_g2 = [Buf() for _ in range(4)]
                    if not mla:
                        for s_ in range(3):
                            qk4 = qk[s_][:, 0:280].rearrange("p (a n) -> p a n", a=4)
                            op("pool", lambda e, qk4=qk4: e.memset(qk4[:, 0:2, 67:70], 1.0), writes=[b_qc[s_]])
                            op("pool", lambda e, qk4=qk4: e.memset(qk4[:, 2:4, 64:67], 1.0), writes=[b_qc[s_]])

                    def tile(t, part):
                        s_ = t % 3
                        ts = slice(t * 128, (t + 1) * 128)
                        pp = s_
                        tb = 6 + (t % 2)
                        vdst = Vt[:, t, :].rearrange("p (a n) -> p a n", a=3)[:, 0:3:2, :]
                        if mla:
                            pq = banks[pp][:, 0:256].rearrange("p (h n) -> p h n", h=2)
                            pkv = banks[pp][:, 256:512].rearrange("p (h n) -> p h n", h=2)
                            qk4 = qk[s_][:, 0:384].rearrange("p (a n) -> p a n", a=4)
                            rsq = rs6[s_][:, 0:4].rearrange("p (h k) -> p h k", k=2)
                            rsk = rs6[s_][:, 4:8].rearrange("p (h k) -> p h k", k=2)
                            tr3 = tmpr[s_].rearrange("p (h n) -> p h n", h=2)
                            tr23 = tmpr2[s_].rearrange("p (h n) -> p h n", h=2)
                            if part == 0:
                                for c in range(3):
                                    op("pe", lambda e, c=c: e.matmul(
                                        banks[pp][:, 0:256], cqT[:, c, ts], wuq[:, c, hp * 256:(hp + 1) * 256],
                                        start=(c == 0), stop=(c == 2)),
                                       reads=[b_cqT[t], B["wuq"]], writes=[bB[pp]], inc=False)
                                for c in range(2):
                                    op("pe", lambda e, c=c: e.matmul(
                                        banks[pp][:, 256:512], ckvT[:, c, ts], wukv[:, c, hp * 256:(hp + 1) * 256],
                                        start=(c == 0), stop=(c == 1)),
                                       reads=[b_cqT[t], B["wukv"]], writes=[bB[pp]], inc=(c == 1))
                                op("act", lambda e: e.activation(sq[s_], banks[pp][:, :], AF.Square),
                                   reads=[bB[pp]], writes=[b_sq[s_]])
                                op("dve", lambda e: e.tensor_reduce(st6[s_][:, 0:8], sq[s_].rearrange("p (a n) -> p a n", a=8),
                                                                    AX.X, ALU.add),
                                   reads=[b_sq[s_]], writes=[b_st[s_]])
                                op("act", lambda e: e.activation(rs6[s_][:, 0:4], st6[s_][:, 0:4], AF.Ln,
                                                                 bias=epsq[:, t:t + 1], scale=1.0 / 64.0),
                                   reads=[b_st[s_], B["epsq"]], writes=[b_rs[s_]])
                                op("act", lambda e: e.activation(rs6[s_][:, 4:8], st6[s_][:, 4:8], AF.Ln,
                                                                 bias=epskv[:, t:t + 1], scale=1.0 / 64.0),
                                   reads=[b_st[s_], B["epskv"]], writes=[b_rs[s_]])
                                op("act", lambda e: e.activation(rs6[s_][:, 0:8], rs6[s_][:, 0:8], AF.Exp, scale=-0.5),
                                   reads=[b_rs[s_]], writes=[b_rs[s_]])
                            elif part == 1:
                                op("dve", lambda e: e.tensor_tensor(qk4[:, 0:2, 0:64], pq[:, :, 0:64],
                                                                    rsq[:, :, 0:1].broadcast_to([128, 2, 64]), ALU.mult),
                                   reads=[bB[pp], b_rs[s_]], writes=[b_qc[s_]])
                                op("dve", lambda e: e.tensor_tensor(qk4[:, 2:4, 0:64], pkv[:, :, 0:64],
                                                                    rsk[:, :, 0:1].broadcast_to([128, 2, 64]), ALU.mult),
                                   reads=[bB[pp], b_rs[s_]], writes=[b_qc[s_]])
                                op("dve", lambda e: e.tensor_tensor(tr3, pq[:, :, 64:128],
                                                                    rsq[:, :, 1:2].broadcast_to([128, 2, 64]), ALU.mult),
                                   reads=[bB[pp], b_rs[s_]], writes=[b_tr[s_]])
                                op("pool", lambda e: e.tensor_tensor(tr23, tr3, bch(GCS[:, t, :], 2), ALU.mult),
                                   reads=[b_tr[s_], B["GCS"]], writes=[b_rm[s_]])
                                op("pool", lambda e: e.tensor_tensor(qk4[:, 0:2, 64:96], tr23[:, :, 0:32], tr23[:, :, 32:64], ALU.add),
                                   reads=[b_rm[s_]], writes=[b_qc[s_]])
                                op("pool", lambda e: e.tensor_copy(qk4[:, 2:4, 64:96], bch(krr[:, t, :], 2)),
                                   reads=[B["krr"]], writes=[b_qc[s_]])
                                op("act", lambda e: e.activation(
                                    vdst, pkv[:, :, 64:128], AF.Identity,
                                    scale=rstdkv[:, t:t + 1]),
                                   reads=[bB[pp], B["rstdkv"]], writes=[b_Vt[t]])
                            gc = 0
                        else:
                            p4 = banks[pp][:, 0:256].rearrange("p (a n) -> p a n", a=4)
                            qk4 = qk[s_][:, 0:280].rearrange("p (a n) -> p a n", a=4)
                            if part == 0:
                                for c in range(8):
                                    op("pe", lambda e, c=c: e.matmul(
                                        banks[pp][:, 0:384], hT[:, c, ts], wpair[:, c, 0:384], start=(c == 0), stop=(c == 7)),
                                       reads=[b_hT[t], b_wp], writes=[bB[pp]], inc=(c == 7))
                                op("act", lambda e: e.activation(sq[s_][:, 0:256], banks[pp][:, 0:256], AF.Square),
                                   reads=[bB[pp]], writes=[b_sq[s_]])
                                op("dve", lambda e: e.tensor_reduce(st6[s_][:, 0:4],
                                                                    sq[s_][:, 0:256].rearrange("p (a n) -> p a n", a=4), AX.X, ALU.add),
                                   reads=[b_sq[s_]], writes=[b_st[s_]])
                                op("act", lambda e: e.activation(rs6[s_][:, 0:4], st6[s_][:, 0:4], AF.Ln, bias=EPS,
                                                                 scale=1.0 / 64.0),
                                   reads=[b_st[s_]], writes=[b_rs[s_]])
                                op("act", lambda e: e.activation(rs6[s_][:, 0:4], rs6[s_][:, 0:4], AF.Exp, scale=-0.5),
                                   reads=[b_rs[s_]], writes=[b_rs[s_]])
                            elif part == 1:
                                op("dve", lambda e: e.tensor_tensor(qk4[:, :, 0:64], p4, bc2(rs6[s_][:, 0:4], 64), ALU.mult),
                                   reads=[bB[pp], b_rs[s_]], writes=[b_qc[s_]])
                                op("pool", lambda e: e.tensor_copy(qk4[:, 0:2, 64:67], nWs[:, t, 2 * hp:2 * hp + 2, :]),
                                   reads=[B["nWs"]], writes=[b_qc[s_]])
                                op("pool", lambda e: e.tensor_copy(qk4[:, 2:4, 67:70], Ws[:, t, 2 * hp:2 * hp + 2, :]),
                                   reads=[B["Ws"]], writes=[b_qc[s_]])
                                op("act", lambda e: e.copy(vdst, banks[pp][:, 256:384].rearrange("p (h n) -> p h n", h=2)),
                                   reads=[bB[pp]], writes=[b_Vt[t]])
                            gc = 2
                        if part == 2:
                            for a in range(4):
                                op("pe", lambda e, a=a: e.transpose(bankbf[tb][0:Kd, a * 128:(a + 1) * 128], qk4[:, a, :], ident[:]),
                                   reads=[b_qc[s_], B["ident"]], writes=[bB[tb]], inc=(a == 3))
                            op("dve", lambda e: e.tensor_scalar(
                                QKT[0:Kd, 0:2, ts], bankbf[tb][0:Kd, 0:256].rearrange("p (a n) -> p a n", a=2),
                                gcols[0:Kd, gc:gc + 1], None, ALU.mult),
                               reads=[bB[tb], B["gcols"]], writes=[b_QKT[t]])
                            op("dve", lambda e: e.tensor_scalar(
                                QKT[0:Kd, 2:4, ts], bankbf[tb][0:Kd, 256:512].rearrange("p (a n) -> p a n", a=2),
                                gcols[0:Kd, gc + 1:gc + 2], None, ALU.mult),
                               reads=[bB[tb], B["gcols"]], writes=[b_QKT[t]])

                    def proj_step(k):
                        if k < NT:
                            tile(k, 0)
                        if 1 <= k < NT + 1:
                            tile(k - 1, 1)
                        if 2 <= k < NT + 2:
                            tile(k - 2, 2)

                    gc0 = 0 if mla else 384

                    def gate(g):
                        gs = slice(g * 512, (g + 1) * 512)
                        gbk = 4 + (g % 2)
                        for c in range(8):
                            op("pe", lambda e, c=c: e.matmul(
                                banks[gbk][:, :], wpair[:, c, gc0:gc0 + 128], hT[:, c, gs], start=(c == 0), stop=(c == 7)),
                               reads=[b_wp] + b_hT[4 * g:4 * g + 4], writes=[bB[gbk]], inc=(c == 7))
                        op("act", lambda e: e.activation(tg[:], banks[gbk][:, :], AF.Tanh, scale=0.5),
                           reads=[bB[gbk]], writes=[b_tg])
                        op("dve", lambda e: e.scalar_tensor_tensor(
                            gate2[:, gs], tg[:], 1.0, banks[gbk][:, :], ALU.add, ALU.mult),
                           reads=[b_tg, bB[gbk]], writes=[b_g2[g]])

                    maskb = mmaskb if mla else fmaskb

                    def issue_s(g, j):
                        N = 512 if j < 4 * g else 512 - (j - 4 * g) * 128
                        qc0 = g * 512 + 512 - N
                        sb0 = 2 * (j % 2)
                        for hh in range(2):
                            kT = QKT[:, 2 + hh, j * 128:(j + 1) * 128]
                            rd_ = [b_QKT[j], B["qkz"]] + b_QKT[qc0 // 128:4 * g + 4]
                            if j >= 4 * g:
                                op("pe", lambda e, hh=hh: e.matmul(banks[sb0 + hh][:, 0:128], ident[:], maskb[:],
                                                                   start=True, stop=False),
                                   reads=[B["ident"], B["maskb"]], writes=[bB[sb0 + hh]], inc=False)
                                op("pe", lambda e, hh=hh, kT=kT: e.matmul(banks[sb0 + hh][:, 0:128], kT, QKT[:, hh, qc0:qc0 + 128],
                                                                          start=False, stop=True),
                                   reads=rd_, writes=[bB[sb0 + hh]], inc=(hh == 1 and N == 128))
                                if N > 128:
                                    op("pe", lambda e, hh=hh, kT=kT: e.matmul(banks[sb0 + hh][:, 128:N], kT,
                                                                              QKT[:, hh, qc0 + 128:qc0 + N], start=True, stop=True),
                                       reads=rd_, writes=[bB[sb0 + hh]], inc=(hh == 1))
                            else:
                                op("pe", lambda e, hh=hh, kT=kT: e.matmul(banks[sb0 + hh][:, 0:N], kT,
                                                                          QKT[:, hh, qc0:qc0 + N], start=True, stop=True),
                                   reads=rd_, writes=[bB[sb0 + hh]], inc=(hh == 1))
                        s2 = psall[:, sb0 * 512:(sb0 + 2) * 512].rearrange("p (h n) -> p h n", h=2)
                        ps_ = j % 2
                        op("act", lambda e: e.activation(pt[ps_][:, :, 0:N], s2[:, :, 0:N], AF.Exp),
                           reads=[bB[sb0], bB[sb0 + 1]], writes=[b_pt[ps_]])

                    def issue_pv(g, j):
                        N = 512 if j < 4 * g else 512 - (j - 4 * g) * 128
                        ps_ = j % 2
                        for hh in range(2):
                            op("pe", lambda e, hh=hh: e.matmul(banks[4 + hh][:, 512 - N:512], Vt[:, j, hh * 64:hh * 64 + 128],
                                                               pt[ps_][:, hh, 0:N], start=(j == 0), stop=(j == 4 * g + 3)),
                               reads=[b_Vt[j], b_pt[ps_], B["vtwos"]], writes=[bB[4 + hh]], inc=(hh == 1))

                    def group_end_a(g):
                        lo, hi = slice(0, 64), slice(64, 128)
                        op("dve", lambda e: e.tensor_copy(tmpo[lo, :], banks[4][lo, :]), reads=[bB[4]], writes=[b_tmpo])
                        op("dve", lambda e: e.tensor_copy(rd[lo, :], banks[4][hi, :]), reads=[bB[4]], writes=[b_rd])
                        op("dve", lambda e: e.tensor_copy(tmpo[hi, :], banks[5][hi, :]), reads=[bB[5]], writes=[b_tmpo])
                        op("dve", lambda e: e.tensor_copy(rd[hi, :], banks[5][lo, :]), reads=[bB[5]], writes=[b_rd])

                    def group_end_b(g):
                        gs = slice(g * 512, (g + 1) * 512)
                        op("act", lambda e: e.activation(rd, rd, AF.Ln), reads=[b_rd], writes=[b_rd])
                        op("act", lambda e: e.activation(rd, rd, AF.Exp, scale=-1.0), reads=[b_rd], writes=[b_rd])
                        op("dve", lambda e: e.tensor_tensor(tmpo, tmpo, rd, ALU.mult),
                           reads=[b_tmpo, b_rd], writes=[b_tmpo])
                        op("pool", lambda e: e.tensor_tensor(yT[:, ychunk, gs], tmpo, gate2[:, gs], ALU.mult),
                           reads=[b_tmpo, b_g2[g]], writes=[b_yT[ychunk][g]])

                    for k in range(NT + 2):
                        proj_step(k)
                    _TagList.tag[0] = "s%d.j%d.gate" % (b, job)
                    for g in range(4):
                        gate(g)
                    if job < 7:
                        load_wpair(job + 1)
                    if job == 0:
                        stage_check(4)
                    _TagList.tag[0] = "s%d.j%d.attn" % (b, job)
                    for g in range(4):
                        nst = 4 * g + 4
                        for i in range(nst + 1):
                            if i < nst:
                                issue_s(g, i)
                            if i >= 1:
                                issue_pv(g, i - 1)
                            if i == 2 and g > 0:
                                group_end_b(g - 1)
                        group_end_a(g)
                    group_end_b(3)
                    P.barrier()
                    if job == 0:
                        stage_check(5)
                    if job == 4:
                        stage_check(6)

                stage_check(7)
                _TagList.tag[0] = "s%d.p4" % b
                b_wmg = [Buf() for _ in range(8)]
                dma("sp", gate_bc[:], ada_d[b, 2 * D:3 * D].partition_broadcast(128), reads=[B["ada_d"]],
                    writes=[B["gate_bc"]])
                op("dve", lambda e: e.tensor_scalar(gate_bc[:], gate_bc[:], 0.5, None, ALU.mult),
                   reads=[B["gate_bc"]], writes=[B["gate_bc"]])
                for c in range(8):
                    dma("sp", wmg[:, c, :], wbf3[:, c, C_MA:C_MA + 2048], reads=[B["wbf_d"]], writes=[b_wmg[c]])
                b_mT = [Buf() for _ in range(8)]
                b_ta = [Buf(), Buf()]
                b_m12 = [Buf(), Buf()]
                b_xr = [Buf(), Buf()]
                b_res = [Buf(), Buf()]
                k4 = 0
                for g in range(4):
                    gs = slice(g * 512, (g + 1) * 512)
                    for mc in range(8):
                        bs = (k4 % 2) * 4
                        k4 += 1
                        ms = slice(mc * 128, (mc + 1) * 128)
                        for c in range(4):
                            op("pe", lambda e, c=c, bs=bs, ms=ms, gs=gs: e.matmul(banks[bs][:, :], wba[:, c, ms], yT[:, c, gs],
                                                                                  start=(c == 0), stop=(c == 3)),
                               reads=[B["wba"], b_yT[c][g]], writes=[bB[bs]], inc=(c == 3))
                        for c in range(4):
                            op("pe", lambda e, c=c, bs=bs, ms=ms, gs=gs: e.matmul(banks[bs + 1][:, :], wbb[:, c, ms], yT[:, 4 + c, gs],
                                                                                  start=(c == 0), stop=(c == 3)),
                               reads=[B["wbb"], b_yT[4 + c][g]], writes=[bB[bs + 1]], inc=(c == 3))
                        for c in range(8):
                            op("pe", lambda e, c=c, bs=bs, ms=ms, gs=gs: e.matmul(banks[bs + 2][:, :], wmg[:, c, ms], hT[:, c, gs],
                                                                                  start=(c == 0), stop=(c == 7)),
                               reads=[b_wmg[c]] + b_hT[4 * g:4 * g + 4], writes=[bB[bs + 2]], inc=(c == 7))
                        for c in range(8):
                            op("pe", lambda e, c=c, bs=bs, mc=mc, gs=gs: e.matmul(
                                banks[bs + 3][:, :], wmg[:, c, 1024 + mc * 128:1024 + (mc + 1) * 128], hT[:, c, gs],
                                start=(c == 0), stop=(c == 7)),
                               reads=[b_wmg[c]] + b_hT[4 * g:4 * g + 4], writes=[bB[bs + 3]], inc=(c == 7))
                        op("act", lambda e, bs=bs: e.activation(ta[0], banks[bs + 2][:, :], AF.Tanh, scale=0.5),
                           reads=[bB[bs + 2]], writes=[b_ta[0]])
                        op("act", lambda e, bs=bs: e.activation(ta[1], banks[bs + 3][:, :], AF.Tanh, scale=0.5),
                           reads=[bB[bs + 3]], writes=[b_ta[1]])
                        op("dve", lambda e, bs=bs: e.scalar_tensor_tensor(m12[0], ta[0], 1.0, banks[bs][:, :], ALU.add, ALU.mult),
                           reads=[b_ta[0], bB[bs]], writes=[b_m12[0]])
                        op("dve", lambda e, bs=bs: e.scalar_tensor_tensor(m12[1], ta[1], 1.0, banks[bs + 1][:, :], ALU.add, ALU.mult),
                           reads=[b_ta[1], bB[bs + 1]], writes=[b_m12[1]])
                        op("pool", lambda e, mc=mc: e.tensor_tensor(mT[:, mc, :], m12[0], m12[1], ALU.add),
                           reads=[b_m12[0], b_m12[1]], writes=[b_mT[mc]])
                    for tt in range(4):
                        t = 4 * g + tt
                        s_ = t % 2
                        ts = slice(t * 128, (t + 1) * 128)
                        bs = (k4 % 2) * 4
                        k4 += 1
                        dma("sp", xr[s_], x_d[b, ts, :], writes=[b_xr[s_]])
                        for hf in range(2):
                            for c in range(8):
                                op("pe", lambda e, c=c, bs=bs, hf=hf, tt=tt: e.matmul(
                                    banks[bs + hf][:, :], mT[:, c, tt * 128:(tt + 1) * 128], wout[:, c, hf * 512:(hf + 1) * 512],
                                    start=(c == 0), stop=(c == 7)),
                                   reads=[b_mT[c], B["wout"]], writes=[bB[bs + hf]], inc=(c == 7))
                        for hf in range(2):
                            hs = slice(hf * 512, (hf + 1) * 512)
                            op("dve", lambda e, bs=bs, hf=hf, hs=hs, s_=s_: e.tensor_tensor(
                                res[s_][:, hs], banks[bs + hf][:, :], gate_bc[:, hs], ALU.mult),
                               reads=[bB[bs + hf], B["gate_bc"]], writes=[b_res[s_]])
                        op("pool", lambda e, s_=s_: e.tensor_tensor(res[s_], res[s_], xr[s_], ALU.add),
                           reads=[b_res[s_], b_xr[s_]], writes=[b_res[s_]])
                        dma("sp", out_d[b, ts, :], res[s_], reads=[b_res[s_]])
                P.barrier()

        except _Stop:
            pass
        P.finish()
        _DBG["sbuf_remaining"] = nc.sbuf_bytes_remaining
        _DBG["ops"] = {n: len(e.ops) for n, e in P.E.items()}
        _DBG["tags"] = {n: list(e.ops.tags) for n, e in P.E.items()}
        P.emit(st)
    return nc


_NC_CACHE = {}


def _consts():
    ident = np.eye(128, dtype=np.float32)
    tri = np.triu(np.ones((128, 128), np.float32))
    kk = np.arange(128)[:, None]
    qq = np.arange(128)[None, :]
    fmask = np.where(kk <= qq, 0.0, NEG).astype(np.float32)
    mmask = np.where((kk // 64) <= (qq // 64), 0.0, NEG).astype(np.float32)
    invf = (np.float32(10000.0) ** (-(np.arange(0, 32, 2, dtype=np.float32)) / np.float32(32))).astype(np.float32)
    return ident, tri, fmask, mmask, invf.reshape(1, 16)


def kernel(x, c, positions, w_ada, b_ada, norm_w, w_in, b_f, q_lora_norm_w, kv_lora_norm_w, w_uq, w_ukv,
           qn_nope_a, qn_rope_a, kn_nope_a, kn_rope_a, qn_b, kn_b, w_branch_a, w_branch_b, w_out):
    f = lambda a: np.ascontiguousarray(np.asarray(a, dtype=np.float32))
    x = f(x)
    c = f(c)
    positions = np.ascontiguousarray(np.asarray(positions, dtype=np.int32))
    ident, tri, fmask, mmask, invf = _consts()
    shared = {
        "w_ada": f(w_ada)[0], "b_ada": f(b_ada)[0].reshape(1, -1),
        "normw": np.ascontiguousarray(f(norm_w)[0].reshape(8, 128).T),
        "w_in": f(w_in)[0], "b_f": f(b_f)[0].reshape(1, 8),
        "qlw": np.ascontiguousarray(f(q_lora_norm_w)[0].reshape(3, 128).T),
        "kvlw": np.ascontiguousarray(f(kv_lora_norm_w)[0].reshape(2, 128).T),
        "w_uq": f(w_uq)[0], "w_ukv": f(w_ukv)[0],
        "g_qna": f(qn_nope_a)[0].reshape(1, -1), "g_qra": f(qn_rope_a)[0].reshape(1, -1),
        "g_kna": f(kn_nope_a)[0].reshape(1, -1), "g_kra": f(kn_rope_a)[0].reshape(1, -1),
        "g_qb": f(qn_b)[0].reshape(1, -1), "g_kb": f(kn_b)[0].reshape(1, -1),
        "w_ba": f(w_branch_a)[0], "w_bb": f(w_branch_b)[0], "w_out": f(w_out)[0],
        "ident": ident, "tri": tri, "fmask": fmask, "mmask": mmask, "invf": invf,
    }
    gcols = np.ones((128, 4), np.float32)
    gcols[0:64, 0] = f(qn_nope_a)[0]
    gcols[0:64, 1] = f(kn_nope_a)[0]
    gcols[0:64, 2] = f(qn_b)[0]
    gcols[0:64, 3] = f(kn_b)[0]
    shared["gcols"] = gcols
    in_maps = []
    for i in range(NCORES):
        bs = slice(i * BPC, (i + 1) * BPC)
        m = dict(shared)
        m["x"] = x[bs]
        m["cT"] = np.ascontiguousarray(c[bs].reshape(BPC, 8, 128).transpose(2, 1, 0))
        m["pos"] = np.ascontiguousarray(positions[bs].reshape(BPC, NT, 128).transpose(2, 0, 1))
        in_maps.append(m)
    if "nc" not in _NC_CACHE:
        _NC_CACHE["nc"] = build_nc()
    res = run_bass_kernel_spmd(_NC_CACHE["nc"], in_maps, core_ids=list(range(NCORES)))
    return np.concatenate([r["out"] for r in res.results], axis=0).astype(np.float32)
```

```python
import math
from contextlib import ExitStack

import numpy as np
import concourse.bass as bass
import concourse.mybir as mybir
from concourse.bass_utils import run_bass_kernel_spmd

F32 = mybir.dt.float32
BF16 = mybir.dt.bfloat16
I32 = mybir.dt.int32
AF = mybir.ActivationFunctionType
ALU = mybir.AluOpType
AX = mybir.AxisListType

NCORES = 8
BPC = 4
S = 2048
D = 1024
NT = 16
D_IN = 5288
EPS = 1e-6
C_CQ, C_CKV, C_KR, C_GA, C_QB, C_KB, C_VB, C_FB, C_GB, C_MA, C_MB = (
    0, 384, 640, 672, 1184, 1696, 2208, 2720, 2728, 3240, 4264)
TWO_PI_HI = 6.28125
TWO_PI_LO = 2.0 * math.pi - 6.28125
NEG = -30000.0


class Buf:
    __slots__ = ("w", "r", "excl")

    def __init__(self, excl=False):
        self.w = None
        self.r = {}
        self.excl = excl


class _TagList(list):
    tag = [""]

    def append(self, x):
        list.append(self, x)
        self.tags.append(_TagList.tag[0])


class _Eng:
    def __init__(self, name, semkey):
        self.name = name
        self.semkey = semkey
        self.count = 0
        self.waited = {}
        self.ops = _TagList()
        self.ops.tags = []
        self.dma_n = 0
        self.dma_vals = {}


class _Rec:
    def __init__(self):
        self.calls = []

    def __getattr__(self, name):
        def f(*a, **k):
            self.calls.append((name, a, k))
        return f


class Prog:
    ENGS = ("pe", "act", "dve", "pool", "sp")
    NDMA = {"sp": 8, "act": 2, "pool": 2}

    def __init__(self, nc):
        self.nc = nc
        self.E = {n: _Eng(n, ("eng", n)) for n in self.ENGS}
        self.semkeys = [e.semkey for e in self.E.values()]
        for q, k in self.NDMA.items():
            for i in range(k):
                self.semkeys.append(("dma", q, i))
                self.E[q].dma_vals[i] = 0

    def _wait(self, E, k, v):
        if E.waited.get(k, 0) < v:
            E.ops.append(("wait", k, v))
            E.waited[k] = v

    def _need(self, E, reads, writes):
        need = {}
        for b in reads:
            if b.w is not None and need.get(b.w[0], 0) < b.w[1]:
                need[b.w[0]] = b.w[1]
        for b in writes:
            if b.w is not None and need.get(b.w[0], 0) < b.w[1]:
                need[b.w[0]] = b.w[1]
            for k, v in b.r.items():
                if need.get(k, 0) < v:
                    need[k] = v
        for k, v in need.items():
            if k == E.semkey:
                if E.name == "pe" or v > E.count:
                    continue
            self._wait(E, k, v)

    def _mark(self, tok, reads, writes):
        for b in writes:
            b.w = tok
            b.r = {}
        for b in reads:
            if b.r.get(tok[0], 0) < tok[1]:
                b.r[tok[0]] = tok[1]

    def op(self, eng, fn, reads=(), writes=(), inc=True):
        E = self.E[eng]
        if any(b.excl for b in reads):
            writes = list(writes) + [b for b in reads if b.excl]
            reads = [b for b in reads if not b.excl]
        self._need(E, reads, writes)
        tok = (E.semkey, E.count + 1)
        rec = _Rec()
        fn(rec)
        E.ops.append(("inst", rec.calls[0], inc))
        if inc:
            E.count += 1
        self._mark(tok, reads, writes)

    def dma(self, q, out, in_, reads=(), writes=(), **kw):
        E = self.E[q]
        slot = E.dma_n % self.NDMA[q]
        E.dma_n += 1
        k = ("dma", q, slot)
        prev = E.dma_vals[slot]
        if prev > 0:
            self._wait(E, k, prev)
        self._need(E, reads, writes)
        E.dma_vals[slot] = prev + 16
        E.ops.append(("dma", out, in_, k, kw))
        self._mark((k, prev + 16), reads, writes)

    def barrier(self):
        toks = []
        for n in ("pe", "act", "dve", "pool"):
            e = self.E[n]
            if e.count > 0:
                toks.append((e.semkey, e.count))
        for q in self.NDMA:
            for slot, v in self.E[q].dma_vals.items():
                if v > 0:
                    toks.append((("dma", q, slot), v))
        for n in self.ENGS:
            E = self.E[n]
            for k, v in toks:
                if k != E.semkey:
                    self._wait(E, k, v)

    def finish(self):
        self.barrier()

    def emit(self, stack):
        nc = self.nc
        sems = {}
        for k in self.semkeys:
            sems[k] = stack.enter_context(nc.semaphore("s_" + "_".join(str(x) for x in k)))
        block = stack.enter_context(nc.Block())

        def run(E):
            def body(e):
                own = sems[E.semkey]
                for o in E.ops:
                    if o[0] == "wait":
                        e.wait_ge(sems[o[1]], o[2])
                    elif o[0] == "inst":
                        name, a, k = o[1]
                        ins = getattr(e, name)(*a, **k)
                        if o[2]:
                            ins.then_inc(own, 1)
                    else:
                        e.dma_start(out=o[1], in_=o[2], **o[4]).then_inc(sems[o[3]], 16)
            return body

        block.tensor(run(self.E["pe"]))
        block.scalar(run(self.E["act"]))
        block.vector(run(self.E["dve"]))
        block.gpsimd(run(self.E["pool"]))
        block.sync(run(self.E["sp"]))


_DBG = {}


class _Stop(Exception):
    pass


def build_nc(nseq=BPC, stage=99):
    def stage_check(n):
        if stage == n:
            raise _Stop()

    nc = bass.Bass("TRN2", target_bir_lowering=False)
    din = lambda n, s, dt=F32: nc.dram_tensor(n, list(s), dt, kind="ExternalInput").ap()
    x_d = din("x", [BPC, S, D])
    cT_d = din("cT", [128, 8, BPC])
    pos_d = din("pos", [128, BPC, NT], I32)
    wada_d = din("w_ada", [D, 3 * D])
    bada_d = din("b_ada", [1, 3 * D])
    normw_d = din("normw", [128, 8])
    win_d = din("w_in", [D, D_IN])
    bf_d = din("b_f", [1, 8])
    qlw_d = din("qlw", [128, 3])
    kvlw_d = din("kvlw", [128, 2])
    wuq_d = din("w_uq", [384, 768])
    wukv_d = din("w_ukv", [256, 1024])
    gqna_d = din("g_qna", [1, 64])
    gqra_d = din("g_qra", [1, 32])
    gkna_d = din("g_kna", [1, 64])
    gkra_d = din("g_kra", [1, 32])
    gqb_d = din("g_qb", [1, 64])
    gkb_d = din("g_kb", [1, 64])
    wba_d = din("w_ba", [512, D])
    wbb_d = din("w_bb", [512, D])
    wout_d = din("w_out", [D, D])
    ident_d = din("ident", [128, 128])
    tri_d = din("tri", [128, 128])
    fmask_d = din("fmask", [128, 128])
    mmask_d = din("mmask", [128, 128])
    invf_d = din("invf", [1, 16])
    gcols_d = din("gcols", [128, 4])
    out_d = nc.dram_tensor("out", [BPC, S, D], F32, kind="ExternalOutput").ap()
    wbf_d = nc.dram_tensor("wbf_scr", [128, 8 * D_IN], BF16).ap()
    wbf3 = wbf_d.rearrange("p (c n) -> p c n", c=8)
    ada_d = nc.dram_tensor("ada_scr", [BPC, 3 * D], F32).ap()

    P = Prog(nc)
    op, dma = P.op, P.dma
    with ExitStack() as st:
        try:
            _n = [0]

            def sb(shape, dt):
                _n[0] += 1
                return st.enter_context(nc.sbuf_tensor("sb%d" % _n[0], list(shape), dt))

            ident = sb([128, 128], BF16)
            identf = sb([128, 128], F32)
            tri = sb([128, 128], F32)
            ones = sb([128, 128], F32)
            negones = sb([128, 128], F32)
            twos = sb([128, 128], BF16)
            fmask = sb([128, 128], F32)
            mmask = sb([128, 128], F32)
            fmaskb = sb([128, 128], BF16)
            mmaskb = sb([128, 128], BF16)
            g_qna = sb([128, 64], F32)
            g_qra = sb([128, 32], F32)
            g_kna = sb([128, 64], F32)
            g_kra = sb([128, 32], F32)
            g_qb = sb([128, 64], F32)
            g_kb = sb([128, 64], F32)
            bf_bc = sb([128, 8], F32)
            invf = sb([128, 16], F32)
            cT = sb([128, 8, BPC], F32)
            normw = sb([128, 8], F32)
            qlw = sb([128, 3], F32)
            kvlw = sb([128, 2], F32)
            pos_sb = sb([128, BPC, NT], I32)
            wuq = sb([128, 3, 1024], BF16)
            wukv = sb([128, 2, 1024], BF16)
            wba = sb([128, 4, D], BF16)
            wbb = sb([128, 4, D], BF16)
            wout = sb([128, 8, D], BF16)
            hT = sb([128, 8, S], BF16)
            yT = sb([128, 8, S], BF16)
            ovl1 = sb([128, D], F32)
            gate_bc = ovl1
            sc_all = sb([128, BPC, 8], F32)
            sh_all = sb([128, BPC, 8], F32)
            A_all = sb([128, BPC, 8], F32)
            posf = sb([128, NT], F32)
            GCS = sb([128, NT, 64], F32)
            gcols = sb([128, 4], F32)
            g_qra_s = sb([128, 64], F32)
            sint = sb([128, NT, 16], F32)
            cost = sb([128, NT, 16], F32)
            ssq = sb([128, NT], F32)
            sskv = sb([128, NT], F32)
            epsq = sb([128, NT], F32)
            epskv = sb([128, NT], F32)
            rstdkv = sb([128, NT], F32)
            kfraw = sb([128, NT, 40], F32)
            krss = sb([128, NT], F32)
            krr = sb([128, NT, 32], BF16)
            spt = sb([128, NT, 8], F32)
            Wf = sb([128, NT * 8], F32)
            Wr = sb([128, NT * 8], F32)
            Ws = sb([128, NT, 8, 3], BF16)
            nWs = sb([128, NT, 8, 3], BF16)
            Ctab = sb([128, 48], F32)
            ssx = [sb([128, 1], F32) for _ in range(2)]
            rsx = [sb([128, 1], F32) for _ in range(2)]
            pt = [sb([128, 2, 512], BF16) for _ in range(2)]
            X = sb([128, 32768], BF16)

            def xv(off, n, dt):
                if dt == BF16:
                    return X[:, off // 2: off // 2 + n]
                return X[:, off // 2: off // 2 + 2 * n].bitcast(dt)

            psall = st.enter_context(nc.psum_tensor("psall", [128, 4096], F32))
            banks = [psall[:, i * 512:(i + 1) * 512] for i in range(8)]
            bB = [Buf(excl=True) for _ in range(8)]
            bankbf = [b.bitcast(BF16) for b in banks]

            B = {k: Buf() for k in (
                "ident", "identf", "tri", "ones", "negones", "twos", "fmask", "mmask", "maskb", "vtwos", "qkz", "gains", "bf", "invf", "cT",
                "normw", "qlw", "kvlw", "pos", "bada4", "ada4", "ada_d", "wuq", "wukv", "wba", "wbb", "wout", "wbf_d",
                "gate_bc", "cols", "A_col", "posf", "ang", "angk", "angi", "sint", "cost", "ssq", "sskv", "epsq", "epskv",
                "rstdkv", "GCS", "gcols", "kfraw", "krt", "krs", "krss", "krm", "krr", "spt", "Wf", "Wr", "Ws", "nWs", "Ctab")}
            b_hT = [Buf() for _ in range(NT)]
            b_yT = [[Buf() for _ in range(4)] for _ in range(8)]
            b_pt = [Buf() for _ in range(2)]
            b_ssx = [Buf(), Buf()]
            b_rsx = [Buf(), Buf()]

            _TagList.tag[0] = "setup"
            dma("sp", identf[:], ident_d, writes=[B["identf"]])
            op("dve", lambda e: e.tensor_copy(ident[:], identf[:]), reads=[B["identf"]], writes=[B["ident"]])
            dma("sp", tri[:], tri_d, writes=[B["tri"]])
            dma("sp", fmask[:], fmask_d, writes=[B["fmask"]])
            dma("sp", mmask[:], mmask_d, writes=[B["mmask"]])
            op("dve", lambda e: e.tensor_copy(fmaskb[:], fmask[:]), reads=[B["fmask"]], writes=[B["maskb"]])
            op("dve", lambda e: e.tensor_copy(mmaskb[:], mmask[:]), reads=[B["mmask"]], writes=[B["maskb"]])
            op("pool", lambda e: e.memset(ones[:], 1.0), writes=[B["ones"]])
            op("pool", lambda e: e.memset(negones[:], -1.0), writes=[B["negones"]])
            op("pool", lambda e: e.memset(twos[:], 2.0), writes=[B["twos"]])
            for t_, d_ in ((g_qna, gqna_d), (g_qra, gqra_d), (g_kna, gkna_d), (g_kra, gkra_d), (g_qb, gqb_d),
                           (g_kb, gkb_d), (bf_bc, bf_d), (invf, invf_d)):
                dma("sp", t_[:], d_[0].partition_broadcast(128), writes=[B["gains"]])
            op("dve", lambda e: e.tensor_scalar(g_qna[:], g_qna[:], 1.0 / math.sqrt(96.0), None, ALU.mult),
               reads=[B["gains"]], writes=[B["gains"]])
            op("dve", lambda e: e.tensor_scalar(g_qra[:], g_qra[:], 1.0 / math.sqrt(96.0), None, ALU.mult),
               reads=[B["gains"]], writes=[B["gains"]])
            op("dve", lambda e: e.tensor_scalar(g_qb[:], g_qb[:], 0.125, None, ALU.mult),
               reads=[B["gains"]], writes=[B["gains"]])
            dma("sp", gcols[:], gcols_d, writes=[B["gcols"]])
            op("dve", lambda e: e.tensor_scalar(gcols[0:64, 0:1], gcols[0:64, 0:1], 1.0 / math.sqrt(96.0), None, ALU.mult),
               reads=[B["gcols"]], writes=[B["gcols"]])
            op("dve", lambda e: e.tensor_scalar(gcols[0:64, 2:3], gcols[0:64, 2:3], 0.125, None, ALU.mult),
               reads=[B["gcols"]], writes=[B["gcols"]])
            op("dve", lambda e: e.tensor_copy(g_qra_s[:, 0:32], g_qra[:]), reads=[B["gains"]], writes=[B["gains"]])
            op("dve", lambda e: e.tensor_scalar(g_qra_s[:, 32:48], g_qra[:, 16:32], -1.0, None, ALU.mult),
               reads=[B["gains"]], writes=[B["gains"]])
            op("dve", lambda e: e.tensor_copy(g_qra_s[:, 48:64], g_qra[:, 0:16]), reads=[B["gains"]], writes=[B["gains"]])
            dma("sp", cT[:], cT_d, writes=[B["cT"]])
            dma("sp", normw[:], normw_d, writes=[B["normw"]])
            dma("sp", qlw[:], qlw_d, writes=[B["qlw"]])
            dma("sp", kvlw[:], kvlw_d, writes=[B["kvlw"]])
            dma("sp", pos_sb[:], pos_d, writes=[B["pos"]])
            pass

            bada4 = xv(32768, 3 * D, F32)[0:BPC, :]
            ada4 = xv(45056, 3 * D, F32)[0:BPC, :]
            dma("sp", bada4, bada_d[0].partition_broadcast(BPC), writes=[B["bada4"]])
            stg = [xv(0, 8 * 512, F32).rearrange("p (c n) -> p c n", c=8),
                   xv(16384, 8 * 512, F32).rearrange("p (c n) -> p c n", c=8)]
            b_stg = [Buf(), Buf()]
            wada3 = wada_d.rearrange("(c p) n -> p c n", p=128)
            for n in range(6):
                s_ = n % 2
                dma("sp", stg[s_], wada3[:, :, n * 512:(n + 1) * 512], writes=[b_stg[s_]])
                for kc in range(8):
                    op("pe", lambda e, s_=s_, kc=kc, n=n: e.matmul(banks[n % 2][0:BPC, :], cT[:, kc, :], stg[s_][:, kc, :],
                                                                  start=(kc == 0), stop=(kc == 7)),
                       reads=[B["cT"], b_stg[s_]], writes=[bB[n % 2]], inc=(kc == 7))
                op("dve", lambda e, n=n: e.tensor_tensor(ada4[:, n * 512:(n + 1) * 512], banks[n % 2][0:BPC, :],
                                                         bada4[:, n * 512:(n + 1) * 512], ALU.add),
                   reads=[bB[n % 2], B["bada4"]], writes=[B["ada4"]])
            dma("sp", ada_d, ada4, reads=[B["ada4"]], writes=[B["ada_d"]])
            P.barrier()
            for b_ in range(BPC):
                dma("sp", sh_all[:, b_, :], ada_d[b_, 0:D].rearrange("(c p) -> p c", p=128), reads=[B["ada_d"]],
                    writes=[B["cols"]], allow_slow_non_contiguous=True)
                dma("sp", sc_all[:, b_, :], ada_d[b_, D:2 * D].rearrange("(c p) -> p c", p=128), reads=[B["ada_d"]],
                    writes=[B["cols"]], allow_slow_non_contiguous=True)
                op("dve", lambda e, b_=b_: e.scalar_tensor_tensor(A_all[:, b_, :], sc_all[:, b_, :], 1.0, normw[:], ALU.add, ALU.mult),
                   reads=[B["cols"], B["normw"]], writes=[B["A_col"]])

            sA = xv(0, 3 * 768, F32).rearrange("p (c n) -> p c n", c=3)
            sBv = xv(16384, 2 * 1024, F32).rearrange("p (c n) -> p c n", c=2)
            b_sA, b_sB = Buf(), Buf()
            dma("sp", sA, wuq_d.rearrange("(c p) n -> p c n", p=128), writes=[b_sA])
            dma("sp", sBv, wukv_d.rearrange("(c p) n -> p c n", p=128), writes=[b_sB])
            for c in range(3):
                src3 = sA[:, c, :].rearrange("p (h n) -> p h n", h=8)
                dst3 = wuq[:, c, :].rearrange("p (h n) -> p h n", h=8)
                for d0, d1, s0, s1 in ((0, 96, 0, 96), (96, 112, 80, 96), (112, 128, 64, 80)):
                    op("dve", lambda e, c=c, src3=src3, dst3=dst3, d0=d0, d1=d1, s0=s0, s1=s1: e.tensor_scalar(
                        dst3[:, :, d0:d1], src3[:, :, s0:s1], qlw[:, c:c + 1], None, ALU.mult),
                       reads=[b_sA, B["qlw"]], writes=[B["wuq"]])
            for c in range(2):
                op("dve", lambda e, c=c: e.tensor_scalar(wukv[:, c, :], sBv[:, c, :], kvlw[:, c:c + 1], None, ALU.mult),
                   reads=[b_sB, B["kvlw"]], writes=[B["wukv"]])
            P.barrier()
            sW = [xv(0, 4 * 1024, F32).rearrange("p (c n) -> p c n", c=4),
                  xv(16384, 4 * 1024, F32).rearrange("p (c n) -> p c n", c=4)]
            b_sW = [Buf(), Buf()]
            jobs = [(wba_d.rearrange("(c p) n -> p c n", p=128), wba, 0, "wba"),
                    (wbb_d.rearrange("(c p) n -> p c n", p=128), wbb, 0, "wbb"),
                    (wout_d.rearrange("(c p) n -> p c n", p=128)[:, 0:4, :], wout, 0, "wout"),
                    (wout_d.rearrange("(c p) n -> p c n", p=128)[:, 4:8, :], wout, 4, "wout")]
            for i, (src, dst, c0, key) in enumerate(jobs):
                s_ = i % 2
                dma("sp", sW[s_], src, writes=[b_sW[s_]])
                eng = "dve" if s_ == 0 else "pool"
                op(eng, lambda e, s_=s_, dst=dst, c0=c0: e.tensor_copy(dst[:, c0:c0 + 4, :], sW[s_]),
                   reads=[b_sW[s_]], writes=[B[key]])
            P.barrier()
            HALF = D_IN // 2
            sI = [xv(0, HALF, F32), xv(16384, HALF, F32)]
            sO = [xv(32768, HALF, BF16), xv(40960, HALF, BF16)]
            b_sI = [Buf(), Buf()]
            b_sO = [Buf(), Buf()]
            i = 0
            for kc in range(8):
                for hf in range(2):
                    s_ = i % 2
                    dma("sp", sI[s_], win_d[kc * 128:(kc + 1) * 128, hf * HALF:(hf + 1) * HALF], writes=[b_sI[s_]])
                    eng = ("dve", "pool", "act")[i % 3]
                    if eng == "act":
                        op(eng, lambda e, s_=s_: e.copy(sO[s_], sI[s_]), reads=[b_sI[s_]], writes=[b_sO[s_]])
                    else:
                        op(eng, lambda e, s_=s_: e.tensor_copy(sO[s_], sI[s_]), reads=[b_sI[s_]], writes=[b_sO[s_]])
                    dma("sp", wbf3[:, kc, hf * HALF:(hf + 1) * HALF], sO[s_], reads=[b_sO[s_]], writes=[B["wbf_d"]])
                    i += 1
            P.barrier()

            stage_check(1)
            cqT = xv(0, 3 * S, BF16).rearrange("p (c n) -> p c n", c=3)
            ckvT = xv(12288, 2 * S, BF16).rearrange("p (c n) -> p c n", c=2)
            b_cqT = [Buf() for _ in range(NT)]
            xt = [xv(20480, D, F32), xv(24576, D, F32)]
            junk = xv(28672, D, BF16)
            xn = [xv(30720, D, BF16), xv(32768, D, BF16)]
            wg1 = xv(34816, 8 * 680, BF16).rearrange("p (c n) -> p c n", c=8)
            cqb = [xv(45696, 640, BF16), xv(46976, 640, BF16)]
            htmp = [xv(48256, D, F32), xv(52352, D, F32)]
            ang = xv(60544, NT * 16, F32).rearrange("p (t n) -> p t n", t=NT)
            angk = xv(61568, NT * 16, F32).rearrange("p (t n) -> p t n", t=NT)
            angi = xv(62592, NT * 16, I32).rearrange("p (t n) -> p t n", t=NT)
            krt = xv(56448, NT * 32, F32).rearrange("p (t n) -> p t n", t=NT)
            krs = xv(58496, NT * 32, F32).rearrange("p (t n) -> p t n", t=NT)
            krm = [xv(60544 + 1024 * k_, NT * 16, F32).rearrange("p (t n) -> p t n", t=NT) for k_ in range(4)]
            QKT = xv(20480, 4 * S, BF16).rearrange("p (a n) -> p a n", a=4)
            Vt = xv(36864, NT * 192, BF16).rearrange("p (t n) -> p t n", t=NT)
            gate2 = xv(43008, S, F32)
            wpair = xv(51200, 8 * 512, BF16).rearrange("p (c n) -> p c n", c=8)
            WB = 59392
            sq = [xv(WB, 512, F32), xv(WB + 2048, 512, F32)]
            tmpr = [xv(WB + 4096, 128, F32), xv(WB + 4608, 128, F32)]
            tmpr2 = [xv(WB + 5120, 128, F32), xv(WB + 5632, 128, F32)]
            wmg = xv(0, 8 * 2048, BF16).rearrange("p (c n) -> p c n", c=8)
            mT = xv(32768, 8 * 512, BF16).rearrange("p (c n) -> p c n", c=8)
            ta = [xv(40960, 512, F32), xv(43008, 512, F32)]
            m12 = [xv(45056, 512, F32), xv(47104, 512, F32)]
            xr = [xv(49152, D, F32), xv(53248, D, F32)]
            res = [xv(57344, D, F32), xv(61440, D, F32)]
            st6 = [sb([128, 8], F32) for _ in range(3)]
            rs6 = [sb([128, 8], F32) for _ in range(3)]
            sq.append(sb([128, 512], F32)[:])
            tmpr.append(sb([128, 128], F32)[:])
            tmpr2.append(sb([128, 128], F32)[:])
            qk = [sb([128, 384], BF16) for _ in range(3)]
            rd = ovl1[:, 0:512]
            tmpo = ovl1[:, 512:1024]
            tg = sb([128, 512], F32)
            b_rd, b_tmpo, b_tg = Buf(), Buf(), Buf()

            def bc2(ap2, n):
                return ap2.unsqueeze(2).broadcast_to([128, ap2.shape[1], n])

            def bch(ap2, h):
                return ap2.unsqueeze(1).broadcast_to([128, h, ap2.shape[1]])

            for b in range(nseq):
                _TagList.tag[0] = "s%d.p1" % b
                A_col = A_all[:, b, :]
                sh_col = sh_all[:, b, :]
                stage_check(11)
                op("dve", lambda e, b=b: e.tensor_copy(posf[:], pos_sb[:, b, :]), reads=[B["pos"]], writes=[B["posf"]])
                op("dve", lambda e: e.tensor_tensor(ang, bc2(posf[:], 16), bch(invf[:], NT), ALU.mult),
                   reads=[B["posf"], B["gains"]], writes=[B["ang"]])

                def reduce_angle(dst_key_unused=None):
                    op("dve", lambda e: e.tensor_scalar(angk, ang, 1.0 / (2.0 * math.pi), None, ALU.mult),
                       reads=[B["ang"]], writes=[B["angk"]])
                    op("dve", lambda e: e.tensor_copy(angi, angk), reads=[B["angk"]], writes=[B["angi"]])
                    op("dve", lambda e: e.tensor_copy(angk, angi), reads=[B["angi"]], writes=[B["angk"]])
                    op("dve", lambda e: e.scalar_tensor_tensor(ang, angk, -TWO_PI_HI, ang, ALU.mult, ALU.add),
                       reads=[B["angk"], B["ang"]], writes=[B["ang"]])
                    op("dve", lambda e: e.scalar_tensor_tensor(ang, angk, -TWO_PI_LO, ang, ALU.mult, ALU.add),
                       reads=[B["angk"], B["ang"]], writes=[B["ang"]])
                    op("dve", lambda e: e.tensor_scalar(ang, ang, math.pi, -math.pi, ALU.min, ALU.max),
                       reads=[B["ang"]], writes=[B["ang"]])

                reduce_angle()
                op("act", lambda e: e.activation(sint[:], ang, AF.Sin), reads=[B["ang"]], writes=[B["sint"]])
                op("dve", lambda e: e.tensor_scalar(ang, ang, 0.5 * math.pi, None, ALU.add),
                   reads=[B["ang"], B["sint"]], writes=[B["ang"]])
                reduce_angle()
                op("act", lambda e: e.activation(cost[:], ang, AF.Sin), reads=[B["ang"]], writes=[B["cost"]])
                G4 = GCS[:].rearrange("p t (k n) -> p t k n", k=4)
                for k_, tab, key in ((0, cost, "cost"), (1, cost, "cost"), (2, sint, "sint"), (3, sint, "sint")):
                    op("dve", lambda e, k_=k_, tab=tab: e.tensor_tensor(
                        G4[:, :, k_, :], tab[:], bch(g_qra_s[:, k_ * 16:(k_ + 1) * 16], NT), ALU.mult),
                       reads=[B[key], B["gains"]], writes=[B["GCS"]])

                stage_check(12)
                b_wg1 = Buf()
                dma("sp", wg1[:, :, 0:672], wbf3[:, :, 0:672], reads=[B["wbf_d"]], writes=[b_wg1])
                dma("sp", wg1[:, :, 672:680], wbf3[:, :, C_FB:C_FB + 8], reads=[B["wbf_d"]], writes=[b_wg1])

                b_xt = [Buf(), Buf()]
                b_junk = Buf()
                b_xn = [Buf(), Buf()]
                b_cqb = [Buf(), Buf()]
                b_htmp = [Buf(), Buf()]
                def p1(t, part):
                  s_ = t % 2
                  ts = slice(t * 128, (t + 1) * 128)
                  gA, gB = 2 + s_, 4 + s_
                  tb = 6 + s_
                  if part == 1:
                    dma("sp", xt[s_], x_d[b, ts, :], writes=[b_xt[s_]])
                    op("act", lambda e, s_=s_: e.activation(junk, xt[s_], AF.Square, accum_out=ssx[s_][:]),
                       reads=[b_xt[s_]], writes=[b_junk, b_ssx[s_]])
                    op("act", lambda e, s_=s_: e.activation(rsx[s_][:], ssx[s_][:], AF.Ln, bias=EPS, scale=1.0 / D),
                       reads=[b_ssx[s_]], writes=[b_rsx[s_]])
                    op("act", lambda e, s_=s_: e.activation(rsx[s_][:], rsx[s_][:], AF.Exp, scale=-0.5),
                       reads=[b_rsx[s_]], writes=[b_rsx[s_]])
                    op("dve", lambda e, s_=s_: e.tensor_scalar(xn[s_], xt[s_], rsx[s_][:], None, ALU.mult),
                       reads=[b_xt[s_], b_rsx[s_]], writes=[b_xn[s_]])
                  pb = s_
                  if part == 2:
                    for c in range(8):
                        op("pe", lambda e, c=c, s_=s_, pb=pb: e.transpose(bankbf[pb][:, c * 128:(c + 1) * 128],
                                                                          xn[s_][:, c * 128:(c + 1) * 128], ident[:]),
                           reads=[b_xn[s_], B["ident"]], writes=[bB[pb]], inc=(c == 7))
                    op("dve", lambda e, s_=s_, pb=pb: e.tensor_tensor(
                        htmp[s_].rearrange("p (c n) -> p c n", c=8), bankbf[pb].rearrange("p (c n) -> p c n", c=8),
                        bc2(A_col, 128), ALU.mult),
                       reads=[bB[pb], B["A_col"]], writes=[b_htmp[s_]])
                    op("pool", lambda e, s_=s_, ts=ts: e.tensor_tensor(
                        hT[:, :, ts], htmp[s_].rearrange("p (c n) -> p c n", c=8), bc2(sh_col, 128), ALU.add),
                       reads=[b_htmp[s_], B["cols"]], writes=[b_hT[t]])
                  if part == 3:
                    for c in range(8):
                        op("pe", lambda e, c=c, ts=ts, gA=gA: e.matmul(banks[gA][:, 0:384], hT[:, c, ts], wg1[:, c, 0:384],
                                                                       start=(c == 0), stop=(c == 7)),
                           reads=[b_hT[t], b_wg1], writes=[bB[gA]], inc=(c == 7))
                    for c in range(8):
                        op("pe", lambda e, c=c, ts=ts, gB=gB: e.matmul(banks[gB][:, 0:296], hT[:, c, ts], wg1[:, c, 384:680],
                                                                       start=(c == 0), stop=(c == 7)),
                           reads=[b_hT[t], b_wg1], writes=[bB[gB]], inc=(c == 7))

                    op("act", lambda e, gA=gA, t=t: e.activation(junk[:, 0:384], banks[gA][:, 0:384], AF.Square,
                                                                 accum_out=ssq[:, t:t + 1]),
                       reads=[bB[gA]], writes=[b_junk, B["ssq"]])
                    op("act", lambda e, gB=gB, t=t: e.activation(junk[:, 0:256], banks[gB][:, 0:256], AF.Square,
                                                                 accum_out=sskv[:, t:t + 1]),
                       reads=[bB[gB]], writes=[b_junk, B["sskv"]])
                    op("dve", lambda e, gA=gA, s_=s_: e.tensor_copy(cqb[s_][:, 0:384], banks[gA][:, 0:384]),
                       reads=[bB[gA]], writes=[b_cqb[s_]])
                    op("dve", lambda e, gB=gB, s_=s_: e.tensor_copy(cqb[s_][:, 384:640], banks[gB][:, 0:256]),
                       reads=[bB[gB]], writes=[b_cqb[s_]])
                    op("act", lambda e, gB=gB, t=t: e.copy(kfraw[:, t, :], banks[gB][:, 256:296]),
                       reads=[bB[gB]], writes=[B["kfraw"]])
                  if part == 4:
                    for c in range(5):
                        op("pe", lambda e, c=c, s_=s_, tb=tb: e.transpose(bankbf[tb][:, c * 128:(c + 1) * 128],
                                                                          cqb[s_][:, c * 128:(c + 1) * 128], ident[:]),
                           reads=[b_cqb[s_], B["ident"]], writes=[bB[tb]], inc=(c == 4))
                    op("dve", lambda e, tb=tb, ts=ts: e.tensor_copy(
                        cqT[:, :, ts], bankbf[tb][:, 0:384].rearrange("p (c n) -> p c n", c=3)),
                       reads=[bB[tb]], writes=[b_cqT[t]])
                    op("act", lambda e, tb=tb, ts=ts: e.copy(
                        ckvT[:, :, ts], bankbf[tb][:, 384:640].rearrange("p (c n) -> p c n", c=2)),
                       reads=[bB[tb]], writes=[b_cqT[t]])


                for t in range(NT + 3):
                    if t < NT:
                        p1(t, 1)
                    if 1 <= t < NT + 1:
                        p1(t - 1, 2)
                    if 2 <= t < NT + 2:
                        p1(t - 2, 3)
                    if t >= 3:
                        p1(t - 3, 4)

                stage_check(2)
                _TagList.tag[0] = "s%d.p1c" % b
                op("dve", lambda e: e.tensor_scalar(epsq[:], ssq[:], EPS / 384.0, EPS * EPS, ALU.mult, ALU.add),
                   reads=[B["ssq"]], writes=[B["epsq"]])
                op("dve", lambda e: e.tensor_scalar(epskv[:], sskv[:], EPS / 256.0, EPS * EPS, ALU.mult, ALU.add),
                   reads=[B["sskv"]], writes=[B["epskv"]])
                op("act", lambda e: e.activation(rstdkv[:], sskv[:], AF.Ln, bias=EPS, scale=1.0 / 256.0),
                   reads=[B["sskv"]], writes=[B["rstdkv"]])
                op("act", lambda e: e.activation(rstdkv[:], rstdkv[:], AF.Exp, scale=-0.5),
                   reads=[B["rstdkv"]], writes=[B["rstdkv"]])
                op("dve", lambda e: e.tensor_tensor(krs, kfraw[:, :, 0:32], kfraw[:, :, 0:32], ALU.mult),
                   reads=[B["kfraw"]], writes=[B["krs"]])
                op("dve", lambda e: e.tensor_reduce(krss[:], krs, AX.X, ALU.add), reads=[B["krs"]], writes=[B["krss"]])
                op("act", lambda e: e.activation(krss[:], krss[:], AF.Ln, bias=EPS, scale=1.0 / 32.0),
                   reads=[B["krss"]], writes=[B["krss"]])
                op("act", lambda e: e.activation(krss[:], krss[:], AF.Exp, scale=-0.5),
                   reads=[B["krss"]], writes=[B["krss"]])
                op("dve", lambda e: e.tensor_tensor(krt, kfraw[:, :, 0:32], bc2(krss[:], 32), ALU.mult),
                   reads=[B["kfraw"], B["krss"]], writes=[B["krt"]])
                op("dve", lambda e: e.tensor_tensor(krt, krt, bch(g_kra[:], NT), ALU.mult),
                   reads=[B["krt"], B["gains"]], writes=[B["krt"]])
                x1, x2 = krt[:, :, 0:16], krt[:, :, 16:32]
                op("dve", lambda e: e.tensor_tensor(krm[0], x1, cost[:], ALU.mult), reads=[B["krt"], B["cost"]], writes=[B["krm"]])
                op("dve", lambda e: e.tensor_tensor(krm[1], x2, sint[:], ALU.mult), reads=[B["krt"], B["sint"]], writes=[B["krm"]])
                op("dve", lambda e: e.tensor_tensor(krm[2], x2, cost[:], ALU.mult), reads=[B["krt"], B["cost"]], writes=[B["krm"]])
                op("dve", lambda e: e.tensor_tensor(krm[3], x1, sint[:], ALU.mult), reads=[B["krt"], B["sint"]], writes=[B["krm"]])
                op("dve", lambda e: e.tensor_tensor(krr[:, :, 0:16], krm[0], krm[1], ALU.subtract),
                   reads=[B["krm"]], writes=[B["krr"]])
                op("dve", lambda e: e.tensor_tensor(krr[:, :, 16:32], krm[2], krm[3], ALU.add),
                   reads=[B["krm"]], writes=[B["krr"]])
                op("dve", lambda e: e.tensor_tensor(spt[:], kfraw[:, :, 32:40], bch(bf_bc[:], NT), ALU.add),
                   reads=[B["kfraw"], B["gains"]], writes=[B["spt"]])
                op("act", lambda e: e.activation(spt[:], spt[:], AF.Exp, scale=-1.0), reads=[B["spt"]], writes=[B["spt"]])
                op("act", lambda e: e.activation(spt[:], spt[:], AF.Ln, bias=1.0), reads=[B["spt"]], writes=[B["spt"]])
                spt2 = spt[:].rearrange("p t h -> p (t h)")
                op("pe", lambda e: e.matmul(banks[0][:, 0:128], tri[:], spt2, start=True, stop=True),
                   reads=[B["tri"], B["spt"]], writes=[bB[0]])
                op("pe", lambda e: e.matmul(banks[1][:, 0:128], ones[:], spt2, start=True, stop=True),
                   reads=[B["ones"], B["spt"]], writes=[bB[1]])
                op("dve", lambda e: e.tensor_copy(Wf[:], banks[0][:, 0:128]), reads=[bB[0]], writes=[B["Wf"]])
                op("act", lambda e: e.copy(Wr[:], banks[1][:, 0:128]), reads=[bB[1]], writes=[B["Wr"]])
                scanA = (Wr[:].rearrange("p (t h) -> p t h", t=NT), "Wr")
                scanB = (spt[:], "spt")
                for d_ in (1, 2, 4, 8):
                    (A_, ka), (B_, kb) = scanA, scanB
                    op("dve", lambda e, A_=A_, B_=B_, d_=d_: e.tensor_copy(B_[:, 0:d_, :], A_[:, 0:d_, :]),
                       reads=[B[ka]], writes=[B[kb]])
                    op("dve", lambda e, A_=A_, B_=B_, d_=d_: e.tensor_tensor(B_[:, d_:NT, :], A_[:, d_:NT, :], A_[:, 0:NT - d_, :], ALU.add),
                       reads=[B[ka]], writes=[B[kb]])
                    scanA, scanB = scanB, scanA
                Wf3_ = Wf[:].rearrange("p (t h) -> p t h", t=NT)
                op("dve", lambda e: e.tensor_tensor(Wf3_[:, 1:NT, :], Wf3_[:, 1:NT, :], scanA[0][:, 0:NT - 1, :], ALU.add),
                   reads=[B["Wf"], B[scanA[1]]], writes=[B["Wf"]])
                Wf3 = Wf[:].rearrange("p (t h) -> p t h", t=NT)
                Wr3 = Wr[:].rearrange("p (t h) -> p t h", t=NT)
                op("dve", lambda e: e.tensor_copy(Ws[:, :, :, 0], Wf3), reads=[B["Wf"]], writes=[B["Ws"]])
                op("dve", lambda e: e.tensor_tensor(Wr3, Wf3, Ws[:, :, :, 0], ALU.subtract),
                   reads=[B["Wf"], B["Ws"]], writes=[B["Wr"]])
                op("dve", lambda e: e.tensor_copy(Ws[:, :, :, 1], Wr3), reads=[B["Wr"]], writes=[B["Ws"]])
                op("dve", lambda e: e.tensor_tensor(Wr3, Wr3, Ws[:, :, :, 1], ALU.subtract),
                   reads=[B["Wr"], B["Ws"]], writes=[B["Wr"]])
                op("dve", lambda e: e.tensor_copy(Ws[:, :, :, 2], Wr3), reads=[B["Wr"]], writes=[B["Ws"]])
                op("dve", lambda e: e.tensor_scalar(nWs[:], Ws[:], -1.0, None, ALU.mult), reads=[B["Ws"]], writes=[B["nWs"]])
                P.barrier()

                stage_check(3)
                op("pool", lambda e: e.memset(Vt[:, :, 64:128], 2.0), writes=[B["vtwos"]])
                op("pool", lambda e: e.memset(QKT[96:128, :, :], 0.0), writes=[B["qkz"]])
                b_wp = Buf()

                def load_wpair(jb):
                    hp_ = jb % 4
                    if jb < 4:
                        dma("sp", wpair[:, :, 0:128], wbf3[:, :, C_GA + hp_ * 128:C_GA + (hp_ + 1) * 128],
                            reads=[B["wbf_d"]], writes=[b_wp])
                    else:
                        for k_, c0 in enumerate((C_QB, C_KB, C_VB, C_GB)):
                            dma("sp", wpair[:, :, k_ * 128:(k_ + 1) * 128], wbf3[:, :, c0 + hp_ * 128:c0 + (hp_ + 1) * 128],
                                reads=[B["wbf_d"]], writes=[b_wp])

                for job in range(8):
                    _TagList.tag[0] = "s%d.j%d.proj" % (b, job)
                    mla = job < 4
                    hp = job % 4
                    Kd = 96 if mla else 70
                    ychunk = hp if mla else 4 + hp
                    if job == 0:
                        load_wpair(0)
                    b_QKT = [Buf() for _ in range(NT)]
                    if job == 4:
                        op("pool", lambda e: e.memset(QKT[64:96, :, :], 0.0), writes=b_QKT + [B["qkz"]])
                    b_Vt = [Buf() for _ in range(NT)]
                    b_sq = [Buf(), Buf(), Buf()]
                    b_st = [Buf(), Buf(), Buf()]
                    b_rs = [Buf(), Buf(), Buf()]
                    b_tn = [Buf(), Buf(), Buf()]
                    b_tr = [Buf(), Buf(), Buf()]
                    b_rm = [Buf(), Buf(), Buf()]
                    b_qc = [Buf(), Buf(), Buf()]
                    b_kc = [Buf(), Buf(), Buf()]
                    b_g2 = [Buf() for _ in range(4)]
                    if not mla:
                        for s_ in range(3):
                            qk4 = qk[s_][:, 0:280].rearrange("p (a n) -> p a n", a=4)
                            op("pool", lambda e, qk4=qk4: e.memset(qk4[:, 0:2, 67:70], 1.0), writes=[b_qc[s_]])
                            op("pool", lambda e, qk4=qk4: e.memset(qk4[:, 2:4, 64:67], 1.0), writes=[b_qc[s_]])

                    def tile(t, part):
                        s_ = t % 3
                        ts = slice(t * 128, (t + 1) * 128)
                        pp = s_
                        tb = 6 + (t % 2)
                        vdst = Vt[:, t, :].rearrange("p (a n) -> p a n", a=3)[:, 0:3:2, :]
                        if mla:
                            pq = banks[pp][:, 0:256].rearrange("p (h n) -> p h n", h=2)
                            pkv = banks[pp][:, 256:512].rearrange("p (h n) -> p h n", h=2)
                            qk4 = qk[s_][:, 0:384].rearrange("p (a n) -> p a n", a=4)
                            rsq = rs6[s_][:, 0:4].rearrange("p (h k) -> p h k", k=2)
                            rsk = rs6[s_][:, 4:8].rearrange("p (h k) -> p h k", k=2)
                            tr3 = tmpr[s_].rearrange("p (h n) -> p h n", h=2)
                            tr23 = tmpr2[s_].rearrange("p (h n) -> p h n", h=2)
                            if part == 0:
                                for c in range(3):
                                    op("pe", lambda e, c=c: e.matmul(
                                        banks[pp][:, 0:256], cqT[:, c, ts], wuq[:, c, hp * 256:(hp + 1) * 256],
                                        start=(c == 0), stop=(c == 2)),
                                       reads=[b_cqT[t], B["wuq"]], writes=[bB[pp]], inc=False)
                                for c in range(2):
                                    op("pe", lambda e, c=c: e.matmul(
                                        banks[pp][:, 256:512], ckvT[:, c, ts], wukv[:, c, hp * 256:(hp + 1) * 256],
                                        start=(c == 0), stop=(c == 1)),
                                       reads=[b_cqT[t], B["wukv"]], writes=[bB[pp]], inc=(c == 1))
                                op("act", lambda e: e.activation(sq[s_], banks[pp][:, :], AF.Square),
                                   reads=[bB[pp]], writes=[b_sq[s_]])
                                op("dve", lambda e: e.tensor_reduce(st6[s_][:, 0:8], sq[s_].rearrange("p (a n) -> p a n", a=8),
                                                                    AX.X, ALU.add),
                                   reads=[b_sq[s_]], writes=[b_st[s_]])
                                op("act", lambda e: e.activation(rs6[s_][:, 0:4], st6[s_][:, 0:4], AF.Ln,
                                                                 bias=epsq[:, t:t + 1], scale=1.0 / 64.0),
                                   reads=[b_st[s_], B["epsq"]], writes=[b_rs[s_]])
                                op("act", lambda e: e.activation(rs6[s_][:, 4:8], st6[s_][:, 4:8], AF.Ln,
                                                                 bias=epskv[:, t:t + 1], scale=1.0 / 64.0),
                                   reads=[b_st[s_], B["epskv"]], writes=[b_rs[s_]])
                                op("act", lambda e: e.activation(rs6[s_][:, 0:8], rs6[s_][:, 0:8], AF.Exp, scale=-0.5),
                                   reads=[b_rs[s_]], writes=[b_rs[s_]])
                            elif part == 1:
                                op("dve", lambda e: e.tensor_tensor(qk4[:, 0:2, 0:64], pq[:, :, 0:64],
                                                                    rsq[:, :, 0:1].broadcast_to([128, 2, 64]), ALU.mult),
                                   reads=[bB[pp], b_rs[s_]], writes=[b_qc[s_]])
                                op("dve", lambda e: e.tensor_tensor(qk4[:, 2:4, 0:64], pkv[:, :, 0:64],
                                                                    rsk[:, :, 0:1].broadcast_to([128, 2, 64]), ALU.mult),
                                   reads=[bB[pp], b_rs[s_]], writes=[b_qc[s_]])
                                op("dve", lambda e: e.tensor_tensor(tr3, pq[:, :, 64:128],
                                                                    rsq[:, :, 1:2].broadcast_to([128, 2, 64]), ALU.mult),
                                   reads=[bB[pp], b_rs[s_]], writes=[b_tr[s_]])
                                op("pool", lambda e: e.tensor_tensor(tr23, tr3, bch(GCS[:, t, :], 2), ALU.mult),
                                   reads=[b_tr[s_], B["GCS"]], writes=[b_rm[s_]])
                                op("pool", lambda e: e.tensor_tensor(qk4[:, 0:2, 64:96], tr23[:, :, 0:32], tr23[:, :, 32:64], ALU.add),
                                   reads=[b_rm[s_]], writes=[b_qc[s_]])
                                op("pool", lambda e: e.tensor_copy(qk4[:, 2:4, 64:96], bch(krr[:, t, :], 2)),
                                   reads=[B["krr"]], writes=[b_qc[s_]])
                                op("act", lambda e: e.activation(
                                    vdst, pkv[:, :, 64:128], AF.Identity,
                                    scale=rstdkv[:, t:t + 1]),
                                   reads=[bB[pp], B["rstdkv"]], writes=[b_Vt[t]])
                            gc = 0
                        else:
                            p4 = banks[pp][:, 0:256].rearrange("p (a n) -> p a n", a=4)
                            qk4 = qk[s_][:, 0:280].rearrange("p (a n) -> p a n", a=4)
                            if part == 0:
                                for c in range(8):
                                    op("pe", lambda e, c=c: e.matmul(
                                        banks[pp][:, 0:384], hT[:, c, ts], wpair[:, c, 0:384], start=(c == 0), stop=(c == 7)),
                                       reads=[b_hT[t], b_wp], writes=[bB[pp]], inc=(c == 7))
                                op("act", lambda e: e.activation(sq[s_][:, 0:256], banks[pp][:, 0:256], AF.Square),
                                   reads=[bB[pp]], writes=[b_sq[s_]])
                                op("dve", lambda e: e.tensor_reduce(st6[s_][:, 0:4],
                                                                    sq[s_][:, 0:256].rearrange("p (a n) -> p a n", a=4), AX.X, ALU.add),
                                   reads=[b_sq[s_]], writes=[b_st[s_]])
                                op("act", lambda e: e.activation(rs6[s_][:, 0:4], st6[s_][:, 0:4], AF.Ln, bias=EPS,
                                                                 scale=1.0 / 64.0),
                                   reads=[b_st[s_]], writes=[b_rs[s_]])
                                op("act", lambda e: e.activation(rs6[s_][:, 0:4], rs6[s_][:, 0:4], AF.Exp, scale=-0.5),
                                   reads=[b_rs[s_]], writes=[b_rs[s_]])
                            elif part == 1:
                                op("dve", lambda e: e.tensor_tensor(qk4[:, :, 0:64], p4, bc2(rs6[s_][:, 0:4], 64), ALU.mult),
                                   reads=[bB[pp], b_rs[s_]], writes=[b_qc[s_]])
                                op("pool", lambda e: e.tensor_copy(qk4[:, 0:2, 64:67], nWs[:, t, 2 * hp:2 * hp + 2, :]),
                                   reads=[B["nWs"]], writes=[b_qc[s_]])
                                op("pool", lambda e: e.tensor_copy(qk4[:, 2:4, 67:70], Ws[:, t, 2 * hp:2 * hp + 2, :]),
                                   reads=[B["Ws"]], writes=[b_qc[s_]])
                                op("act", lambda e: e.copy(vdst, banks[pp][:, 256:384].rearrange("p (h n) -> p h n", h=2)),
                                   reads=[bB[pp]], writes=[b_Vt[t]])
                            gc = 2
                        if part == 2:
                            for a in range(4):
                                op("pe", lambda e, a=a: e.transpose(bankbf[tb][0:Kd, a * 128:(a + 1) * 128], qk4[:, a, :], ident[:]),
                                   reads=[b_qc[s_], B["ident"]], writes=[bB[tb]], inc=(a == 3))
                            op("dve", lambda e: e.tensor_scalar(
                                QKT[0:Kd, 0:2, ts], bankbf[tb][0:Kd, 0:256].rearrange("p (a n) -> p a n", a=2),
                                gcols[0:Kd, gc:gc + 1], None, ALU.mult),
                               reads=[bB[tb], B["gcols"]], writes=[b_QKT[t]])
                            op("dve", lambda e: e.tensor_scalar(
                                QKT[0:Kd, 2:4, ts], bankbf[tb][0:Kd, 256:512].rearrange("p (a n) -> p a n", a=2),
                                gcols[0:Kd, gc + 1:gc + 2], None, ALU.mult),
                               reads=[bB[tb], B["gcols"]], writes=[b_QKT[t]])

                    def proj_step(k):
                        if k < NT:
                            tile(k, 0)
                        if 1 <= k < NT + 1:
                            tile(k - 1, 1)
                        if 2 <= k < NT + 2:
                            tile(k - 2, 2)

                    gc0 = 0 if mla else 384

                    def gate(g):
                        gs = slice(g * 512, (g + 1) * 512)
                        gbk = 4 + (g % 2)
                        for c in range(8):
                            op("pe", lambda e, c=c: e.matmul(
                                banks[gbk][:, :], wpair[:, c, gc0:gc0 + 128], hT[:, c, gs], start=(c == 0), stop=(c == 7)),
                               reads=[b_wp] + b_hT[4 * g:4 * g + 4], writes=[bB[gbk]], inc=(c == 7))
                        op("act", lambda e: e.activation(tg[:], banks[gbk][:, :], AF.Tanh, scale=0.5),
                           reads=[bB[gbk]], writes=[b_tg])
                        op("dve", lambda e: e.scalar_tensor_tensor(
                            gate2[:, gs], tg[:], 1.0, banks[gbk][:, :], ALU.add, ALU.mult),
                           reads=[b_tg, bB[gbk]], writes=[b_g2[g]])

                    maskb = mmaskb if mla else fmaskb

                    def issue_s(g, j):
                        N = 512 if j < 4 * g else 512 - (j - 4 * g) * 128
                        qc0 = g * 512 + 512 - N
                        sb0 = 2 * (j % 2)
                        for hh in range(2):
                            kT = QKT[:, 2 + hh, j * 128:(j + 1) * 128]
                            rd_ = [b_QKT[j], B["qkz"]] + b_QKT[qc0 // 128:4 * g + 4]
                            if j >= 4 * g:
                                op("pe", lambda e, hh=hh: e.matmul(banks[sb0 + hh][:, 0:128], ident[:], maskb[:],
                                                                   start=True, stop=False),
                                   reads=[B["ident"], B["maskb"]], writes=[bB[sb0 + hh]], inc=False)
                                op("pe", lambda e, hh=hh, kT=kT: e.matmul(banks[sb0 + hh][:, 0:128], kT, QKT[:, hh, qc0:qc0 + 128],
                                                                          start=False, stop=True),
                                   reads=rd_, writes=[bB[sb0 + hh]], inc=(hh == 1 and N == 128))
                                if N > 128:
                                    op("pe", lambda e, hh=hh, kT=kT: e.matmul(banks[sb0 + hh][:, 128:N], kT,
                                                                              QKT[:, hh, qc0 + 128:qc0 + N], start=True, stop=True),
                                       reads=rd_, writes=[bB[sb0 + hh]], inc=(hh == 1))
                            else:
                                op("pe", lambda e, hh=hh, kT=kT: e.matmul(banks[sb0 + hh][:, 0:N], kT,
                                                                          QKT[:, hh, qc0:qc0 + N], start=True, stop=True),
                                   reads=rd_, writes=[bB[sb0 + hh]], inc=(hh == 1))
                        s2 = psall[:, sb0 * 512:(sb0 + 2) * 512].rearrange("p (h n) -> p h n", h=2)
                        ps_ = j % 2
                        op("act", lambda e: e.activation(pt[ps_][:, :, 0:N], s2[:, :, 0:N], AF.Exp),
                           reads=[bB[sb0], bB[sb0 + 1]], writes=[b_pt[ps_]])

                    def issue_pv(g, j):
                        N = 512 if j < 4 * g else 512 - (j - 4 * g) * 128
                        ps_ = j % 2
                        for hh in range(2):
                            op("pe", lambda e, hh=hh: e.matmul(banks[4 + hh][:, 512 - N:512], Vt[:, j, hh * 64:hh * 64 + 128],
                                                               pt[ps_][:, hh, 0:N], start=(j == 0), stop=(j == 4 * g + 3)),
                               reads=[b_Vt[j], b_pt[ps_], B["vtwos"]], writes=[bB[4 + hh]], inc=(hh == 1))

                    def group_end_a(g):
                        lo, hi = slice(0, 64), slice(64, 128)
                        op("dve", lambda e: e.tensor_copy(tmpo[lo, :], banks[4][lo, :]), reads=[bB[4]], writes=[b_tmpo])
                        op("dve", lambda e: e.tensor_copy(rd[lo, :], banks[4][hi, :]), reads=[bB[4]], writes=[b_rd])
                        op("dve", lambda e: e.tensor_copy(tmpo[hi, :], banks[5][hi, :]), reads=[bB[5]], writes=[b_tmpo])
                        op("dve", lambda e: e.tensor_copy(rd[hi, :], banks[5][lo, :]), reads=[bB[5]], writes=[b_rd])

                    def group_end_b(g):
                        gs = slice(g * 512, (g + 1) * 512)
                        op("act", lambda e: e.activation(rd, rd, AF.Ln), reads=[b_rd], writes=[b_rd])
                        op("act", lambda e: e.activation(rd, rd, AF.Exp, scale=-1.0), reads=[b_rd], writes=[b_rd])
                        op("dve", lambda e: e.tensor_tensor(tmpo, tmpo, rd, ALU.mult),
                           reads=[b_tmpo, b_rd], writes=[b_tmpo])
                        op("pool", lambda e: e.tensor_tensor(yT[:, ychunk, gs], tmpo, gate2[:, gs], ALU.mult),
                           reads=[b_tmpo, b_g2[g]], writes=[b_yT[ychunk][g]])

                    for k in range(NT + 2):
                        proj_step(k)
                    _TagList.tag[0] = "s%d.j%d.gate" % (b, job)
                    for g in range(4):
                        gate(g)
                    if job < 7:
                        load_wpair(job + 1)
                    if job == 0:
                        stage_check(4)
                    _TagList.tag[0] = "s%d.j%d.attn" % (b, job)
                    for g in range(4):
                        nst = 4 * g + 4
                        for i in range(nst + 1):
                            if i < nst:
                                issue_s(g, i)
                            if i >= 1:
                                issue_pv(g, i - 1)
                            if i == 2 and g > 0:
                                group_end_b(g - 1)
                        group_end_a(g)
                    group_end_b(3)
                    P.barrier()
                    if job == 0:
                        stage_check(5)
                    if job == 4:
                        stage_check(6)

                stage_check(7)
                _TagList.tag[0] = "s%d.p4" % b
                b_wmg = [Buf() for _ in range(8)]
                dma("sp", gate_bc[:], ada_d[b, 2 * D:3 * D].partition_broadcast(128), reads=[B["ada_d"]],
                    writes=[B["gate_bc"]])
                op("dve", lambda e: e.tensor_scalar(gate_bc[:], gate_bc[:], 0.5, None, ALU.mult),
                   reads=[B["gate_bc"]], writes=[B["gate_bc"]])
                for c in range(8):
                    dma("sp", wmg[:, c, :], wbf3[:, c, C_MA:C_MA + 2048], reads=[B["wbf_d"]], writes=[b_wmg[c]])
                b_mT = [Buf() for _ in range(8)]
                b_ta = [Buf(), Buf()]
                b_m12 = [Buf(), Buf()]
                b_xr = [Buf(), Buf()]
                b_res = [Buf(), Buf()]
                k4 = 0
                for g in range(4):
                    gs = slice(g * 512, (g + 1) * 512)
                    for mc in range(8):
                        bs = (k4 % 2) * 4
                        k4 += 1
                        ms = slice(mc * 128, (mc + 1) * 128)
                        for c in range(4):
                            op("pe", lambda e, c=c, bs=bs, ms=ms, gs=gs: e.matmul(banks[bs][:, :], wba[:, c, ms], yT[:, c, gs],
                                                                                  start=(c == 0), stop=(c == 3)),
                               reads=[B["wba"], b_yT[c][g]], writes=[bB[bs]], inc=(c == 3))
                        for c in range(4):
                            op("pe", lambda e, c=c, bs=bs, ms=ms, gs=gs: e.matmul(banks[bs + 1][:, :], wbb[:, c, ms], yT[:, 4 + c, gs],
                                                                                  start=(c == 0), stop=(c == 3)),
                               reads=[B["wbb"], b_yT[4 + c][g]], writes=[bB[bs + 1]], inc=(c == 3))
                        for c in range(8):
                            op("pe", lambda e, c=c, bs=bs, ms=ms, gs=gs: e.matmul(banks[bs + 2][:, :], wmg[:, c, ms], hT[:, c, gs],
                                                                                  start=(c == 0), stop=(c == 7)),
                               reads=[b_wmg[c]] + b_hT[4 * g:4 * g + 4], writes=[bB[bs + 2]], inc=(c == 7))
                        for c in range(8):
                            op("pe", lambda e, c=c, bs=bs, mc=mc, gs=gs: e.matmul(
                                banks[bs + 3][:, :], wmg[:, c, 1024 + mc * 128:1024 + (mc + 1) * 128], hT[:, c, gs],
                                start=(c == 0), stop=(c == 7)),
                               reads=[b_wmg[c]] + b_hT[4 * g:4 * g + 4], writes=[bB[bs + 3]], inc=(c == 7))
                        op("act", lambda e, bs=bs: e.activation(ta[0], banks[bs + 2][:, :], AF.Tanh, scale=0.5),
                           reads=[bB[bs + 2]], writes=[b_ta[0]])
                        op("act", lambda e, bs=bs: e.activation(ta[1], banks[bs + 3][:, :], AF.Tanh, scale=0.5),
                           reads=[bB[bs + 3]], writes=[b_ta[1]])
                        op("dve", lambda e, bs=bs: e.scalar_tensor_tensor(m12[0], ta[0], 1.0, banks[bs][:, :], ALU.add, ALU.mult),
                           reads=[b_ta[0], bB[bs]], writes=[b_m12[0]])
                        op("dve", lambda e, bs=bs: e.scalar_tensor_tensor(m12[1], ta[1], 1.0, banks[bs + 1][:, :], ALU.add, ALU.mult),
                           reads=[b_ta[1], bB[bs + 1]], writes=[b_m12[1]])
                        op("pool", lambda e, mc=mc: e.tensor_tensor(mT[:, mc, :], m12[0], m12[1], ALU.add),
                           reads=[b_m12[0], b_m12[1]], writes=[b_mT[mc]])
                    for tt in range(4):
                        t = 4 * g + tt
                        s_ = t % 2
                        ts = slice(t * 128, (t + 1) * 128)
                        bs = (k4 % 2) * 4
                        k4 += 1
                        dma("sp", xr[s_], x_d[b, ts, :], writes=[b_xr[s_]])
                        for hf in range(2):
                            for c in range(8):
                                op("pe", lambda e, c=c, bs=bs, hf=hf, tt=tt: e.matmul(
                                    banks[bs + hf][:, :], mT[:, c, tt * 128:(tt + 1) * 128], wout[:, c, hf * 512:(hf + 1) * 512],
                                    start=(c == 0), stop=(c == 7)),
                                   reads=[b_mT[c], B["wout"]], writes=[bB[bs + hf]], inc=(c == 7))
                        for hf in range(2):
                            hs = slice(hf * 512, (hf + 1) * 512)
                            op("dve", lambda e, bs=bs, hf=hf, hs=hs, s_=s_: e.tensor_tensor(
                                res[s_][:, hs], banks[bs + hf][:, :], gate_bc[:, hs], ALU.mult),
                               reads=[bB[bs + hf], B["gate_bc"]], writes=[b_res[s_]])
                        op("pool", lambda e, s_=s_: e.tensor_tensor(res[s_], res[s_], xr[s_], ALU.add),
                           reads=[b_res[s_], b_xr[s_]], writes=[b_res[s_]])
                        dma("sp", out_d[b, ts, :], res[s_], reads=[b_res[s_]])
                P.barrier()

        except _Stop:
            pass
        P.finish()
        _DBG["sbuf_remaining"] = nc.sbuf_bytes_remaining
        _DBG["ops"] = {n: len(e.ops) for n, e in P.E.items()}
        _DBG["tags"] = {n: list(e.ops.tags) for n, e in P.E.items()}
        P.emit(st)
    return nc


_NC_CACHE = {}


def _consts():
    ident = np.eye(128, dtype=np.float32)
    tri = np.triu(np.ones((128, 128), np.float32))
    kk = np.arange(128)[:, None]
    qq = np.arange(128)[None, :]
    fmask = np.where(kk <= qq, 0.0, NEG).astype(np.float32)
    mmask = np.where((kk // 64) <= (qq // 64), 0.0, NEG).astype(np.float32)
    invf = (np.float32(10000.0) ** (-(np.arange(0, 32, 2, dtype=np.float32)) / np.float32(32))).astype(np.float32)
    return ident, tri, fmask, mmask, invf.reshape(1, 16)


def kernel(x, c, positions, w_ada, b_ada, norm_w, w_in, b_f, q_lora_norm_w, kv_lora_norm_w, w_uq, w_ukv,
           qn_nope_a, qn_rope_a, kn_nope_a, kn_rope_a, qn_b, kn_b, w_branch_a, w_branch_b, w_out):
    f = lambda a: np.ascontiguousarray(np.asarray(a, dtype=np.float32))
    x = f(x)
    c = f(c)
    positions = np.ascontiguousarray(np.asarray(positions, dtype=np.int32))
    ident, tri, fmask, mmask, invf = _consts()
    shared = {
        "w_ada": f(w_ada)[0], "b_ada": f(b_ada)[0].reshape(1, -1),
        "normw": np.ascontiguousarray(f(norm_w)[0].reshape(8, 128).T),
        "w_in": f(w_in)[0], "b_f": f(b_f)[0].reshape(1, 8),
        "qlw": np.ascontiguousarray(f(q_lora_norm_w)[0].reshape(3, 128).T),
        "kvlw": np.ascontiguousarray(f(kv_lora_norm_w)[0].reshape(2, 128).T),
        "w_uq": f(w_uq)[0], "w_ukv": f(w_ukv)[0],
        "g_qna": f(qn_nope_a)[0].reshape(1, -1), "g_qra": f(qn_rope_a)[0].reshape(1, -1),
        "g_kna": f(kn_nope_a)[0].reshape(1, -1), "g_kra": f(kn_rope_a)[0].reshape(1, -1),
        "g_qb": f(qn_b)[0].reshape(1, -1), "g_kb": f(kn_b)[0].reshape(1, -1),
        "w_ba": f(w_branch_a)[0], "w_bb": f(w_branch_b)[0], "w_out": f(w_out)[0],
        "ident": ident, "tri": tri, "fmask": fmask, "mmask": mmask, "invf": invf,
    }
    gcols = np.ones((128, 4), np.float32)
    gcols[0:64, 0] = f(qn_nope_a)[0]
    gcols[0:64, 1] = f(kn_nope_a)[0]
    gcols[0:64, 2] = f(qn_b)[0]
    gcols[0:64, 3] = f(kn_b)[0]
    shared["gcols"] = gcols
    in_maps = []
    for i in range(NCORES):
        bs = slice(i * BPC, (i + 1) * BPC)
        m = dict(shared)
        m["x"] = x[bs]
        m["cT"] = np.ascontiguousarray(c[bs].reshape(BPC, 8, 128).transpose(2, 1, 0))
        m["pos"] = np.ascontiguousarray(positions[bs].reshape(BPC, NT, 128).transpose(2, 0, 1))
        in_maps.append(m)
    if "nc" not in _NC_CACHE:
        _NC_CACHE["nc"] = build_nc()
    res = run_bass_kernel_spmd(_NC_CACHE["nc"], in_maps, core_ids=list(range(NCORES)))
    return np.concatenate([r["out"] for r in res.results], axis=0).astype(np.float32)
```

```python
import math
from contextlib import ExitStack

import numpy as np
import concourse.bass as bass
import concourse.mybir as mybir
from concourse.bass_utils import run_bass_kernel_spmd

F32 = mybir.dt.float32
BF16 = mybir.dt.bfloat16
I32 = mybir.dt.int32
AF = mybir.ActivationFunctionType
ALU = mybir.AluOpType
AX = mybir.AxisListType

NCORES = 8
BPC = 4
S = 2048
D = 1024
NT = 16
D_IN = 5288
EPS = 1e-6
C_CQ, C_CKV, C_KR, C_GA, C_QB, C_KB, C_VB, C_FB, C_GB, C_MA, C_MB = (
    0, 384, 640, 672, 1184, 1696, 2208, 2720, 2728, 3240, 4264)
TWO_PI_HI = 6.28125
TWO_PI_LO = 2.0 * math.pi - 6.28125
NEG = -30000.0


class Buf:
    __slots__ = ("w", "r", "excl")

    def __init__(self, excl=False):
        self.w = None
        self.r = {}
        self.excl = excl


class _TagList(list):
    tag = [""]

    def append(self, x):
        list.append(self, x)
        self.tags.append(_TagList.tag[0])


class _Eng:
    def __init__(self, name, semkey):
        self.name = name
        self.semkey = semkey
        self.count = 0
        self.waited = {}
        self.ops = _TagList()
        self.ops.tags = []
        self.dma_n = 0
        self.dma_vals = {}


class _Rec:
    def __init__(self):
        self.calls = []

    def __getattr__(self, name):
        def f(*a, **k):
            self.calls.append((name, a, k))
        return f


class Prog:
    ENGS = ("pe", "act", "dve", "pool", "sp")
    NDMA = {"sp": 8, "act": 2, "pool": 2}

    def __init__(self, nc):
        self.nc = nc
        self.E = {n: _Eng(n, ("eng", n)) for n in self.ENGS}
        self.semkeys = [e.semkey for e in self.E.values()]
        for q, k in self.NDMA.items():
            for i in range(k):
                self.semkeys.append(("dma", q, i))
                self.E[q].dma_vals[i] = 0

    def _wait(self, E, k, v):
        if E.waited.get(k, 0) < v:
            E.ops.append(("wait", k, v))
            E.waited[k] = v

    def _need(self, E, reads, writes):
        need = {}
        for b in reads:
            if b.w is not None and need.get(b.w[0], 0) < b.w[1]:
                need[b.w[0]] = b.w[1]
        for b in writes:
            if b.w is not None and need.get(b.w[0], 0) < b.w[1]:
                need[b.w[0]] = b.w[1]
            for k, v in b.r.items():
                if need.get(k, 0) < v:
                    need[k] = v
        for k, v in need.items():
            if k == E.semkey:
                if E.name == "pe" or v > E.count:
                    continue
            self._wait(E, k, v)

    def _mark(self, tok, reads, writes):
        for b in writes:
            b.w = tok
            b.r = {}
        for b in reads:
            if b.r.get(tok[0], 0) < tok[1]:
                b.r[tok[0]] = tok[1]

    def op(self, eng, fn, reads=(), writes=(), inc=True):
        E = self.E[eng]
        if any(b.excl for b in reads):
            writes = list(writes) + [b for b in reads if b.excl]
            reads = [b for b in reads if not b.excl]
        self._need(E, reads, writes)
        tok = (E.semkey, E.count + 1)
        rec = _Rec()
        fn(rec)
        E.ops.append(("inst", rec.calls[0], inc))
        if inc:
            E.count += 1
        self._mark(tok, reads, writes)

    def dma(self, q, out, in_, reads=(), writes=(), **kw):
        E = self.E[q]
        slot = E.dma_n % self.NDMA[q]
        E.dma_n += 1
        k = ("dma", q, slot)
        prev = E.dma_vals[slot]
        if prev > 0:
            self._wait(E, k, prev)
        self._need(E, reads, writes)
        E.dma_vals[slot] = prev + 16
        E.ops.append(("dma", out, in_, k, kw))
        self._mark((k, prev + 16), reads, writes)

    def barrier(self):
        toks = []
        for n in ("pe", "act", "dve", "pool"):
            e = self.E[n]
            if e.count > 0:
                toks.append((e.semkey, e.count))
        for q in self.NDMA:
            for slot, v in self.E[q].dma_vals.items():
                if v > 0:
                    toks.append((("dma", q, slot), v))
        for n in self.ENGS:
            E = self.E[n]
            for k, v in toks:
                if k != E.semkey:
                    self._wait(E, k, v)

    def finish(self):
        self.barrier()

    def emit(self, stack):
        nc = self.nc
        sems = {}
        for k in self.semkeys:
            sems[k] = stack.enter_context(nc.semaphore("s_" + "_".join(str(x) for x in k)))
        block = stack.enter_context(nc.Block())

        def run(E):
            def body(e):
                own = sems[E.semkey]
                for o in E.ops:
                    if o[0] == "wait":
                        e.wait_ge(sems[o[1]], o[2])
                    elif o[0] == "inst":
                        name, a, k = o[1]
                        ins = getattr(e, name)(*a, **k)
                        if o[2]:
                            ins.then_inc(own, 1)
                    else:
                        e.dma_start(out=o[1], in_=o[2], **o[4]).then_inc(sems[o[3]], 16)
            return body

        block.tensor(run(self.E["pe"]))
        block.scalar(run(self.E["act"]))
        block.vector(run(self.E["dve"]))
        block.gpsimd(run(self.E["pool"]))
        block.sync(run(self.E["sp"]))


_DBG = {}


class _Stop(Exception):
    pass


def build_nc(nseq=BPC, stage=99):
    def stage_check(n):
        if stage == n:
            raise _Stop()

    nc = bass.Bass("TRN2", target_bir_lowering=False)
    din = lambda n, s, dt=F32: nc.dram_tensor(n, list(s), dt, kind="ExternalInput").ap()
    x_d = din("x", [BPC, S, D])
    cT_d = din("cT", [128, 8, BPC])
    pos_d = din("pos", [128, BPC, NT], I32)
    wada_d = din("w_ada", [D, 3 * D])
    bada_d = din("b_ada", [1, 3 * D])
    normw_d = din("normw", [128, 8])
    win_d = din("w_in", [D, D_IN])
    bf_d = din("b_f", [1, 8])
    qlw_d = din("qlw", [128, 3])
    kvlw_d = din("kvlw", [128, 2])
    wuq_d = din("w_uq", [384, 768])
    wukv_d = din("w_ukv", [256, 1024])
    gqna_d = din("g_qna", [1, 64])
    gqra_d = din("g_qra", [1, 32])
    gkna_d = din("g_kna", [1, 64])
    gkra_d = din("g_kra", [1, 32])
    gqb_d = din("g_qb", [1, 64])
    gkb_d = din("g_kb", [1, 64])
    wba_d = din("w_ba", [512, D])
    wbb_d = din("w_bb", [512, D])
    wout_d = din("w_out", [D, D])
    ident_d = din("ident", [128, 128])
    tri_d = din("tri", [128, 128])
    fmask_d = din("fmask", [128, 128])
    mmask_d = din("mmask", [128, 128])
    invf_d = din("invf", [1, 16])
    gcols_d = din("gcols", [128, 4])
    out_d = nc.dram_tensor("out", [BPC, S, D], F32, kind="ExternalOutput").ap()
    wbf_d = nc.dram_tensor("wbf_scr", [128, 8 * D_IN], BF16).ap()
    wbf3 = wbf_d.rearrange("p (c n) -> p c n", c=8)
    ada_d = nc.dram_tensor("ada_scr", [BPC, 3 * D], F32).ap()

    P = Prog(nc)
    op, dma = P.op, P.dma
    with ExitStack() as st:
        try:
            _n = [0]

            def sb(shape, dt):
                _n[0] += 1
                return st.enter_context(nc.sbuf_tensor("sb%d" % _n[0], list(shape), dt))

            ident = sb([128, 128], BF16)
            identf = sb([128, 128], F32)
            tri = sb([128, 128], F32)
            ones = sb([128, 128], F32)
            negones = sb([128, 128], F32)
            twos = sb([128, 128], BF16)
            fmask = sb([128, 128], F32)
            mmask = sb([128, 128], F32)
            fmaskb = sb([128, 128], BF16)
            mmaskb = sb([128, 128], BF16)
            g_qna = sb([128, 64], F32)
            g_qra = sb([128, 32], F32)
            g_kna = sb([128, 64], F32)
            g_kra = sb([128, 32], F32)
            g_qb = sb([128, 64], F32)
            g_kb = sb([128, 64], F32)
            bf_bc = sb([128, 8], F32)
            invf = sb([128, 16], F32)
            cT = sb([128, 8, BPC], F32)
            normw = sb([128, 8], F32)
            qlw = sb([128, 3], F32)
            kvlw = sb([128, 2], F32)
            pos_sb = sb([128, BPC, NT], I32)
            wuq = sb([128, 3, 1024], BF16)
            wukv = sb([128, 2, 1024], BF16)
            wba = sb([128, 4, D], BF16)
            wbb = sb([128, 4, D], BF16)
            wout = sb([128, 8, D], BF16)
            hT = sb([128, 8, S], BF16)
            yT = sb([128, 8, S], BF16)
            ovl1 = sb([128, D], F32)
            gate_bc = ovl1
            sc_col = sb([128, 8], F32)
            sh_col = sb([128, 8], F32)
            A_col = sb([128, 8], F32)
            posf = sb([128, NT], F32)
            GCS = sb([128, NT, 64], F32)
            gcols = sb([128, 4], F32)
            g_qra_s = sb([128, 64], F32)
            sint = sb([128, NT, 16], F32)
            cost = sb([128, NT, 16], F32)
            ssq = sb([128, NT], F32)
            sskv = sb([128, NT], F32)
            epsq = sb([128, NT], F32)
            epskv = sb([128, NT], F32)
            rstdkv = sb([128, NT], F32)
            kfraw = sb([128, NT, 40], F32)
            krss = sb([128, NT], F32)
            krr = sb([128, NT, 32], BF16)
            spt = sb([128, NT, 8], F32)
            Wf = sb([128, NT * 8], F32)
            Wr = sb([128, NT * 8], F32)
            Ws = sb([128, NT, 8, 3], BF16)
            nWs = sb([128, NT, 8, 3], BF16)
            Ctab = sb([128, 48], F32)
            ssx = [sb([128, 1], F32) for _ in range(2)]
            rsx = [sb([128, 1], F32) for _ in range(2)]
            pt = [sb([128, 2, 512], BF16) for _ in range(2)]
            X = sb([128, 32768], BF16)

            def xv(off, n, dt):
                if dt == BF16:
                    return X[:, off // 2: off // 2 + n]
                return X[:, off // 2: off // 2 + 2 * n].bitcast(dt)

            psall = st.enter_context(nc.psum_tensor("psall", [128, 4096], F32))
            banks = [psall[:, i * 512:(i + 1) * 512] for i in range(8)]
            bB = [Buf(excl=True) for _ in range(8)]
            bankbf = [b.bitcast(BF16) for b in banks]

            B = {k: Buf() for k in (
                "ident", "identf", "tri", "ones", "negones", "twos", "fmask", "mmask", "maskb", "vtwos", "qkz", "gains", "bf", "invf", "cT",
                "normw", "qlw", "kvlw", "pos", "bada4", "ada4", "ada_d", "wuq", "wukv", "wba", "wbb", "wout", "wbf_d",
                "gate_bc", "cols", "A_col", "posf", "ang", "angk", "angi", "sint", "cost", "ssq", "sskv", "epsq", "epskv",
                "rstdkv", "GCS", "gcols", "kfraw", "krt", "krs", "krss", "krm", "krr", "spt", "Wf", "Wr", "Ws", "nWs", "Ctab")}
            b_hT = [Buf() for _ in range(NT)]
            b_yT = [[Buf() for _ in range(4)] for _ in range(8)]
            b_pt = [Buf() for _ in range(2)]
            b_ssx = [Buf(), Buf()]
            b_rsx = [Buf(), Buf()]

            _TagList.tag[0] = "setup"
            dma("sp", identf[:], ident_d, writes=[B["identf"]])
            op("dve", lambda e: e.tensor_copy(ident[:], identf[:]), reads=[B["identf"]], writes=[B["ident"]])
            dma("sp", tri[:], tri_d, writes=[B["tri"]])
            dma("sp", fmask[:], fmask_d, writes=[B["fmask"]])
            dma("sp", mmask[:], mmask_d, writes=[B["mmask"]])
            op("dve", lambda e: e.tensor_copy(fmaskb[:], fmask[:]), reads=[B["fmask"]], writes=[B["maskb"]])
            op("dve", lambda e: e.tensor_copy(mmaskb[:], mmask[:]), reads=[B["mmask"]], writes=[B["maskb"]])
            op("pool", lambda e: e.memset(ones[:], 1.0), writes=[B["ones"]])
            op("pool", lambda e: e.memset(negones[:], -1.0), writes=[B["negones"]])
            op("pool", lambda e: e.memset(twos[:], 2.0), writes=[B["twos"]])
            for t_, d_ in ((g_qna, gqna_d), (g_qra, gqra_d), (g_kna, gkna_d), (g_kra, gkra_d), (g_qb, gqb_d),
                           (g_kb, gkb_d), (bf_bc, bf_d), (invf, invf_d)):
                dma("sp", t_[:], d_[0].partition_broadcast(128), writes=[B["gains"]])
            op("dve", lambda e: e.tensor_scalar(g_qna[:], g_qna[:], 1.0 / math.sqrt(96.0), None, ALU.mult),
               reads=[B["gains"]], writes=[B["gains"]])
            op("dve", lambda e: e.tensor_scalar(g_qra[:], g_qra[:], 1.0 / math.sqrt(96.0), None, ALU.mult),
               reads=[B["gains"]], writes=[B["gains"]])
            op("dve", lambda e: e.tensor_scalar(g_qb[:], g_qb[:], 0.125, None, ALU.mult),
               reads=[B["gains"]], writes=[B["gains"]])
            dma("sp", gcols[:], gcols_d, writes=[B["gcols"]])
            op("dve", lambda e: e.tensor_scalar(gcols[0:64, 0:1], gcols[0:64, 0:1], 1.0 / math.sqrt(96.0), None, ALU.mult),
               reads=[B["gcols"]], writes=[B["gcols"]])
            op("dve", lambda e: e.tensor_scalar(gcols[0:64, 2:3], gcols[0:64, 2:3], 0.125, None, ALU.mult),
               reads=[B["gcols"]], writes=[B["gcols"]])
            op("dve", lambda e: e.tensor_copy(g_qra_s[:, 0:32], g_qra[:]), reads=[B["gains"]], writes=[B["gains"]])
            op("dve", lambda e: e.tensor_scalar(g_qra_s[:, 32:48], g_qra[:, 16:32], -1.0, None, ALU.mult),
               reads=[B["gains"]], writes=[B["gains"]])
            op("dve", lambda e: e.tensor_copy(g_qra_s[:, 48:64], g_qra[:, 0:16]), reads=[B["gains"]], writes=[B["gains"]])
            dma("sp", cT[:], cT_d, writes=[B["cT"]])
            dma("sp", normw[:], normw_d, writes=[B["normw"]])
            dma("sp", qlw[:], qlw_d, writes=[B["qlw"]])
            dma("sp", kvlw[:], kvlw_d, writes=[B["kvlw"]])
            dma("sp", pos_sb[:], pos_d, writes=[B["pos"]])
            pass

            bada4 = xv(32768, 3 * D, F32)[0:BPC, :]
            ada4 = xv(45056, 3 * D, F32)[0:BPC, :]
            dma("sp", bada4, bada_d[0].partition_broadcast(BPC), writes=[B["bada4"]])
            stg = [xv(0, 8 * 512, F32).rearrange("p (c n) -> p c n", c=8),
                   xv(16384, 8 * 512, F32).rearrange("p (c n) -> p c n", c=8)]
            b_stg = [Buf(), Buf()]
            wada3 = wada_d.rearrange("(c p) n -> p c n", p=128)
            for n in range(6):
                s_ = n % 2
                dma("sp", stg[s_], wada3[:, :, n * 512:(n + 1) * 512], writes=[b_stg[s_]])
                for kc in range(8):
                    op("pe", lambda e, s_=s_, kc=kc, n=n: e.matmul(banks[n % 2][0:BPC, :], cT[:, kc, :], stg[s_][:, kc, :],
                                                                  start=(kc == 0), stop=(kc == 7)),
                       reads=[B["cT"], b_stg[s_]], writes=[bB[n % 2]], inc=(kc == 7))
                op("dve", lambda e, n=n: e.tensor_tensor(ada4[:, n * 512:(n + 1) * 512], banks[n % 2][0:BPC, :],
                                                         bada4[:, n * 512:(n + 1) * 512], ALU.add),
                   reads=[bB[n % 2], B["bada4"]], writes=[B["ada4"]])
            dma("sp", ada_d, ada4, reads=[B["ada4"]], writes=[B["ada_d"]])
            P.barrier()

            sA = xv(0, 3 * 768, F32).rearrange("p (c n) -> p c n", c=3)
            sBv = xv(16384, 2 * 1024, F32).rearrange("p (c n) -> p c n", c=2)
            b_sA, b_sB = Buf(), Buf()
            dma("sp", sA, wuq_d.rearrange("(c p) n -> p c n", p=128), writes=[b_sA])
            dma("sp", sBv, wukv_d.rearrange("(c p) n -> p c n", p=128), writes=[b_sB])
            for c in range(3):
                src3 = sA[:, c, :].rearrange("p (h n) -> p h n", h=8)
                dst3 = wuq[:, c, :].rearrange("p (h n) -> p h n", h=8)
                for d0, d1, s0, s1 in ((0, 96, 0, 96), (96, 112, 80, 96), (112, 128, 64, 80)):
                    op("dve", lambda e, c=c, src3=src3, dst3=dst3, d0=d0, d1=d1, s0=s0, s1=s1: e.tensor_scalar(
                        dst3[:, :, d0:d1], src3[:, :, s0:s1], qlw[:, c:c + 1], None, ALU.mult),
                       reads=[b_sA, B["qlw"]], writes=[B["wuq"]])
            for c in range(2):
                op("dve", lambda e, c=c: e.tensor_scalar(wukv[:, c, :], sBv[:, c, :], kvlw[:, c:c + 1], None, ALU.mult),
                   reads=[b_sB, B["kvlw"]], writes=[B["wukv"]])
            P.barrier()
            sW = [xv(0, 4 * 1024, F32).rearrange("p (c n) -> p c n", c=4),
                  xv(16384, 4 * 1024, F32).rearrange("p (c n) -> p c n", c=4)]
            b_sW = [Buf(), Buf()]
            jobs = [(wba_d.rearrange("(c p) n -> p c n", p=128), wba, 0, "wba"),
                    (wbb_d.rearrange("(c p) n -> p c n", p=128), wbb, 0, "wbb"),
                    (wout_d.rearrange("(c p) n -> p c n", p=128)[:, 0:4, :], wout, 0, "wout"),
                    (wout_d.rearrange("(c p) n -> p c n", p=128)[:, 4:8, :], wout, 4, "wout")]
            for i, (src, dst, c0, key) in enumerate(jobs):
                s_ = i % 2
                dma("sp", sW[s_], src, writes=[b_sW[s_]])
                eng = "dve" if s_ == 0 else "pool"
                op(eng, lambda e, s_=s_, dst=dst, c0=c0: e.tensor_copy(dst[:, c0:c0 + 4, :], sW[s_]),
                   reads=[b_sW[s_]], writes=[B[key]])
            P.barrier()
            HALF = D_IN // 2
            sI = [xv(0, HALF, F32), xv(16384, HALF, F32)]
            sO = [xv(32768, HALF, BF16), xv(40960, HALF, BF16)]
            b_sI = [Buf(), Buf()]
            b_sO = [Buf(), Buf()]
            i = 0
            for kc in range(8):
                for hf in range(2):
                    s_ = i % 2
                    dma("sp", sI[s_], win_d[kc * 128:(kc + 1) * 128, hf * HALF:(hf + 1) * HALF], writes=[b_sI[s_]])
                    eng = ("dve", "pool", "act")[i % 3]
                    if eng == "act":
                        op(eng, lambda e, s_=s_: e.copy(sO[s_], sI[s_]), reads=[b_sI[s_]], writes=[b_sO[s_]])
                    else:
                        op(eng, lambda e, s_=s_: e.tensor_copy(sO[s_], sI[s_]), reads=[b_sI[s_]], writes=[b_sO[s_]])
                    dma("sp", wbf3[:, kc, hf * HALF:(hf + 1) * HALF], sO[s_], reads=[b_sO[s_]], writes=[B["wbf_d"]])
                    i += 1
            P.barrier()

            stage_check(1)
            cqT = xv(0, 3 * S, BF16).rearrange("p (c n) -> p c n", c=3)
            ckvT = xv(12288, 2 * S, BF16).rearrange("p (c n) -> p c n", c=2)
            b_cqT = [Buf() for _ in range(NT)]
            xt = [xv(20480, D, F32), xv(24576, D, F32)]
            junk = xv(28672, D, BF16)
            xn = [xv(30720, D, BF16), xv(32768, D, BF16)]
            wg1 = xv(34816, 8 * 680, BF16).rearrange("p (c n) -> p c n", c=8)
            cqb = [xv(45696, 640, BF16), xv(46976, 640, BF16)]
            htmp = [xv(48256, D, F32), xv(52352, D, F32)]
            ang = xv(60544, NT * 16, F32).rearrange("p (t n) -> p t n", t=NT)
            angk = xv(61568, NT * 16, F32).rearrange("p (t n) -> p t n", t=NT)
            angi = xv(62592, NT * 16, I32).rearrange("p (t n) -> p t n", t=NT)
            krt = xv(56448, NT * 32, F32).rearrange("p (t n) -> p t n", t=NT)
            krs = xv(58496, NT * 32, F32).rearrange("p (t n) -> p t n", t=NT)
            krm = [xv(60544 + 1024 * k_, NT * 16, F32).rearrange("p (t n) -> p t n", t=NT) for k_ in range(4)]
            QKT = xv(20480, 4 * S, BF16).rearrange("p (a n) -> p a n", a=4)
            Vt = xv(36864, NT * 192, BF16).rearrange("p (t n) -> p t n", t=NT)
            gate2 = xv(43008, S, F32)
            wpair = xv(51200, 8 * 512, BF16).rearrange("p (c n) -> p c n", c=8)
            WB = 59392
            sq = [xv(WB, 512, F32), xv(WB + 2048, 512, F32)]
            tmpr = [xv(WB + 4096, 128, F32), xv(WB + 4608, 128, F32)]
            tmpr2 = [xv(WB + 5120, 128, F32), xv(WB + 5632, 128, F32)]
            wmg = xv(0, 8 * 2048, BF16).rearrange("p (c n) -> p c n", c=8)
            mT = xv(32768, 8 * 512, BF16).rearrange("p (c n) -> p c n", c=8)
            ta = [xv(40960, 512, F32), xv(43008, 512, F32)]
            m12 = [xv(45056, 512, F32), xv(47104, 512, F32)]
            xr = [xv(49152, D, F32), xv(53248, D, F32)]
            res = [xv(57344, D, F32), xv(61440, D, F32)]
            st6 = [sb([128, 8], F32) for _ in range(3)]
            rs6 = [sb([128, 8], F32) for _ in range(3)]
            sq.append(sb([128, 512], F32)[:])
            tmpr.append(sb([128, 128], F32)[:])
            tmpr2.append(sb([128, 128], F32)[:])
            qk = [sb([128, 384], BF16) for _ in range(3)]
            rd = ovl1[:, 0:512]
            tmpo = ovl1[:, 512:1024]
            tg = sb([128, 512], F32)
            b_rd, b_tmpo, b_tg = Buf(), Buf(), Buf()

            def bc2(ap2, n):
                return ap2.unsqueeze(2).broadcast_to([128, ap2.shape[1], n])

            def bch(ap2, h):
                return ap2.unsqueeze(1).broadcast_to([128, h, ap2.shape[1]])

            for b in range(nseq):
                _TagList.tag[0] = "s%d.p1" % b
                dma("sp", sh_col[:], ada_d[b, 0:D].rearrange("(c p) -> p c", p=128), reads=[B["ada_d"]],
                    writes=[B["cols"]], allow_slow_non_contiguous=True)
                dma("sp", sc_col[:], ada_d[b, D:2 * D].rearrange("(c p) -> p c", p=128), reads=[B["ada_d"]],
                    writes=[B["cols"]], allow_slow_non_contiguous=True)
                op("dve", lambda e: e.scalar_tensor_tensor(A_col[:], sc_col[:], 1.0, normw[:], ALU.add, ALU.mult),
                   reads=[B["cols"], B["normw"]], writes=[B["A_col"]])
                stage_check(11)
                op("dve", lambda e, b=b: e.tensor_copy(posf[:], pos_sb[:, b, :]), reads=[B["pos"]], writes=[B["posf"]])
                op("dve", lambda e: e.tensor_tensor(ang, bc2(posf[:], 16), bch(invf[:], NT), ALU.mult),
                   reads=[B["posf"], B["gains"]], writes=[B["ang"]])

                def reduce_angle(dst_key_unused=None):
                    op("dve", lambda e: e.tensor_scalar(angk, ang, 1.0 / (2.0 * math.pi), None, ALU.mult),
                       reads=[B["ang"]], writes=[B["angk"]])
                    op("dve", lambda e: e.tensor_copy(angi, angk), reads=[B["angk"]], writes=[B["angi"]])
                    op("dve", lambda e: e.tensor_copy(angk, angi), reads=[B["angi"]], writes=[B["angk"]])
                    op("dve", lambda e: e.scalar_tensor_tensor(ang, angk, -TWO_PI_HI, ang, ALU.mult, ALU.add),
                       reads=[B["angk"], B["ang"]], writes=[B["ang"]])
                    op("dve", lambda e: e.scalar_tensor_tensor(ang, angk, -TWO_PI_LO, ang, ALU.mult, ALU.add),
                       reads=[B["angk"], B["ang"]], writes=[B["ang"]])
                    op("dve", lambda e: e.tensor_scalar(ang, ang, math.pi, -math.pi, ALU.min, ALU.max),
                       reads=[B["ang"]], writes=[B["ang"]])

                reduce_angle()
                op("act", lambda e: e.activation(sint[:], ang, AF.Sin), reads=[B["ang"]], writes=[B["sint"]])
                op("dve", lambda e: e.tensor_scalar(ang, ang, 0.5 * math.pi, None, ALU.add),
                   reads=[B["ang"], B["sint"]], writes=[B["ang"]])
                reduce_angle()
                op("act", lambda e: e.activation(cost[:], ang, AF.Sin), reads=[B["ang"]], writes=[B["cost"]])
                G4 = GCS[:].rearrange("p t (k n) -> p t k n", k=4)
                for k_, tab, key in ((0, cost, "cost"), (1, cost, "cost"), (2, sint, "sint"), (3, sint, "sint")):
                    op("dve", lambda e, k_=k_, tab=tab: e.tensor_tensor(
                        G4[:, :, k_, :], tab[:], bch(g_qra_s[:, k_ * 16:(k_ + 1) * 16], NT), ALU.mult),
                       reads=[B[key], B["gains"]], writes=[B["GCS"]])

                stage_check(12)
                b_wg1 = Buf()
                dma("sp", wg1[:, :, 0:672], wbf3[:, :, 0:672], reads=[B["wbf_d"]], writes=[b_wg1])
                dma("sp", wg1[:, :, 672:680], wbf3[:, :, C_FB:C_FB + 8], reads=[B["wbf_d"]], writes=[b_wg1])

                b_xt = [Buf(), Buf()]
                b_junk = Buf()
                b_xn = [Buf(), Buf()]
                b_cqb = [Buf(), Buf()]
                b_htmp = [Buf(), Buf()]
                def p1(t, part):
                  s_ = t % 2
                  ts = slice(t * 128, (t + 1) * 128)
                  gA, gB = 2 + s_, 4 + s_
                  tb = 6 + s_
                  if part == 1:
                    dma("sp", xt[s_], x_d[b, ts, :], writes=[b_xt[s_]])
                    op("act", lambda e, s_=s_: e.activation(junk, xt[s_], AF.Square, accum_out=ssx[s_][:]),
                       reads=[b_xt[s_]], writes=[b_junk, b_ssx[s_]])
                    op("act", lambda e, s_=s_: e.activation(rsx[s_][:], ssx[s_][:], AF.Ln, bias=EPS, scale=1.0 / D),
                       reads=[b_ssx[s_]], writes=[b_rsx[s_]])
                    op("act", lambda e, s_=s_: e.activation(rsx[s_][:], rsx[s_][:], AF.Exp, scale=-0.5),
                       reads=[b_rsx[s_]], writes=[b_rsx[s_]])
                    op("dve", lambda e, s_=s_: e.tensor_scalar(xn[s_], xt[s_], rsx[s_][:], None, ALU.mult),
                       reads=[b_xt[s_], b_rsx[s_]], writes=[b_xn[s_]])
                  pb = s_
                  if part == 2:
                    for c in range(8):
                        op("pe", lambda e, c=c, s_=s_, pb=pb: e.transpose(bankbf[pb][:, c * 128:(c + 1) * 128],
                                                                          xn[s_][:, c * 128:(c + 1) * 128], ident[:]),
                           reads=[b_xn[s_], B["ident"]], writes=[bB[pb]], inc=(c == 7))
                    op("dve", lambda e, s_=s_, pb=pb: e.tensor_tensor(
                        htmp[s_].rearrange("p (c n) -> p c n", c=8), bankbf[pb].rearrange("p (c n) -> p c n", c=8),
                        bc2(A_col[:], 128), ALU.mult),
                       reads=[bB[pb], B["A_col"]], writes=[b_htmp[s_]])
                    op("pool", lambda e, s_=s_, ts=ts: e.tensor_tensor(
                        hT[:, :, ts], htmp[s_].rearrange("p (c n) -> p c n", c=8), bc2(sh_col[:], 128), ALU.add),
                       reads=[b_htmp[s_], B["cols"]], writes=[b_hT[t]])
                  if part == 3:
                    for c in range(8):
                        op("pe", lambda e, c=c, ts=ts, gA=gA: e.matmul(banks[gA][:, 0:384], hT[:, c, ts], wg1[:, c, 0:384],
                                                                       start=(c == 0), stop=(c == 7)),
                           reads=[b_hT[t], b_wg1], writes=[bB[gA]], inc=(c == 7))
                    for c in range(8):
                        op("pe", lambda e, c=c, ts=ts, gB=gB: e.matmul(banks[gB][:, 0:296], hT[:, c, ts], wg1[:, c, 384:680],
                                                                       start=(c == 0), stop=(c == 7)),
                           reads=[b_hT[t], b_wg1], writes=[bB[gB]], inc=(c == 7))

                    op("dve", lambda e, gA=gA, s_=s_: e.tensor_copy(cqb[s_][:, 0:384], banks[gA][:, 0:384]),
                       reads=[bB[gA]], writes=[b_cqb[s_]])
                    op("dve", lambda e, gB=gB, s_=s_: e.tensor_copy(cqb[s_][:, 384:640], banks[gB][:, 0:256]),
                       reads=[bB[gB]], writes=[b_cqb[s_]])
                    op("act", lambda e, gA=gA, t=t: e.activation(junk[:, 0:384], banks[gA][:, 0:384], AF.Square,
                                                                 accum_out=ssq[:, t:t + 1]),
                       reads=[bB[gA]], writes=[b_junk, B["ssq"]])
                    op("act", lambda e, gB=gB, t=t: e.activation(junk[:, 0:256], banks[gB][:, 0:256], AF.Square,
                                                                 accum_out=sskv[:, t:t + 1]),
                       reads=[bB[gB]], writes=[b_junk, B["sskv"]])
                    op("act", lambda e, gB=gB, t=t: e.copy(kfraw[:, t, :], banks[gB][:, 256:296]),
                       reads=[bB[gB]], writes=[B["kfraw"]])
                  if part == 4:
                    for c in range(5):
                        op("pe", lambda e, c=c, s_=s_, tb=tb: e.transpose(bankbf[tb][:, c * 128:(c + 1) * 128],
                                                                          cqb[s_][:, c * 128:(c + 1) * 128], ident[:]),
                           reads=[b_cqb[s_], B["ident"]], writes=[bB[tb]], inc=(c == 4))
                    op("dve", lambda e, tb=tb, ts=ts: e.tensor_copy(
                        cqT[:, :, ts], bankbf[tb][:, 0:384].rearrange("p (c n) -> p c n", c=3)),
                       reads=[bB[tb]], writes=[b_cqT[t]])
                    op("act", lambda e, tb=tb, ts=ts: e.copy(
                        ckvT[:, :, ts], bankbf[tb][:, 384:640].rearrange("p (c n) -> p c n", c=2)),
                       reads=[bB[tb]], writes=[b_cqT[t]])


                for t in range(NT + 3):
                    if t < NT:
                        p1(t, 1)
                    if 1 <= t < NT + 1:
                        p1(t - 1, 2)
                    if 2 <= t < NT + 2:
                        p1(t - 2, 3)
                    if t >= 3:
                        p1(t - 3, 4)

                stage_check(2)
                _TagList.tag[0] = "s%d.p1c" % b
                op("dve", lambda e: e.tensor_scalar(epsq[:], ssq[:], EPS / 384.0, EPS * EPS, ALU.mult, ALU.add),
                   reads=[B["ssq"]], writes=[B["epsq"]])
                op("dve", lambda e: e.tensor_scalar(epskv[:], sskv[:], EPS / 256.0, EPS * EPS, ALU.mult, ALU.add),
                   reads=[B["sskv"]], writes=[B["epskv"]])
                op("act", lambda e: e.activation(rstdkv[:], sskv[:], AF.Ln, bias=EPS, scale=1.0 / 256.0),
                   reads=[B["sskv"]], writes=[B["rstdkv"]])
                op("act", lambda e: e.activation(rstdkv[:], rstdkv[:], AF.Exp, scale=-0.5),
                   reads=[B["rstdkv"]], writes=[B["rstdkv"]])
                op("dve", lambda e: e.tensor_tensor(krs, kfraw[:, :, 0:32], kfraw[:, :, 0:32], ALU.mult),
                   reads=[B["kfraw"]], writes=[B["krs"]])
                op("dve", lambda e: e.tensor_reduce(krss[:], krs, AX.X, ALU.add), reads=[B["krs"]], writes=[B["krss"]])
                op("act", lambda e: e.activation(krss[:], krss[:], AF.Ln, bias=EPS, scale=1.0 / 32.0),
                   reads=[B["krss"]], writes=[B["krss"]])
                op("act", lambda e: e.activation(krss[:], krss[:], AF.Exp, scale=-0.5),
                   reads=[B["krss"]], writes=[B["krss"]])
                op("dve", lambda e: e.tensor_tensor(krt, kfraw[:, :, 0:32], bc2(krss[:], 32), ALU.mult),
                   reads=[B["kfraw"], B["krss"]], writes=[B["krt"]])
                op("dve", lambda e: e.tensor_tensor(krt, krt, bch(g_kra[:], NT), ALU.mult),
                   reads=[B["krt"], B["gains"]], writes=[B["krt"]])
                x1, x2 = krt[:, :, 0:16], krt[:, :, 16:32]
                op("dve", lambda e: e.tensor_tensor(krm[0], x1, cost[:], ALU.mult), reads=[B["krt"], B["cost"]], writes=[B["krm"]])
                op("dve", lambda e: e.tensor_tensor(krm[1], x2, sint[:], ALU.mult), reads=[B["krt"], B["sint"]], writes=[B["krm"]])
                op("dve", lambda e: e.tensor_tensor(krm[2], x2, cost[:], ALU.mult), reads=[B["krt"], B["cost"]], writes=[B["krm"]])
                op("dve", lambda e: e.tensor_tensor(krm[3], x1, sint[:], ALU.mult), reads=[B["krt"], B["sint"]], writes=[B["krm"]])
                op("dve", lambda e: e.tensor_tensor(krr[:, :, 0:16], krm[0], krm[1], ALU.subtract),
                   reads=[B["krm"]], writes=[B["krr"]])
                op("dve", lambda e: e.tensor_tensor(krr[:, :, 16:32], krm[2], krm[3], ALU.add),
                   reads=[B["krm"]], writes=[B["krr"]])
                op("dve", lambda e: e.tensor_tensor(spt[:], kfraw[:, :, 32:40], bch(bf_bc[:], NT), ALU.add),
                   reads=[B["kfraw"], B["gains"]], writes=[B["spt"]])
                op("act", lambda e: e.activation(spt[:], spt[:], AF.Exp, scale=-1.0), reads=[B["spt"]], writes=[B["spt"]])
                op("act", lambda e: e.activation(spt[:], spt[:], AF.Ln, bias=1.0), reads=[B["spt"]], writes=[B["spt"]])
                spt2 = spt[:].rearrange("p t h -> p (t h)")
                op("pe", lambda e: e.matmul(banks[0][:, 0:128], tri[:], spt2, start=True, stop=True),
                   reads=[B["tri"], B["spt"]], writes=[bB[0]])
                op("pe", lambda e: e.matmul(banks[1][:, 0:128], ones[:], spt2, start=True, stop=True),
                   reads=[B["ones"], B["spt"]], writes=[bB[1]])
                op("dve", lambda e: e.tensor_copy(Wf[:], banks[0][:, 0:128]), reads=[bB[0]], writes=[B["Wf"]])
                op("act", lambda e: e.copy(Wr[:], banks[1][:, 0:128]), reads=[bB[1]], writes=[B["Wr"]])
                scanA = (Wr[:].rearrange("p (t h) -> p t h", t=NT), "Wr")
                scanB = (spt[:], "spt")
                for d_ in (1, 2, 4, 8):
                    (A_, ka), (B_, kb) = scanA, scanB
                    op("dve", lambda e, A_=A_, B_=B_, d_=d_: e.tensor_copy(B_[:, 0:d_, :], A_[:, 0:d_, :]),
                       reads=[B[ka]], writes=[B[kb]])
                    op("dve", lambda e, A_=A_, B_=B_, d_=d_: e.tensor_tensor(B_[:, d_:NT, :], A_[:, d_:NT, :], A_[:, 0:NT - d_, :], ALU.add),
                       reads=[B[ka]], writes=[B[kb]])
                    scanA, scanB = scanB, scanA
                Wf3_ = Wf[:].rearrange("p (t h) -> p t h", t=NT)
                op("dve", lambda e: e.tensor_tensor(Wf3_[:, 1:NT, :], Wf3_[:, 1:NT, :], scanA[0][:, 0:NT - 1, :], ALU.add),
                   reads=[B["Wf"], B[scanA[1]]], writes=[B["Wf"]])
                Wf3 = Wf[:].rearrange("p (t h) -> p t h", t=NT)
                Wr3 = Wr[:].rearrange("p (t h) -> p t h", t=NT)
                op("dve", lambda e: e.tensor_copy(Ws[:, :, :, 0], Wf3), reads=[B["Wf"]], writes=[B["Ws"]])
                op("dve", lambda e: e.tensor_tensor(Wr3, Wf3, Ws[:, :, :, 0], ALU.subtract),
                   reads=[B["Wf"], B["Ws"]], writes=[B["Wr"]])
                op("dve", lambda e: e.tensor_copy(Ws[:, :, :, 1], Wr3), reads=[B["Wr"]], writes=[B["Ws"]])
                op("dve", lambda e: e.tensor_tensor(Wr3, Wr3, Ws[:, :, :, 1], ALU.subtract),
                   reads=[B["Wr"], B["Ws"]], writes=[B["Wr"]])
                op("dve", lambda e: e.tensor_copy(Ws[:, :, :, 2], Wr3), reads=[B["Wr"]], writes=[B["Ws"]])
                op("dve", lambda e: e.tensor_scalar(nWs[:], Ws[:], -1.0, None, ALU.mult), reads=[B["Ws"]], writes=[B["nWs"]])
                P.barrier()

                stage_check(3)
                op("pool", lambda e: e.memset(Vt[:, :, 64:128], 2.0), writes=[B["vtwos"]])
                op("pool", lambda e: e.memset(QKT[96:128, :, :], 0.0), writes=[B["qkz"]])
                b_wp = Buf()

                def load_wpair(jb):
                    hp_ = jb % 4
                    if jb < 4:
                        dma("sp", wpair[:, :, 0:128], wbf3[:, :, C_GA + hp_ * 128:C_GA + (hp_ + 1) * 128],
                            reads=[B["wbf_d"]], writes=[b_wp])
                    else:
                        for k_, c0 in enumerate((C_QB, C_KB, C_VB, C_GB)):
                            dma("sp", wpair[:, :, k_ * 128:(k_ + 1) * 128], wbf3[:, :, c0 + hp_ * 128:c0 + (hp_ + 1) * 128],
                                reads=[B["wbf_d"]], writes=[b_wp])

                for job in range(8):
                    _TagList.tag[0] = "s%d.j%d.proj" % (b, job)
                    mla = job < 4
                    hp = job % 4
                    Kd = 96 if mla else 70
                    ychunk = hp if mla else 4 + hp
                    if job == 0:
                        load_wpair(0)
                    b_QKT = [Buf() for _ in range(NT)]
                    if job == 4:
                        op("pool", lambda e: e.memset(QKT[64:96, :, :], 0.0), writes=b_QKT + [B["qkz"]])
                    b_Vt = [Buf() for _ in range(NT)]
                    b_sq = [Buf(), Buf(), Buf()]
                    b_st = [Buf(), Buf(), Buf()]
                    b_rs = [Buf(), Buf(), Buf()]
                    b_tn = [Buf(), Buf(), Buf()]
                    b_tr = [Buf(), Buf(), Buf()]
                    b_rm = [Buf(), Buf(), Buf()]
                    b_qc = [Buf(), Buf(), Buf()]
                    b_kc = [Buf(), Buf(), Buf()]
                    b_g2 = [Buf() for _ in range(4)]
                    if not mla:
                        for s_ in range(3):
                            qk4 = qk[s_][:, 0:280].rearrange("p (a n) -> p a n", a=4)
                            op("pool", lambda e, qk4=qk4: e.memset(qk4[:, 0:2, 67:70], 1.0), writes=[b_qc[s_]])
                            op("pool", lambda e, qk4=qk4: e.memset(qk4[:, 2:4, 64:67], 1.0), writes=[b_qc[s_]])

                    def tile(t, part):
                        s_ = t % 3
                        ts = slice(t * 128, (t + 1) * 128)
                        pp = s_
                        tb = 6 + (t % 2)
                        vdst = Vt[:, t, :].rearrange("p (a n) -> p a n", a=3)[:, 0:3:2, :]
                        if mla:
                            pq = banks[pp][:, 0:256].rearrange("p (h n) -> p h n", h=2)
                            pkv = banks[pp][:, 256:512].rearrange("p (h n) -> p h n", h=2)
                            qk4 = qk[s_][:, 0:384].rearrange("p (a n) -> p a n", a=4)
                            rsq = rs6[s_][:, 0:4].rearrange("p (h k) -> p h k", k=2)
                            rsk = rs6[s_][:, 4:8].rearrange("p (h k) -> p h k", k=2)
                            tr3 = tmpr[s_].rearrange("p (h n) -> p h n", h=2)
                            tr23 = tmpr2[s_].rearrange("p (h n) -> p h n", h=2)
                            if part == 0:
                                for c in range(3):
                                    op("pe", lambda e, c=c: e.matmul(
                                        banks[pp][:, 0:256], cqT[:, c, ts], wuq[:, c, hp * 256:(hp + 1) * 256],
                                        start=(c == 0), stop=(c == 2)),
                                       reads=[b_cqT[t], B["wuq"]], writes=[bB[pp]], inc=False)
                                for c in range(2):
                                    op("pe", lambda e, c=c: e.matmul(
                                        banks[pp][:, 256:512], ckvT[:, c, ts], wukv[:, c, hp * 256:(hp + 1) * 256],
                                        start=(c == 0), stop=(c == 1)),
                                       reads=[b_cqT[t], B["wukv"]], writes=[bB[pp]], inc=(c == 1))
                                op("act", lambda e: e.activation(sq[s_], banks[pp][:, :], AF.Square),
                                   reads=[bB[pp]], writes=[b_sq[s_]])
                                op("dve", lambda e: e.tensor_reduce(st6[s_][:, 0:8], sq[s_].rearrange("p (a n) -> p a n", a=8),
                                                                    AX.X, ALU.add),
                                   reads=[b_sq[s_]], writes=[b_st[s_]])
                                op("act", lambda e: e.activation(rs6[s_][:, 0:4], st6[s_][:, 0:4], AF.Ln,
                                                                 bias=epsq[:, t:t + 1], scale=1.0 / 64.0),
                                   reads=[b_st[s_], B["epsq"]], writes=[b_rs[s_]])
                                op("act", lambda e: e.activation(rs6[s_][:, 4:8], st6[s_][:, 4:8], AF.Ln,
                                                                 bias=epskv[:, t:t + 1], scale=1.0 / 64.0),
                                   reads=[b_st[s_], B["epskv"]], writes=[b_rs[s_]])
                                op("act", lambda e: e.activation(rs6[s_][:, 0:8], rs6[s_][:, 0:8], AF.Exp, scale=-0.5),
                                   reads=[b_rs[s_]], writes=[b_rs[s_]])
                            elif part == 1:
                                op("dve", lambda e: e.tensor_tensor(qk4[:, 0:2, 0:64], pq[:, :, 0:64],
                                                                    rsq[:, :, 0:1].broadcast_to([128, 2, 64]), ALU.mult),
                                   reads=[bB[pp], b_rs[s_]], writes=[b_qc[s_]])
                                op("dve", lambda e: e.tensor_tensor(qk4[:, 2:4, 0:64], pkv[:, :, 0:64],
                                                                    rsk[:, :, 0:1].broadcast_to([128, 2, 64]), ALU.mult),
                                   reads=[bB[pp], b_rs[s_]], writes=[b_qc[s_]])
                                op("dve", lambda e: e.tensor_tensor(tr3, pq[:, :, 64:128],
                                                                    rsq[:, :, 1:2].broadcast_to([128, 2, 64]), ALU.mult),
                                   reads=[bB[pp], b_rs[s_]], writes=[b_tr[s_]])
                                op("pool", lambda e: e.tensor_tensor(tr23, tr3, bch(GCS[:, t, :], 2), ALU.mult),
                                   reads=[b_tr[s_], B["GCS"]], writes=[b_rm[s_]])
                                op("pool", lambda e: e.tensor_tensor(qk4[:, 0:2, 64:96], tr23[:, :, 0:32], tr23[:, :, 32:64], ALU.add),
                                   reads=[b_rm[s_]], writes=[b_qc[s_]])
                                op("pool", lambda e: e.tensor_copy(qk4[:, 2:4, 64:96], bch(krr[:, t, :], 2)),
                                   reads=[B["krr"]], writes=[b_qc[s_]])
                                op("act", lambda e: e.activation(
                                    vdst, pkv[:, :, 64:128], AF.Identity,
                                    scale=rstdkv[:, t:t + 1]),
                                   reads=[bB[pp], B["rstdkv"]], writes=[b_Vt[t]])
                            gc = 0
                        else:
                            p4 = banks[pp][:, 0:256].rearrange("p (a n) -> p a n", a=4)
                            qk4 = qk[s_][:, 0:280].rearrange("p (a n) -> p a n", a=4)
                            if part == 0:
                                for c in range(8):
                                    op("pe", lambda e, c=c: e.matmul(
                                        banks[pp][:, 0:384], hT[:, c, ts], wpair[:, c, 0:384], start=(c == 0), stop=(c == 7)),
                                       reads=[b_hT[t], b_wp], writes=[bB[pp]], inc=(c == 7))
                                op("act", lambda e: e.activation(sq[s_][:, 0:256], banks[pp][:, 0:256], AF.Square),
                                   reads=[bB[pp]], writes=[b_sq[s_]])
                                op("dve", lambda e: e.tensor_reduce(st6[s_][:, 0:4],
                                                                    sq[s_][:, 0:256].rearrange("p (a n) -> p a n", a=4), AX.X, ALU.add),
                                   reads=[b_sq[s_]], writes=[b_st[s_]])
                                op("act", lambda e: e.activation(rs6[s_][:, 0:4], st6[s_][:, 0:4], AF.Ln, bias=EPS,
                                                                 scale=1.0 / 64.0),
                                   reads=[b_st[s_]], writes=[b_rs[s_]])
                                op("act", lambda e: e.activation(rs6[s_][:, 0:4], rs6[s_][:, 0:4], AF.Exp, scale=-0.5),
                                   reads=[b_rs[s_]], writes=[b_rs[s_]])
                            elif part == 1:
                                op("dve", lambda e: e.tensor_tensor(qk4[:, :, 0:64], p4, bc2(rs6[s_][:, 0:4], 64), ALU.mult),
                                   reads=[bB[pp], b_rs[s_]], writes=[b_qc[s_]])
                                op("pool", lambda e: e.tensor_copy(qk4[:, 0:2, 64:67], nWs[:, t, 2 * hp:2 * hp + 2, :]),
                                   reads=[B["nWs"]], writes=[b_qc[s_]])
                                op("pool", lambda e: e.tensor_copy(qk4[:, 2:4, 67:70], Ws[:, t, 2 * hp:2 * hp + 2, :]),
                                   reads=[B["Ws"]], writes=[b_qc[s_]])
                                op("act", lambda e: e.copy(vdst, banks[pp][:, 256:384].rearrange("p (h n) -> p h n", h=2)),
                                   reads=[bB[pp]], writes=[b_Vt[t]])
                            gc = 2
                        if part == 2:
                            for a in range(4):
                                op("pe", lambda e, a=a: e.transpose(bankbf[tb][0:Kd, a * 128:(a + 1) * 128], qk4[:, a, :], ident[:]),
                                   reads=[b_qc[s_], B["ident"]], writes=[bB[tb]], inc=(a == 3))
                            op("dve", lambda e: e.tensor_scalar(
                                QKT[0:Kd, 0:2, ts], bankbf[tb][0:Kd, 0:256].rearrange("p (a n) -> p a n", a=2),
                                gcols[0:Kd, gc:gc + 1], None, ALU.mult),
                               reads=[bB[tb], B["gcols"]], writes=[b_QKT[t]])
                            op("dve", lambda e: e.tensor_scalar(
                                QKT[0:Kd, 2:4, ts], bankbf[tb][0:Kd, 256:512].rearrange("p (a n) -> p a n", a=2),
                                gcols[0:Kd, gc + 1:gc + 2], None, ALU.mult),
                               reads=[bB[tb], B["gcols"]], writes=[b_QKT[t]])

                    def proj_step(k):
                        if k < NT:
                            tile(k, 0)
                        if 1 <= k < NT + 1:
                            tile(k - 1, 1)
                        if 2 <= k < NT + 2:
                            tile(k - 2, 2)

                    gc0 = 0 if mla else 384

                    def gate(g):
                        gs = slice(g * 512, (g + 1) * 512)
                        gbk = 4 + (g % 2)
                        for c in range(8):
                            op("pe", lambda e, c=c: e.matmul(
                                banks[gbk][:, :], wpair[:, c, gc0:gc0 + 128], hT[:, c, gs], start=(c == 0), stop=(c == 7)),
                               reads=[b_wp] + b_hT[4 * g:4 * g + 4], writes=[bB[gbk]], inc=(c == 7))
                        op("act", lambda e: e.activation(tg[:], banks[gbk][:, :], AF.Tanh, scale=0.5),
                           reads=[bB[gbk]], writes=[b_tg])
                        op("dve", lambda e: e.scalar_tensor_tensor(
                            gate2[:, gs], tg[:], 1.0, banks[gbk][:, :], ALU.add, ALU.mult),
                           reads=[b_tg, bB[gbk]], writes=[b_g2[g]])

                    maskb = mmaskb if mla else fmaskb

                    def issue_s(g, j):
                        N = 512 if j < 4 * g else 512 - (j - 4 * g) * 128
                        qc0 = g * 512 + 512 - N
                        sb0 = 2 * (j % 2)
                        for hh in range(2):
                            kT = QKT[:, 2 + hh, j * 128:(j + 1) * 128]
                            rd_ = [b_QKT[j], B["qkz"]] + b_QKT[qc0 // 128:4 * g + 4]
                            if j >= 4 * g:
                                op("pe", lambda e, hh=hh: e.matmul(banks[sb0 + hh][:, 0:128], ident[:], maskb[:],
                                                                   start=True, stop=False),
                                   reads=[B["ident"], B["maskb"]], writes=[bB[sb0 + hh]], inc=False)
                                op("pe", lambda e, hh=hh, kT=kT: e.matmul(banks[sb0 + hh][:, 0:128], kT, QKT[:, hh, qc0:qc0 + 128],
                                                                          start=False, stop=True),
                                   reads=rd_, writes=[bB[sb0 + hh]], inc=(hh == 1 and N == 128))
                                if N > 128:
                                    op("pe", lambda e, hh=hh, kT=kT: e.matmul(banks[sb0 + hh][:, 128:N], kT,
                                                                              QKT[:, hh, qc0 + 128:qc0 + N], start=True, stop=True),
                                       reads=rd_, writes=[bB[sb0 + hh]], inc=(hh == 1))
                            else:
                                op("pe", lambda e, hh=hh, kT=kT: e.matmul(banks[sb0 + hh][:, 0:N], kT,
                                                                          QKT[:, hh, qc0:qc0 + N], start=True, stop=True),
                                   reads=rd_, writes=[bB[sb0 + hh]], inc=(hh == 1))
                        s2 = psall[:, sb0 * 512:(sb0 + 2) * 512].rearrange("p (h n) -> p h n", h=2)
                        ps_ = j % 2
                        op("act", lambda e: e.activation(pt[ps_][:, :, 0:N], s2[:, :, 0:N], AF.Exp),
                           reads=[bB[sb0], bB[sb0 + 1]], writes=[b_pt[ps_]])

                    def issue_pv(g, j):
                        N = 512 if j < 4 * g else 512 - (j - 4 * g) * 128
                        ps_ = j % 2
                        for hh in range(2):
                            op("pe", lambda e, hh=hh: e.matmul(banks[4 + hh][:, 512 - N:512], Vt[:, j, hh * 64:hh * 64 + 128],
                                                               pt[ps_][:, hh, 0:N], start=(j == 0), stop=(j == 4 * g + 3)),
                               reads=[b_Vt[j], b_pt[ps_], B["vtwos"]], writes=[bB[4 + hh]], inc=(hh == 1))

                    def group_end_a(g):
                        lo, hi = slice(0, 64), slice(64, 128)
                        op("dve", lambda e: e.tensor_copy(tmpo[lo, :], banks[4][lo, :]), reads=[bB[4]], writes=[b_tmpo])
                        op("dve", lambda e: e.tensor_copy(rd[lo, :], banks[4][hi, :]), reads=[bB[4]], writes=[b_rd])
                        op("dve", lambda e: e.tensor_copy(tmpo[hi, :], banks[5][hi, :]), reads=[bB[5]], writes=[b_tmpo])
                        op("dve", lambda e: e.tensor_copy(rd[hi, :], banks[5][lo, :]), reads=[bB[5]], writes=[b_rd])

                    def group_end_b(g):
                        gs = slice(g * 512, (g + 1) * 512)
                        op("act", lambda e: e.activation(rd, rd, AF.Ln), reads=[b_rd], writes=[b_rd])
                        op("act", lambda e: e.activation(rd, rd, AF.Exp, scale=-1.0), reads=[b_rd], writes=[b_rd])
                        op("dve", lambda e: e.tensor_tensor(tmpo, tmpo, rd, ALU.mult),
                           reads=[b_tmpo, b_rd], writes=[b_tmpo])
                        op("pool", lambda e: e.tensor_tensor(yT[:, ychunk, gs], tmpo, gate2[:, gs], ALU.mult),
                           reads=[b_tmpo, b_g2[g]], writes=[b_yT[ychunk][g]])

                    for k in range(NT + 2):
                        proj_step(k)
                    _TagList.tag[0] = "s%d.j%d.gate" % (b, job)
                    for g in range(4):
                        gate(g)
                    if job < 7:
                        load_wpair(job + 1)
                    if job == 0:
                        stage_check(4)
                    _TagList.tag[0] = "s%d.j%d.attn" % (b, job)
                    for g in range(4):
                        nst = 4 * g + 4
                        for i in range(nst + 1):
                            if i < nst:
                                issue_s(g, i)
                            if i >= 1:
                                issue_pv(g, i - 1)
                            if i == 2 and g > 0:
                                group_end_b(g - 1)
                        group_end_a(g)
                    group_end_b(3)
                    P.barrier()
                    if job == 0:
                        stage_check(5)
                    if job == 4:
                        stage_check(6)

                stage_check(7)
                _TagList.tag[0] = "s%d.p4" % b
                b_wmg = [Buf() for _ in range(8)]
                dma("sp", gate_bc[:], ada_d[b, 2 * D:3 * D].partition_broadcast(128), reads=[B["ada_d"]],
                    writes=[B["gate_bc"]])
                op("dve", lambda e: e.tensor_scalar(gate_bc[:], gate_bc[:], 0.5, None, ALU.mult),
                   reads=[B["gate_bc"]], writes=[B["gate_bc"]])
                for c in range(8):
                    dma("sp", wmg[:, c, :], wbf3[:, c, C_MA:C_MA + 2048], reads=[B["wbf_d"]], writes=[b_wmg[c]])
                b_mT = [Buf() for _ in range(8)]
                b_ta = [Buf(), Buf()]
                b_m12 = [Buf(), Buf()]
                b_xr = [Buf(), Buf()]
                b_res = [Buf(), Buf()]
                k4 = 0
                for g in range(4):
                    gs = slice(g * 512, (g + 1) * 512)
                    for mc in range(8):
                        bs = (k4 % 2) * 4
                        k4 += 1
                        ms = slice(mc * 128, (mc + 1) * 128)
                        for c in range(4):
                            op("pe", lambda e, c=c, bs=bs, ms=ms, gs=gs: e.matmul(banks[bs][:, :], wba[:, c, ms], yT[:, c, gs],
                                                                                  start=(c == 0), stop=(c == 3)),
                               reads=[B["wba"], b_yT[c][g]], writes=[bB[bs]], inc=(c == 3))
                        for c in range(4):
                            op("pe", lambda e, c=c, bs=bs, ms=ms, gs=gs: e.matmul(banks[bs + 1][:, :], wbb[:, c, ms], yT[:, 4 + c, gs],
                                                                                  start=(c == 0), stop=(c == 3)),
                               reads=[B["wbb"], b_yT[4 + c][g]], writes=[bB[bs + 1]], inc=(c == 3))
                        for c in range(8):
                            op("pe", lambda e, c=c, bs=bs, ms=ms, gs=gs: e.matmul(banks[bs + 2][:, :], wmg[:, c, ms], hT[:, c, gs],
                                                                                  start=(c == 0), stop=(c == 7)),
                               reads=[b_wmg[c]] + b_hT[4 * g:4 * g + 4], writes=[bB[bs + 2]], inc=(c == 7))
                        for c in range(8):
                            op("pe", lambda e, c=c, bs=bs, mc=mc, gs=gs: e.matmul(
                                banks[bs + 3][:, :], wmg[:, c, 1024 + mc * 128:1024 + (mc + 1) * 128], hT[:, c, gs],
                                start=(c == 0), stop=(c == 7)),
                               reads=[b_wmg[c]] + b_hT[4 * g:4 * g + 4], writes=[bB[bs + 3]], inc=(c == 7))
                        op("act", lambda e, bs=bs: e.activation(ta[0], banks[bs + 2][:, :], AF.Tanh, scale=0.5),
                           reads=[bB[bs + 2]], writes=[b_ta[0]])
                        op("act", lambda e, bs=bs: e.activation(ta[1], banks[bs + 3][:, :], AF.Tanh, scale=0.5),
                           reads=[bB[bs + 3]], writes=[b_ta[1]])
                        op("dve", lambda e, bs=bs: e.scalar_tensor_tensor(m12[0], ta[0], 1.0, banks[bs][:, :], ALU.add, ALU.mult),
                           reads=[b_ta[0], bB[bs]], writes=[b_m12[0]])
                        op("dve", lambda e, bs=bs: e.scalar_tensor_tensor(m12[1], ta[1], 1.0, banks[bs + 1][:, :], ALU.add, ALU.mult),
                           reads=[b_ta[1], bB[bs + 1]], writes=[b_m12[1]])
                        op("pool", lambda e, mc=mc: e.tensor_tensor(mT[:, mc, :], m12[0], m12[1], ALU.add),
                           reads=[b_m12[0], b_m12[1]], writes=[b_mT[mc]])
                    for tt in range(4):
                        t = 4 * g + tt
                        s_ = t % 2
                        ts = slice(t * 128, (t + 1) * 128)
                        bs = (k4 % 2) * 4
                        k4 += 1
                        dma("sp", xr[s_], x_d[b, ts, :], writes=[b_xr[s_]])
                        for hf in range(2):
                            for c in range(8):
                                op("pe", lambda e, c=c, bs=bs, hf=hf, tt=tt: e.matmul(
                                    banks[bs + hf][:, :], mT[:, c, tt * 128:(tt + 1) * 128], wout[:, c, hf * 512:(hf + 1) * 512],
                                    start=(c == 0), stop=(c == 7)),
                                   reads=[b_mT[c], B["wout"]], writes=[bB[bs + hf]], inc=(c == 7))
                        for hf in range(2):
                            hs = slice(hf * 512, (hf + 1) * 512)
                            op("dve", lambda e, bs=bs, hf=hf, hs=hs, s_=s_: e.tensor_tensor(
                                res[s_][:, hs], banks[bs + hf][:, :], gate_bc[:, hs], ALU.mult),
                               reads=[bB[bs + hf], B["gate_bc"]], writes=[b_res[s_]])
                        op("pool", lambda e, s_=s_: e.tensor_tensor(res[s_], res[s_], xr[s_], ALU.add),
                           reads=[b_res[s_], b_xr[s_]], writes=[b_res[s_]])
                        dma("sp", out_d[b, ts, :], res[s_], reads=[b_res[s_]])
                P.barrier()

        except _Stop:
            pass
        P.finish()
        _DBG["sbuf_remaining"] = nc.sbuf_bytes_remaining
        _DBG["ops"] = {n: len(e.ops) for n, e in P.E.items()}
        _DBG["tags"] = {n: list(e.ops.tags) for n, e in P.E.items()}
        P.emit(st)
    return nc


_NC_CACHE = {}


def _consts():
    ident = np.eye(128, dtype=np.float32)
    tri = np.triu(np.ones((128, 128), np.float32))
    kk = np.arange(128)[:, None]
    qq = np.arange(128)[None, :]
    fmask = np.where(kk <= qq, 0.0, NEG).astype(np.float32)
    mmask = np.where((kk // 64) <= (qq // 64), 0.0, NEG).astype(np.float32)
    invf = (np.float32(10000.0) ** (-(np.arange(0, 32, 2, dtype=np.float32)) / np.float32(32))).astype(np.float32)
    return ident, tri, fmask, mmask, invf.reshape(1, 16)


def kernel(x, c, positions, w_ada, b_ada, norm_w, w_in, b_f, q_lora_norm_w, kv_lora_norm_w, w_uq, w_ukv,
           qn_nope_a, qn_rope_a, kn_nope_a, kn_rope_a, qn_b, kn_b, w_branch_a, w_branch_b, w_out):
    f = lambda a: np.ascontiguousarray(np.asarray(a, dtype=np.float32))
    x = f(x)
    c = f(c)
    positions = np.ascontiguousarray(np.asarray(positions, dtype=np.int32))
    ident, tri, fmask, mmask, invf = _consts()
    shared = {
        "w_ada": f(w_ada)[0], "b_ada": f(b_ada)[0].reshape(1, -1),
        "normw": np.ascontiguousarray(f(norm_w)[0].reshape(8, 128).T),
        "w_in": f(w_in)[0], "b_f": f(b_f)[0].reshape(1, 8),
        "qlw": np.ascontiguousarray(f(q_lora_norm_w)[0].reshape(3, 128).T),
        "kvlw": np.ascontiguousarray(f(kv_lora_norm_w)[0].reshape(2, 128).T),
        "w_uq": f(w_uq)[0], "w_ukv": f(w_ukv)[0],
        "g_qna": f(qn_nope_a)[0].reshape(1, -1), "g_qra": f(qn_rope_a)[0].reshape(1, -1),
        "g_kna": f(kn_nope_a)[0].reshape(1, -1), "g_kra": f(kn_rope_a)[0].reshape(1, -1),
        "g_qb": f(qn_b)[0].reshape(1, -1), "g_kb": f(kn_b)[0].reshape(1, -1),
        "w_ba": f(w_branch_a)[0], "w_bb": f(w_branch_b)[0], "w_out": f(w_out)[0],
        "ident": ident, "tri": tri, "fmask": fmask, "mmask": mmask, "invf": invf,
    }
    gcols = np.ones((128, 4), np.float32)
    gcols[0:64, 0] = f(qn_nope_a)[0]
    gcols[0:64, 1] = f(kn_nope_a)[0]
    gcols[0:64, 2] = f(qn_b)[0]
    gcols[0:64, 3] = f(kn_b)[0]
    shared["gcols"] = gcols
    in_maps = []
    for i in range(NCORES):
        bs = slice(i * BPC, (i + 1) * BPC)
        m = dict(shared)
        m["x"] = x[bs]
        m["cT"] = np.ascontiguousarray(c[bs].reshape(BPC, 8, 128).transpose(2, 1, 0))
        m["pos"] = np.ascontiguousarray(positions[bs].reshape(BPC, NT, 128).transpose(2, 0, 1))
        in_maps.append(m)
    if "nc" not in _NC_CACHE:
        _NC_CACHE["nc"] = build_nc()
    res = run_bass_kernel_spmd(_NC_CACHE["nc"], in_maps, core_ids=list(range(NCORES)))
    return np.concatenate([r["out"] for r in res.results], axis=0).astype(np.float32)
```
